# Optimizing a Trainium2 kernel written in Bass

```python
import jax, jax.numpy as jnp
from jax import lax
import numpy as np

D_MODEL = 1024
BATCH = 8
SEQ = 2048
DEPTH = 1

MLA_HEADS = 8
MLA_NOPE = 64
MLA_ROPE = 32
MLA_V = 64
Q_LORA = 384
KV_LORA = 256
ROPE_THETA = 10000.0
SWA_HEADS = 8
SWA_KV_HEADS = 2
SWA_HEAD_DIM = 64
WINDOW = 128
Q_BLOCK = 128
D_FF = 4 * D_MODEL
EPS = 1e-6
MIX_WIDTH = MLA_HEADS * MLA_V + SWA_HEADS * SWA_HEAD_DIM
SWA_Q_COLS = SWA_HEADS * SWA_HEAD_DIM
SWA_KV_COLS = SWA_KV_HEADS * SWA_HEAD_DIM
IN_COLS = Q_LORA + KV_LORA + MLA_ROPE + SWA_Q_COLS + 2 * SWA_KV_COLS
N_MOD = 6

kernel_name = "hybrid_mla_swa_sink_alibi_sqrelu_adaln"


def rmsnorm(x, g):
    xf = x.astype(jnp.float32)
    y = xf * lax.rsqrt(jnp.mean(xf * xf, axis=-1, keepdims=True) + EPS)
    return (y * g.astype(jnp.float32)).astype(x.dtype)


def rope(x, pos):
    r = x.shape[-1]
    freqs = ROPE_THETA ** (-jnp.arange(0, r, 2, dtype=jnp.float32) / r)
    ang = pos[:, None] * freqs[None, :]
    cos = jnp.cos(ang)[None, :, None, :]
    sin = jnp.sin(ang)[None, :, None, :]
    xf = x.astype(jnp.float32)
    x1, x2 = xf[..., : r // 2], xf[..., r // 2:]
    out = jnp.concatenate([x1 * cos - x2 * sin, x1 * sin + x2 * cos], axis=-1)
    return out.astype(x.dtype)


def alibi_slopes(n):
    return jnp.asarray([2.0 ** (-8.0 * (h + 1) / n) for h in range(n)], dtype=jnp.float32)


def mla_group(q_lat, kv_lat, k_rope, g_qa, w_qb, g_kva, w_kvb):
    b, s, _ = q_lat.shape
    q = jnp.einsum('bsr,rhd->bshd', rmsnorm(q_lat, g_qa), w_qb)
    kv = jnp.einsum('bsr,rhd->bshd', rmsnorm(kv_lat, g_kva), w_kvb)
    pos = jnp.arange(s, dtype=jnp.float32)
    q_nope = q[..., :MLA_NOPE]
    q_pe = rope(q[..., MLA_NOPE:], pos)
    k_nope = kv[..., :MLA_NOPE]
    v = kv[..., MLA_NOPE:]
    k_pe = rope(k_rope[:, :, None, :], pos)[:, :, 0, :]
    scale = (MLA_NOPE + MLA_ROPE) ** -0.5
    nb = s // Q_BLOCK
    qn_b = q_nope.reshape(b, nb, Q_BLOCK, MLA_HEADS, MLA_NOPE).transpose(1, 0, 2, 3, 4)
    qp_b = q_pe.reshape(b, nb, Q_BLOCK, MLA_HEADS, MLA_ROPE).transpose(1, 0, 2, 3, 4)
    kpos = jnp.arange(s)

    def block(args):
        i, qn, qp = args
        sc = (jnp.einsum('bqhd,bkhd->bhqk', qn, k_nope).astype(jnp.float32)
              + jnp.einsum('bqhd,bkd->bhqk', qp, k_pe).astype(jnp.float32)) * scale
        qpos = i * Q_BLOCK + jnp.arange(Q_BLOCK)
        causal = kpos[None, :] <= qpos[:, None]
        sc = jnp.where(causal[None, None], sc, -jnp.inf)
        p = jax.nn.softmax(sc, axis=-1).astype(v.dtype)
        return jnp.einsum('bhqk,bkhd->bqhd', p, v)

    out = lax.map(block, (jnp.arange(nb), qn_b, qp_b))
    return out.transpose(1, 0, 2, 3, 4).reshape(b, s, MLA_HEADS * MLA_V)


def swa_group(q, k, v, sinks):
    b, s, _ = q.shape
    nb = s // Q_BLOCK
    grp = SWA_HEADS // SWA_KV_HEADS
    qb = q.reshape(b, nb, Q_BLOCK, SWA_KV_HEADS, grp, SWA_HEAD_DIM)
    kb = k.reshape(b, nb, Q_BLOCK, SWA_KV_HEADS, SWA_HEAD_DIM)
    vb = v.reshape(b, nb, Q_BLOCK, SWA_KV_HEADS, SWA_HEAD_DIM)
    zpad = jnp.zeros_like(kb[:, :1])
    kw = jnp.concatenate([jnp.concatenate([zpad, kb[:, :-1]], axis=1), kb], axis=2)
    vw = jnp.concatenate([jnp.concatenate([zpad, vb[:, :-1]], axis=1), vb], axis=2)
    sc = jnp.einsum('bnqkgd,bnjkd->bnkgqj', qb, kw).astype(jnp.float32) * (SWA_HEAD_DIM ** -0.5)
    qi = jnp.arange(Q_BLOCK)[:, None] + Q_BLOCK
    kj = jnp.arange(2 * Q_BLOCK)[None, :]
    dist = qi - kj
    blk = jnp.arange(nb)
    valid = ((dist >= 0) & (dist < WINDOW))[None] & ((blk[:, None, None] > 0) | (kj[None] >= Q_BLOCK))
    slopes = alibi_slopes(SWA_HEADS).reshape(SWA_KV_HEADS, grp)
    sc = sc - slopes[:, :, None, None] * dist.astype(jnp.float32)[None, None]
    sc = jnp.where(valid[None, :, None, None], sc, -jnp.inf)
    sink = jnp.broadcast_to(sinks.astype(jnp.float32).reshape(1, 1, SWA_KV_HEADS, grp, 1, 1),
                            sc.shape[:-1] + (1,))
    p = jax.nn.softmax(jnp.concatenate([sc, sink], axis=-1), axis=-1)[..., :-1]
    out = jnp.einsum('bnkgqj,bnjkd->bnqkgd', p.astype(v.dtype), vw)
    return out.reshape(b, s, SWA_Q_COLS)


def setup_inputs(seed: int = 0) -> dict:
    key = jax.random.key(seed)
    ks = jax.random.split(key, 20)
    n = jax.random.normal
    f32 = jnp.float32
    return {
        "x": n(ks[0], (BATCH, SEQ, D_MODEL), f32),
        "c": n(ks[1], (BATCH, D_MODEL), f32),
        "w_ada": n(ks[2], (DEPTH, D_MODEL, N_MOD * D_MODEL), f32) * D_MODEL ** -0.5,
        "b_ada": n(ks[3], (DEPTH, N_MOD * D_MODEL), f32) * 0.02,
        "norm_mix_g": 1.0 + 0.05 * n(ks[4], (DEPTH, D_MODEL), f32),
        "w_in": n(ks[5], (DEPTH, D_MODEL, IN_COLS), f32) * D_MODEL ** -0.5,
        "g_qa": 1.0 + 0.05 * n(ks[6], (DEPTH, Q_LORA), f32),
        "w_qb": n(ks[7], (DEPTH, Q_LORA, MLA_HEADS, MLA_NOPE + MLA_ROPE), f32) * Q_LORA ** -0.5,
        "g_kva": 1.0 + 0.05 * n(ks[8], (DEPTH, KV_LORA), f32),
        "w_kvb": n(ks[9], (DEPTH, KV_LORA, MLA_HEADS, MLA_NOPE + MLA_V), f32) * KV_LORA ** -0.5,
        "sinks": 0.5 * n(ks[10], (DEPTH, SWA_HEADS), f32),
        "w_o": n(ks[11], (DEPTH, MIX_WIDTH, D_MODEL), f32) * MIX_WIDTH ** -0.5,
        "norm_mlp_g": 1.0 + 0.05 * n(ks[12], (DEPTH, D_MODEL), f32),
        "w_up": n(ks[13], (DEPTH, D_MODEL, D_FF), f32) * D_MODEL ** -0.5,
        "w_down": n(ks[14], (DEPTH, D_FF, D_MODEL), f32) * D_FF ** -0.5,
        "final_g": 1.0 + 0.05 * n(ks[15], (D_MODEL,), f32),
    }


def reference(x, c, w_ada, b_ada, norm_mix_g, w_in, g_qa, w_qb, g_kva, w_kvb, sinks,
              w_o, norm_mlp_g, w_up, w_down, final_g):
    o1 = Q_LORA
    o2 = o1 + KV_LORA
    o3 = o2 + MLA_ROPE
    o4 = o3 + SWA_Q_COLS
    o5 = o4 + SWA_KV_COLS
    silu_c = jax.nn.silu(c)
    for l in range(DEPTH):
        mod = jnp.einsum('bd,de->be', silu_c, w_ada[l]) + b_ada[l]
        sh1, sc1, g1, sh2, sc2, g2 = jnp.split(mod[:, None, :], N_MOD, axis=-1)
        h = rmsnorm(x, norm_mix_g[l]) * (1.0 + sc1) + sh1
        proj = jnp.einsum('bsd,de->bse', h, w_in[l])
        y_mla = mla_group(proj[..., :o1], proj[..., o1:o2], proj[..., o2:o3],
                          g_qa[l], w_qb[l], g_kva[l], w_kvb[l])
        y_swa = swa_group(proj[..., o3:o4], proj[..., o4:o5], proj[..., o5:], sinks[l])
        mix = jnp.concatenate([y_mla, y_swa], axis=-1)
        x = x + g1 * jnp.einsum('bse,ed->bsd', mix, w_o[l])
        h = rmsnorm(x, norm_mlp_g[l]) * (1.0 + sc2) + sh2
        u = jnp.square(jax.nn.relu(jnp.einsum('bsd,df->bsf', h, w_up[l])))
        x = x + g2 * jnp.einsum('bsf,fd->bsd', u, w_down[l])
    return rmsnorm(x, final_g)
```

```python
import numpy as np
import concourse.bass as bass
import concourse.mybir as mybir
from concourse.bass_utils import run_bass_kernel_spmd
from contextlib import ExitStack

F32 = mybir.dt.float32
BF16 = mybir.dt.bfloat16
ALU = mybir.AluOpType
AF = mybir.ActivationFunctionType

T = 2048
D = 1024
NT = 16
NG = 4
EPS = 1e-6
NEG = -1.0e6
WINC = 1472
PH_ALL = "SABCDEF"

DEBUG_DUMPS = False


def _sz(dt):
    return 2 if dt == BF16 else 4


class Alloc:
    def __init__(self, name, lo, hi, phases):
        self.name, self.lo, self.hi, self.phases = name, lo, hi, phases
        self.bufs = []
        self.overl = []


class Buf:
    __slots__ = ("name", "alloc", "w", "r", "dsem", "dcnt")

    def __init__(self, name, alloc=None):
        self.name, self.alloc = name, alloc
        self.w = None
        self.r = []
        self.dsem = None
        self.dcnt = 0
        if alloc is not None:
            alloc.bufs.append(self)


class Op:
    __slots__ = ("eng", "fn", "deps", "sig", "cnt", "dma", "dbuf", "dval")

    def __init__(self, eng, fn):
        self.eng, self.fn = eng, fn
        self.deps = []
        self.sig = False
        self.cnt = 0
        self.dma = False
        self.dbuf = None
        self.dval = 0


class Sched:
    ENGS = ["pe", "act", "dve", "pool", "sp"]

    def __init__(self):
        self.ops = {e: [] for e in self.ENGS}
        self.final = []

    def _add_dep(self, op, d, kind):
        if d is None or d is op:
            return
        if (not d.dma) and d.eng == op.eng:
            if op.eng == "pe" or op.eng == "sp":
                return
        op.deps.append((d, d.dbuf.dcnt if d.dma else 0))

    def _collect(self, op, r, w):
        for b in r:
            self._add_dep(op, b.w, "raw")
        for b in w:
            self._add_dep(op, b.w, "waw")
            for x in b.r:
                self._add_dep(op, x, "war")
            if b.alloc is not None:
                for a2 in b.alloc.overl:
                    for b2 in a2.bufs:
                        self._add_dep(op, b2.w, "waw")
                        for x in b2.r:
                            self._add_dep(op, x, "waw")
        for b in r:
            b.r.append(op)
        for b in w:
            b.w = op
            b.r = []

    def op(self, eng, fn, r=(), w=()):
        o = Op(eng, fn)
        self._collect(o, r, w)
        self.ops[eng].append(o)
        return o

    def dma(self, q, fn, key, r=(), w=()):
        o = Op(q, fn)
        o.dma = True
        o.dbuf = key
        self._collect(o, r, w)
        key.dcnt += 16
        o.dval = key.dcnt
        self.ops[q].append(o)
        return o

    def emit(self, nc, es, block, handles):
        sem_budget = [0]

        def newsem(name):
            sem_budget[0] += 1
            return es.enter_context(nc.semaphore(name))

        esem = {e: newsem("s_" + e) for e in ["pe", "act", "dve", "pool"]}
        for e in self.ENGS:
            for o in self.ops[e]:
                for (d, dv) in o.deps:
                    if d.dma:
                        if d.dbuf.dsem is None:
                            d.dbuf.dsem = newsem("d_" + d.dbuf.name.replace("[", "_").replace("]", ""))
                    else:
                        d.sig = True
            for o in self.ops[e]:
                if o.dma and o.dbuf.dsem is None:
                    o.dbuf.dsem = newsem("d_" + o.dbuf.name.replace("[", "_").replace("]", ""))
        for o in self.final:
            if o.dbuf.dsem is None:
                o.dbuf.dsem = newsem("d_" + o.dbuf.name)
        for e in self.ENGS:
            c = 0
            for o in self.ops[e]:
                if o.sig:
                    c += 1
                o.cnt = c

        def run(e, h):
            seen = {}
            for o in self.ops[e]:
                need = {}
                for (d, dv) in o.deps:
                    if d.dma:
                        k, v = d.dbuf.dsem, dv
                    else:
                        k, v = esem[d.eng], d.cnt
                    kk = k.num
                    if seen.get(kk, 0) >= v:
                        continue
                    if kk not in need or need[kk][1] < v:
                        need[kk] = (k, v)
                for kk, (k, v) in need.items():
                    h.wait_ge(k, v)
                    seen[kk] = v
                ins = o.fn(h)
                if o.dma:
                    ins.then_inc(o.dbuf.dsem, 16)
                elif o.sig:
                    ins.then_inc(esem[e], 1)
            if e == "sp":
                fin = {}
                for o in self.final:
                    kk = o.dbuf.dsem.num
                    if kk not in fin or fin[kk][1] < o.dval:
                        fin[kk] = (o.dbuf.dsem, o.dval)
                for kk, (k, v) in fin.items():
                    h.wait_ge(k, v)

        @block.tensor
        def _(h):
            run("pe", h)

        @block.scalar
        def _(h):
            run("act", h)

        @block.vector
        def _(h):
            run("dve", h)

        @block.gpsimd
        def _(h):
            run("pool", h)

        @block.sync
        def _(h):
            run("sp", h)


class Arena:
    def __init__(self, nbytes):
        self.nbytes = nbytes
        self.allocs = []
        self.t = None

    def alloc(self, name, nbytes, phases):
        nbytes = (nbytes + 63) // 64 * 64
        lo = 0
        placed = None
        ivs = sorted((a.lo, a.hi) for a in self.allocs if set(a.phases) & set(phases))
        for (l, h) in ivs:
            if l - lo >= nbytes:
                placed = lo
                break
            lo = max(lo, h)
        if placed is None:
            if self.nbytes - lo >= nbytes:
                placed = lo
            else:
                raise RuntimeError(f"arena OOM for {name} ({nbytes}B, phases {phases}); top={lo}")
        a = Alloc(name, placed, placed + nbytes, phases)
        for o in self.allocs:
            if o.lo < a.hi and a.lo < o.hi:
                o.overl.append(a)
                a.overl.append(o)
        self.allocs.append(a)
        return a

    def ap(self, a, shape, dtype, byte_off=0, p0=0):
        free = 1
        for s in shape[1:]:
            free *= s
        nb = free * _sz(dtype)
        assert a.lo + byte_off + nb <= a.hi, (a.name, shape)
        e0 = (a.lo + byte_off) // 2
        v = self.t[p0:p0 + shape[0], e0:e0 + nb // 2]
        if dtype != BF16:
            v = v.bitcast(dtype)
        if len(shape) == 3:
            v = v.rearrange("p (a b) -> p a b", a=shape[1], b=shape[2])
        elif len(shape) == 4:
            v = v.rearrange("p (a b c) -> p a b c", a=shape[1], b=shape[2], c=shape[3])
        return v

    def __lt__(self, o):
        return False


Alloc.__lt__ = lambda self, o: self.lo < o.lo


def build_program():
    nc = bass.Bass("TRN2", target_bir_lowering=False)
    S = Sched()
    AR = Arena(207 * 1024)
    _NC_CACHE["arena"] = AR

    def din(name, shape, dt=F32):
        return nc.dram_tensor(name, list(shape), dt, kind="ExternalInput").ap()

    x_d = din("x", [T, D])
    ccol_d = din("ccol", [128, 8])
    badacol_d = din("badacol", [128, 48])
    gmixcol_d = din("gmixcol", [128, 8])
    gmlpcol_d = din("gmlpcol", [128, 8])
    fg_d = din("fg", [1, D])
    gqa_d = din("gqa", [128, 3])
    gkva_d = din("gkva", [128, 2])
    sinks_d = din("sinks", [1, 8])
    wada_d = din("wada", [D, 6144])
    win_d = din("win", [D, WINC])
    wqb_d = din("wqb", [384, 1024])
    wkvk_d = din("wkvk", [256, 512])
    wkvv_d = din("wkvv", [256, 512])
    wo_d = din("wo", [D, D])
    wup_d = din("wup", [D, 4096])
    wdn_d = din("wdn", [4096, D])
    identf_d = din("identf", [128, 128])
    onesf_d = din("onesf", [128, 128])
    cs_d = din("cs", [32, 2, T])
    mmask_d = din("mmask", [128, 256])
    smask_d = din("smask", [128, 2048])
    out_d = nc.dram_tensor("out", [T, D], F32, kind="ExternalOutput").ap()
    dbg = {}

    es = ExitStack()
    with es:
        AR.t = es.enter_context(nc.sbuf_tensor("arena", [128, AR.nbytes // 2], BF16))
        PS = es.enter_context(nc.psum_tensor("ps", [128, 8, 512], F32))
        bankb = [Buf(f"bank{i}") for i in range(8)]

        def bk(i, rows=None, cols=None):
            r0, r1 = rows if rows else (0, 128)
            c0, c1 = cols if cols else (0, 512)
            return PS[r0:r1, i, c0:c1]

        def bk_bf(i):
            return PS[:, i, :].bitcast(BF16)

        def bk_pair(i):
            return PS[:, i:i + 2, :]

        KB = 1024

        prealloc = {}

        def region(name, shape, dtype, phases, p0=0):
            free = 1
            for s in shape[1:]:
                free *= s
            nb = free * _sz(dtype)
            if name in prealloc:
                a = prealloc[name]
                assert a.phases == phases and a.hi - a.lo >= nb, name
            else:
                a = AR.alloc(name, nb, phases)
            return a, AR.ap(a, shape, dtype, p0=p0)

        for (nm, nbytes, ph) in [
            ("identb", 256, PH_ALL), ("onesb", 256, PH_ALL), ("onesf", 512, PH_ALL), ("stat", 768, PH_ALL),
            ("junk", 2048, PH_ALL), ("SILU", 16, PH_ALL), ("IDENTF", 512, PH_ALL), ("MODCOL", 192, PH_ALL),
            ("GCOLS", 128, PH_ALL), ("DIAG0", 512, PH_ALL), ("DIAG1", 512, PH_ALL), ("EPSC", 4, PH_ALL),
            ("X1", NT * D * 4, "CDEF"), ("H2T", 8 * T * 2, "CE"), ("WUP0", 16384, "CE"), ("WDN0", 16384, "CE"),
            ("BC2A", 4096, "C"), ("BC2B", 4096, "C"), ("T1C", 4096, "C"), ("HBC0", 2048, "C"), ("HBC1", 2048, "C"),
            ("MIXT", 8 * T * 2, "BC"), ("WO", 16384, "BC"),
        ]:
            prealloc[nm] = AR.alloc(nm, nbytes, ph)

        a_identb, IDENTB = region("identb", [128, 128], BF16, PH_ALL)
        a_onesb, ONESB = region("onesb", [128, 128], BF16, PH_ALL)
        a_onesf, ONESF = region("onesf", [128, 128], F32, PH_ALL)
        a_stat, STAT = region("stat", [128, 192], F32, PH_ALL)
        b_identb, b_onesb, b_onesf = Buf("identb", a_identb), Buf("onesb", a_onesb), Buf("onesf", a_onesf)
        a_junk, JUNK = region("junk", [128, 1024], BF16, PH_ALL)
        a_EPS, EPSC = region("EPSC", [128, 1], F32, PH_ALL)
        bEPS = Buf("EPSC", a_EPS)
        S.op("dve", lambda h: h.memset(EPSC, EPS), w=[bEPS])

        S.dma("pool", lambda h: h.dma_start(out=IDENTB, in_=identf_d), b_identb, w=[b_identb])
        S.dma("pool", lambda h: h.dma_start(out=ONESB, in_=onesf_d), b_onesb, w=[b_onesb])
        S.dma("sp", lambda h: h.dma_start(out=ONESF, in_=onesf_d), b_onesf, w=[b_onesf])

        a_K, K_ = region("K", [96, 8, T], BF16, "AB")
        a_V = AR.alloc("V", 16 * 768 * 2, "AB")
        a_SWQ, SWQ = region("SWQ", [128, NT, 4, 128], BF16, "AB")
        a_SWK, SWK = region("SWK", [128, T], BF16, "AB")
        a_SWV = AR.alloc("SWV", 16 * 192 * 2, "AB")
        a_QLN, QLN = region("QLN", [128, 3, T], BF16, "AB")
        bK = [[Buf(f"K[{h}][{g}]", a_K) for g in range(NG)] for h in range(8)]
        bV = [Buf(f"V[{t}]", a_V) for t in range(NT)]
        bVones = Buf("Vones", a_V)
        bSWQ = [Buf(f"SWQ[{g}]", a_SWQ) for g in range(NG)]
        bSWK = [Buf(f"SWK[{g}]", a_SWK) for g in range(NG)]
        bSWV = [Buf(f"SWV[{t}]", a_SWV) for t in range(NT)]
        bSWVones = Buf("SWVones", a_SWV)
        bQLN = [Buf(f"QLN[{g}]", a_QLN) for g in range(NG)]

        V_E0 = a_V.lo // 2
        SWV_E0 = a_SWV.lo // 2
        PSTR = AR.nbytes // 2
        arena_handle = AR.t

        def rap(off, dims, parts=128):
            return bass.AP(tensor=arena_handle, offset=off, ap=[[PSTR, parts]] + [list(d) for d in dims])

        def v_tile_out(t):
            return rap(V_E0 + t * 768, [[192, 4], [128, 2], [1, 64]])

        def vaug(t, h):
            base = V_E0 + t * 768 + (h // 2) * 192
            return rap(base + (0 if h % 2 == 0 else 64), [[1, 128]])

        def swv_tile_out(t):
            return rap(SWV_E0 + t * 192, [[128, 2], [1, 64]])

        def swvaug(t, kv):
            return rap(SWV_E0 + t * 192 + (0 if kv == 0 else 64), [[1, 128]])

        S.op("dve", lambda h: h.memset(rap(V_E0 + 64, [[192, 64], [1, 64]]), 1.0), w=[bVones])
        S.op("dve", lambda h: h.memset(rap(SWV_E0 + 64, [[192, 16], [1, 64]]), 1.0), w=[bSWVones])

        a_WIN, WIN = region("WIN", [128, 8, WINC], BF16, "SA")
        a_WKVK, WKVK = region("WKVK", [128, 2, 512], BF16, "SA")
        a_WKVV, WKVV = region("WKVV", [128, 2, 512], BF16, "SA")
        a_WQB, WQB = region("WQB", [128, 3, 1024], BF16, "SAB")
        a_GQA, GQA = region("GQA", [128, 3], F32, "SAB")
        a_GKVA, GKVA = region("GKVA", [128, 2], F32, "SAB")
        bWINL, bWIN, bWKVK, bWKVV, bWQB = (Buf("WINL", a_WIN), Buf("WIN", a_WIN), Buf("WKVK", a_WKVK), Buf("WKVV", a_WKVV),
                                             Buf("WQB", a_WQB))
        bGQA, bGKVA = Buf("GQA", a_GQA), Buf("GKVA", a_GKVA)

        a_CCOL, CCOL = region("CCOL", [128, 8], F32, "S")
        a_SILU, SILU = region("SILU", [128, 8], BF16, PH_ALL)
        bCCOL, bSILU = Buf("CCOL", a_CCOL), Buf("SILU", a_SILU)
        a_IDF, IDENTF = region("IDENTF", [128, 128], F32, PH_ALL)
        b_identf = Buf("IDENTF", a_IDF)
        a_MODCOL, MODCOL = region("MODCOL", [128, 48], F32, PH_ALL)
        bMODCOL = [Buf(f"MODCOL[{j}]", a_MODCOL) for j in range(24)]
        a_BADAC, BADACOL = region("BADACOL", [128, 48], F32, "SA")
        bBADAC = Buf("BADACOL", a_BADAC)
        a_GCOLS, GCOLS = region("GCOLS", [128, 4, 8], F32, PH_ALL)
        bGCOLS = [Buf(f"GCOLS[{i}]", a_GCOLS) for i in range(4)]
        DIAG, bDIAG = [], []
        for i in range(2):
            a, v = region(f"DIAG{i}", [128, 128], F32, PH_ALL)
            DIAG.append(v)
            bDIAG.append(Buf(f"DIAG{i}", a))
        WADA, bWADA = [], []
        for i in range(2):
            a, v = region(f"WADA{i}", [128, 8, 256], BF16, "SA")
            WADA.append(v)
            bWADA.append(Buf(f"WADA{i}", a))

        S.dma("sp", lambda h: h.dma_start(out=CCOL, in_=ccol_d), bCCOL, w=[bCCOL])
        S.op("act", lambda h: h.activation(out=SILU, in_=CCOL, func=AF.Silu), r=[bCCOL], w=[bSILU])
        S.dma("sp", lambda h: h.dma_start(out=IDENTF, in_=identf_d), b_identf, w=[b_identf])
        S.dma("sp", lambda h: h.dma_start(out=BADACOL, in_=badacol_d), bBADAC, w=[bBADAC])
        S.dma("sp", lambda h: h.dma_start(out=GCOLS[:, 0, :], in_=gmixcol_d), bGCOLS[0], w=[bGCOLS[0]])
        S.dma("sp", lambda h: h.dma_start(out=GCOLS[:, 1, :], in_=gmlpcol_d), bGCOLS[1], w=[bGCOLS[1]])

        ada_ctr = [0]
        wada_v = wada_d.rearrange("(c p) n -> p c n", p=128)

        ada_issued = set()
        raw_issued = set()

        STG = [AR.ap(a_K, [128, 8, 256], F32, byte_off=i * 8192) for i in range(2)]
        a_STG = Alloc("STG", a_K.lo, a_K.lo + 16384, "S")
        a_STG.overl = [a_K] + list(a_K.overl)
        for o_ in a_STG.overl:
            o_.overl.append(a_STG)
        bSTG = [Buf(f"STG{i}", a_STG) for i in range(2)]

        STAGED = [1, 3, 4, 5, 6, 7]

        def adaln_raw(n2):
            if n2 in raw_issued or n2 not in STAGED:
                return
            raw_issued.add(n2)
            sg = STAGED.index(n2) % 2
            S.dma("sp", lambda h: h.dma_start(out=STG[sg], in_=wada_v[:, :, n2 * 256:(n2 + 1) * 256]),
                  bSTG[sg], w=[bSTG[sg]])

        def adaln_dma(n2):
            if n2 in ada_issued:
                return
            ada_issued.add(n2)
            s_ = n2 % 2
            if n2 in STAGED:
                adaln_raw(n2)
                i_ = STAGED.index(n2)
                sg = i_ % 2
                S.op("dve", lambda h: h.tensor_copy(out=WADA[s_], in_=STG[sg]), r=[bSTG[sg]], w=[bWADA[s_]])
                if i_ + 2 < len(STAGED):
                    adaln_raw(STAGED[i_ + 2])
                return
            S.dma("pool", lambda h: h.dma_start(out=WADA[s_], in_=wada_v[:, :, n2 * 256:(n2 + 1) * 256]),
                  bWADA[s_], w=[bWADA[s_]])

        def adaln_cols(n2, bank):
            s_ = n2 % 2
            adaln_dma(n2)
            for half in range(2):
                j = 2 * n2 + half
                for k in range(8):
                    S.op("pe", lambda h, k=k, j=j, half=half: h.matmul(
                        bk(bank, None, (j, j + 1)), lhsT=WADA[s_][:, k, half * 128:(half + 1) * 128], rhs=SILU[:, k:k + 1],
                        start=(k == 0), stop=(k == 7)), r=[bSILU, bWADA[s_]], w=[bankb[bank]])
            S.op("dve", lambda h: h.tensor_tensor(out=MODCOL[:, 2 * n2:2 * n2 + 2], in0=bk(bank, None, (2 * n2, 2 * n2 + 2)),
                                                  in1=BADACOL[:, 2 * n2:2 * n2 + 2], op=ALU.add),
                 r=[bankb[bank], bBADAC], w=[bMODCOL[n2]])
            if n2 + 2 < 24 and not (n2 < 8 <= n2 + 2):
                adaln_dma(n2 + 2)

        dg_ctr = [0]

        def expand_bc(col_ap_fn, col_bufs, dst, dst_bufs, bank_list):
            for half in range(2):
                b = bank_list[half % len(bank_list)]
                for q in range(4):
                    jj = half * 4 + q
                    ds = dg_ctr[0] % 2
                    dg_ctr[0] += 1
                    S.op("dve", lambda h, jj=jj, ds=ds: h.tensor_scalar(out=DIAG[ds], in0=IDENTF, scalar1=col_ap_fn(jj),
                                                                        scalar2=None, op0=ALU.mult),
                         r=[b_identf] + col_bufs, w=[bDIAG[ds]])
                    S.op("pe", lambda h, q=q, ds=ds, b=b: h.matmul(bk(b, None, (q * 128, (q + 1) * 128)), lhsT=ONESF,
                                                                    rhs=DIAG[ds], start=True, stop=True),
                         r=[bDIAG[ds], b_onesf], w=[bankb[b]])
                S.op("dve", lambda h, half=half, b=b: h.tensor_copy(out=dst[:, half * 512:(half + 1) * 512], in_=bk(b)),
                     r=[bankb[b]], w=[dst_bufs[half]])

        a_BCA, BCA = region("BCA", [128, D], F32, "SA")
        a_BCB, BCB = region("BCB", [128, D], F32, "SA")
        bBCA = [Buf(f"BCA{i}", a_BCA) for i in range(2)]
        bBCB = [Buf(f"BCB{i}", a_BCB) for i in range(2)]

        def bcast_row(d_ap, n):
            return bass.AP(tensor=d_ap.tensor, offset=d_ap.offset, ap=[[0, 128], [1, n]])

        win_v = win_d.rearrange("(c p) n -> p c n", p=128)
        adaln_raw(1)
        adaln_raw(3)
        for n2 in range(8):
            adaln_cols(n2, n2 % 2)
            if n2 == 3:
                expand_bc(lambda jj: MODCOL[:, jj:jj + 1], bMODCOL[0:4], BCA, bBCA, [2, 3])
        S.op("dve", lambda h: h.scalar_tensor_tensor(out=GCOLS[:, 2, :], in0=MODCOL[:, 8:16], scalar=1.0, in1=GCOLS[:, 0, :],
                                                     op0=ALU.add, op1=ALU.mult),
             r=bMODCOL[4:8] + [bGCOLS[0]], w=[bGCOLS[2]])
        expand_bc(lambda jj: GCOLS[:, 2, jj:jj + 1], [bGCOLS[2]], BCB, bBCB, [2, 3])

        for c in range(8):
            S.dma("pool", lambda h, c=c: h.dma_start(out=WIN[:, c, 0:640], in_=win_v[:, c, 0:640]), bWINL,
                  w=[bWINL] if c == 0 else [])
        for c in range(8):
            S.dma("pool", lambda h, c=c: h.dma_start(out=WIN[:, c, 640:WINC], in_=win_v[:, c, 640:WINC]), bWIN,
                  w=[bWIN] if c == 0 else [])
        S.dma("pool", lambda h: h.dma_start(out=WKVK, in_=wkvk_d.rearrange("(c p) n -> p c n", p=128)), bWKVK, w=[bWKVK])
        S.dma("pool", lambda h: h.dma_start(out=WKVV, in_=wkvv_d.rearrange("(c p) n -> p c n", p=128)), bWKVV, w=[bWKVV])
        S.dma("pool", lambda h: h.dma_start(out=WQB, in_=wqb_d.rearrange("(c p) n -> p c n", p=128)), bWQB, w=[bWQB])
        adaln_dma(8)
        adaln_dma(9)
        S.dma("sp", lambda h: h.dma_start(out=GQA, in_=gqa_d), bGQA, w=[bGQA])
        S.dma("sp", lambda h: h.dma_start(out=GKVA, in_=gkva_d), bGKVA, w=[bGKVA])
        def weight_prep_a():
            S.op("dve", lambda h: h.tensor_scalar(out=WIN[:, :, 1440:1456], in0=WIN[:, :, 1440:1456], scalar1=-1.0,
                                                  scalar2=None, op0=ALU.mult), r=[bWIN], w=[bWIN])
            for c in range(2):
                S.op("dve", lambda h, c=c: h.tensor_scalar(out=WKVK[:, c, :], in0=WKVK[:, c, :], scalar1=GKVA[:, c:c + 1],
                                                           scalar2=None, op0=ALU.mult), r=[bWKVK, bGKVA], w=[bWKVK])
                S.op("dve", lambda h, c=c: h.tensor_scalar(out=WKVV[:, c, :], in0=WKVV[:, c, :], scalar1=GKVA[:, c:c + 1],
                                                           scalar2=None, op0=ALU.mult), r=[bWKVV, bGKVA], w=[bWKVV])

        def weight_prep_b():
            for c in range(3):
                S.op("dve", lambda h, c=c: h.tensor_scalar(out=WQB[:, c, :], in0=WQB[:, c, :], scalar1=GQA[:, c:c + 1],
                                                           scalar2=None, op0=ALU.mult), r=[bWQB, bGQA], w=[bWQB])
                wq4 = WQB[:, c, :].rearrange("p (h e) -> p h e", h=8, e=128)
                S.op("dve", lambda h, wq4=wq4: h.tensor_scalar(out=wq4[:, :, 96:112], in0=wq4[:, :, 96:112], scalar1=-1.0,
                                                               scalar2=None, op0=ALU.mult), r=[bWQB], w=[bWQB])

        def make_norm_bufs(ph):
            a_T1, T1 = region("T1" + ph, [128, D], F32, ph)
            HB, bHB = [], []
            for i in range(2):
                a, v = region(f"HB{ph}{i}", [128, D], BF16, ph)
                HB.append(v)
                bHB.append(Buf(f"HB{ph}{i}", a))
            return T1, Buf("T1" + ph, a_T1), HB, bHB

        stat_ctr = [0]

        def stat_cols(n):
            c = stat_ctr[0]
            stat_ctr[0] += n
            assert stat_ctr[0] <= 192
            return c

        def rms_rstd(src, src_bufs, nfeat, tag):
            c = stat_cols(3)
            bs = Buf(f"st_{tag}", a_stat)
            S.op("act", lambda h: h.activation(out=JUNK[:, 0:nfeat], in_=src, func=AF.Square, accum_out=STAT[:, c:c + 1]),
                 r=src_bufs, w=[bs])
            S.op("act", lambda h: h.activation(out=STAT[:, c + 1:c + 2], in_=STAT[:, c:c + 1], func=AF.Ln,
                                               scale=1.0 / nfeat, bias=EPSC), r=[bs, bEPS], w=[bs])
            S.op("act", lambda h: h.activation(out=STAT[:, c + 2:c + 3], in_=STAT[:, c + 1:c + 2], func=AF.Exp, scale=-0.5),
                 r=[bs], w=[bs])
            return STAT[:, c + 2:c + 3], bs


        def norm_pre(src, src_bufs, T1, bT1, HB, bHB, hs, GM, bGM, SH, bSH, tag, add_eng="dve", extra_r=()):
            rstd, bs = rms_rstd(src, src_bufs, D, tag)
            S.op("dve", lambda h: h.scalar_tensor_tensor(out=T1, in0=src, scalar=rstd, in1=GM, op0=ALU.mult, op1=ALU.mult),
                 r=src_bufs + [bs] + bGM, w=[bT1])
            S.op(add_eng, lambda h: h.tensor_tensor(out=HB[hs], in0=T1, in1=SH, op=ALU.add),
                 r=[bT1] + bSH + list(extra_r), w=[bHB[hs]])

        def norm_tr(HB, bHB, hs, tp_bank, dstT, dst_bufs, evac="dve"):
            tpv = bk_bf(tp_bank)
            for k in range(8):
                S.op("pe", lambda h, k=k: h.transpose(tpv[:, k * 128:(k + 1) * 128], HB[hs][:, k * 128:(k + 1) * 128], IDENTB),
                     r=[bHB[hs], b_identb], w=[bankb[tp_bank]])
            if evac == "act":
                S.op("act", lambda h: h.activation(out=dstT, in_=tpv.rearrange("p (a b) -> p a b", a=8, b=128), func=AF.Copy),
                     r=[bankb[tp_bank]], w=dst_bufs)
            else:
                S.op("dve", lambda h: h.tensor_copy(out=dstT, in_=tpv.rearrange("p (a b) -> p a b", a=8, b=128)),
                     r=[bankb[tp_bank]], w=dst_bufs)

        XT, bXT = [], []
        for i in range(2):
            a, v = region(f"XT{i}", [128, D], F32, "A")
            XT.append(v)
            bXT.append(Buf(f"XT{i}", a))
        HBA, bHBA = [], []
        for i in range(2):
            a, v = region(f"HBA{i}", [128, D], BF16, "A")
            HBA.append(v)
            bHBA.append(Buf(f"HBA{i}", a))
        H1T, bH1T = [], []
        for i in range(2):
            a, v = region(f"H1T{i}", [128, 8, 512], BF16, "A")
            H1T.append(v)
            bH1T.append([Buf(f"H1T{i}[{j}]", a) for j in range(4)])
        a_LATF, LATF = region("LATF", [128, 5, 512], F32, "A")
        a_SQN, SQN = region("SQN", [128, 5, 512], BF16, "A")
        a_RSTD, RSTD = region("RSTD", [128, 2, 512], F32, "A")
        bLATF = [Buf(f"LATF[{m}]", a_LATF) for m in range(5)]
        bSQN = [Buf(f"SQN[{m}]", a_SQN) for m in range(5)]
        bRSTD = [Buf(f"RSTD[{m}]", a_RSTD) for m in range(2)]
        CSA, bCSA = [], []
        for i in range(1):
            a, v = region(f"CSA{i}", [32, 2, 512], F32, "A", p0=64)
            CSA.append(v)
            bCSA.append(Buf(f"CSA{i}", a))
        a_TQA, TQA = region("TQA", [32, 2, 512], F32, "A", p0=64)
        bTQA = [Buf(f"TQA[{i}]", a_TQA) for i in range(2)]
        a_KPE, KPE = region("KPE", [32, 512], BF16, "A", p0=64)
        bKPEt = Buf("KPEt", a_KPE)

        def mm_group(bank, rows, ncols, lhs_list, rhs_list, rbufs):
            n = len(lhs_list)
            for i in range(n):
                S.op("pe", lambda h, i=i: h.matmul(bk(bank, rows, (0, ncols)), lhsT=lhs_list[i], rhs=rhs_list[i],
                                                   start=(i == 0), stop=(i == n - 1)),
                     r=rbufs, w=[bankb[bank]])

        brr = [0]

        def next_bank(lst):
            b = lst[brr[0] % len(lst)]
            brr[0] += 1
            return b

        tp_ctr = [0]

        a_H1T1 = bH1T[1][0].alloc
        XTALT = [AR.ap(a_H1T1, [128, D], F32, byte_off=i * D * 4) for i in range(2)]
        a_XTALT = Alloc("XTALT", a_H1T1.lo, a_H1T1.hi, "A")
        a_XTALT.overl = [a_H1T1] + list(a_H1T1.overl)
        for o_ in a_XTALT.overl:
            o_.overl.append(a_XTALT)
        bXTALT = [Buf(f"XTALT{i}", a_XTALT) for i in range(2)]

        def a_pre(t):
            if t in (2, 3):
                xa, bxa = XTALT[t - 2], bXTALT[t - 2]
                S.dma("sp", lambda h: h.dma_start(out=xa, in_=x_d[t * 128:(t + 1) * 128, :]), bxa, w=[bxa])
                norm_pre(xa, [bxa], xa, bxa, HBA, bHBA, t % 2, BCB, bBCB, BCA, bBCA, f"n1_{t}")
                return
            xs = t % 2
            S.dma("sp", lambda h: h.dma_start(out=XT[xs], in_=x_d[t * 128:(t + 1) * 128, :]), bXT[xs], w=[bXT[xs]])
            norm_pre(XT[xs], [bXT[xs]], XT[xs], bXT[xs], HBA, bHBA, t % 2, BCB, bBCB, BCA, bBCA, f"n1_{t}")

        def a_tr(t):
            G, j = t // 4, t % 4
            gs = G % 2
            tpb = tp_ctr[0] % 2
            tp_ctr[0] += 1
            norm_tr(HBA, bHBA, t % 2, tpb, H1T[gs][:, :, j * 128:(j + 1) * 128], [bH1T[gs][j]],
                    evac=("act" if t < 4 else "dve"))

        def a_part1(G):
            gs = G % 2
            gcols = slice(G * 512, (G + 1) * 512)
            hb = bH1T[gs]
            hrhs = [H1T[gs][:, k, :] for k in range(8)]
            for m in range(5):
                b = 2 + (m % 2)
                mm_group(b, None, 512, [WIN[:, k, m * 128:(m + 1) * 128] for k in range(8)], hrhs, hb + [bWINL])
                S.op("act", lambda h, b=b, m=m: h.activation(out=LATF[:, m, :], in_=bk(b), func=AF.Copy),
                     r=[bankb[b]], w=[bLATF[m]])
                S.op("act", lambda h, b=b, m=m: h.activation(out=SQN[:, m, :], in_=bk(b), func=AF.Square),
                     r=[bankb[b]], w=[bSQN[m]])

        def a_stats(G):
            gcols = slice(G * 512, (G + 1) * 512)
            mm_group(4, None, 512, [ONESB] * 3, [SQN[:, m, :] for m in range(3)], bSQN[0:3] + [b_onesb])
            mm_group(5, None, 512, [ONESB] * 2, [SQN[:, m, :] for m in range(3, 5)], bSQN[3:5] + [b_onesb])
            for (i, b, nf) in ((1, 5, 256), (0, 4, 384)):
                S.op("act", lambda h, i=i, b=b, nf=nf: h.activation(out=RSTD[:, i, :], in_=bk(b), func=AF.Ln,
                                                                    scale=1.0 / nf, bias=EPSC),
                     r=[bankb[b], bEPS], w=[bRSTD[i]])
                S.op("act", lambda h, i=i: h.activation(out=RSTD[:, i, :], in_=RSTD[:, i, :], func=AF.Exp, scale=-0.5),
                     r=[bRSTD[i]], w=[bRSTD[i]])
            for m in range(3, 5):
                S.op("dve", lambda h, m=m: h.tensor_tensor(out=SQN[:, m, :], in0=LATF[:, m, :], in1=RSTD[:, 1, :],
                                                           op=ALU.mult),
                     r=[bLATF[m], bRSTD[1]], w=[bSQN[m]])
            for m in range(3):
                S.op("dve", lambda h, m=m: h.tensor_tensor(out=QLN[:, m, gcols], in0=LATF[:, m, :],
                                                           in1=RSTD[:, 0, :], op=ALU.mult),
                     r=[bLATF[m], bRSTD[0]], w=[bQLN[G]])

        def a_swq(G, c4):
            gs = G % 2
            hb = bH1T[gs]
            hrhs = [H1T[gs][:, k, :] for k in range(8)]
            b = 2 + (c4 % 2)
            mm_group(b, None, 512, [WIN[:, k, 672 + c4 * 128:672 + (c4 + 1) * 128] for k in range(8)], hrhs, hb + [bWIN])
            if c4 % 2 == 0:
                S.op("act", lambda h: h.activation(out=SWQ[:, G * 4:(G + 1) * 4, c4, :],
                                                   in_=bk(b).rearrange("p (a b) -> p a b", a=4, b=128), func=AF.Copy),
                     r=[bankb[b]], w=[bSWQ[G]])
            else:
                S.op("dve", lambda h: h.tensor_copy(out=SWQ[:, G * 4:(G + 1) * 4, c4, :],
                                                    in_=bk(b).rearrange("p (a b) -> p a b", a=4, b=128)),
                     r=[bankb[b]], w=[bSWQ[G]])

        def a_part2_tail(G):
            gs = G % 2
            gcols = slice(G * 512, (G + 1) * 512)
            hb = bH1T[gs]
            hrhs = [H1T[gs][:, k, :] for k in range(8)]
            S.dma("sp", lambda h: h.dma_start(out=CSA[0], in_=cs_d[:, :, G * 512:(G + 1) * 512]), bCSA[0], w=[bCSA[0]])
            mm_group(4, None, 512, [WIN[:, k, 1184:1312] for k in range(8)], hrhs, hb + [bWIN])
            S.op("dve", lambda h: h.tensor_copy(out=SWK[:, gcols], in_=bk(4)), r=[bankb[4]], w=[bSWK[G]])
            mm_group(6, (64, 96), 512, [WIN[:, k, 640:672] for k in range(8)], hrhs, hb + [bWIN])
            mm_group(7, (64, 96), 512, [WIN[:, k, 1440:1472] for k in range(8)], hrhs, hb + [bWIN])
            S.op("dve", lambda h: h.tensor_tensor(out=TQA[:, 0, :], in0=bk(6, (64, 96)), in1=CSA[0][:, 0, :], op=ALU.mult),
                 r=[bankb[6], bCSA[0]], w=[bTQA[0]])
            S.op("dve", lambda h: h.tensor_tensor(out=TQA[:, 1, :], in0=bk(7, (64, 96)), in1=CSA[0][:, 1, :], op=ALU.mult),
                 r=[bankb[7], bCSA[0]], w=[bTQA[1]])
            S.op("dve", lambda h: h.tensor_tensor(out=KPE, in0=TQA[:, 0, :], in1=TQA[:, 1, :], op=ALU.add),
                 r=bTQA, w=[bKPEt])
            for hh in range(8):
                S.op("dve", lambda h, hh=hh: h.tensor_copy(out=K_[64:96, hh, gcols], in_=KPE),
                     r=[bKPEt], w=[bK[hh][G]])
            for j in range(4):
                t = G * 4 + j
                b2 = 6 + (j % 2)
                mm_group(b2, None, 128, [H1T[gs][:, k, j * 128:(j + 1) * 128] for k in range(8)],
                         [WIN[:, k, 1312:1440] for k in range(8)], hb + [bWIN])
                S.op("act", lambda h, t=t, b2=b2: h.activation(out=swv_tile_out(t), in_=bk(b2, None, (0, 128)).rearrange("p (a b) -> p a b", a=2, b=64), func=AF.Copy),
                     r=[bankb[b2]], w=[bSWV[t]])

        def a_part3(G):
            gcols = slice(G * 512, (G + 1) * 512)
            kvln = [SQN[:, 3, :], SQN[:, 4, :]]
            bkvln = [bSQN[3], bSQN[4]]
            for hp in range(4):
                b = 2 + (hp % 2)
                mm_group(b, None, 512, [WKVK[:, c, hp * 128:(hp + 1) * 128] for c in range(2)], kvln, bkvln + [bWKVK])
                for half in range(2):
                    hh = 2 * hp + half
                    S.op("act", lambda h, hh=hh, b=b, half=half: h.activation(
                        out=K_[0:64, hh, gcols], in_=bk(b, (half * 64, half * 64 + 64)), func=AF.Copy),
                         r=[bankb[b]], w=[bK[hh][G]])
            for j in range(4):
                t = G * 4 + j
                b = 4 + (j % 2)
                mm_group(b, None, 512, [kvln[c][:, j * 128:(j + 1) * 128] for c in range(2)],
                         [WKVV[:, c, :] for c in range(2)], bkvln + [bWKVV])
                S.op("dve", lambda h, t=t, b=b: h.tensor_copy(out=v_tile_out(t), in_=bk(b).rearrange("p (a b c) -> p a b c", a=4, b=2, c=64)), r=[bankb[b]], w=[bV[t]])

        a_pre(0)
        a_pre(1)
        a_tr(0)
        a_pre(2)
        a_tr(1)
        a_pre(3)
        a_tr(2)
        a_tr(3)
        weight_prep_a()
        for G in range(NG):
            nxt = G + 1 < NG
            t0 = (G + 1) * 4
            n2 = 8 + 4 * G
            a_part1(G)
            if nxt:
                a_pre(t0)
                a_pre(t0 + 1)
            a_stats(G)
            adaln_cols(n2, 6)
            a_swq(G, 0)
            if nxt:
                a_tr(t0)
                a_pre(t0 + 2)
            a_swq(G, 1)
            if nxt:
                a_tr(t0 + 1)
                a_pre(t0 + 3)
            adaln_cols(n2 + 1, 7)
            a_swq(G, 2)
            if nxt:
                a_tr(t0 + 2)
            a_swq(G, 3)
            if nxt:
                a_tr(t0 + 3)
            adaln_cols(n2 + 2, 5)
            a_part2_tail(G)
            adaln_cols(n2 + 3, 1)
            a_part3(G)
            if G == 0:
                weight_prep_b()
        S.op("dve", lambda h: h.scalar_tensor_tensor(out=GCOLS[:, 3, :], in0=MODCOL[:, 32:40], scalar=1.0, in1=GCOLS[:, 1, :],
                                                     op0=ALU.add, op1=ALU.mult),
             r=bMODCOL[16:20] + [bGCOLS[1]], w=[bGCOLS[3]])


        a_MIXT, MIXT = region("MIXT", [128, 8, T], BF16, "BC")
        bMIXT = [[Buf(f"MIXT[{c}][{g}]", a_MIXT) for g in range(NG)] for c in range(8)]
        QG, bQG = [], []
        for i in range(2):
            a, v = region(f"QG{i}", [96, 8, 512], BF16, "B")
            QG.append(v)
            bQG.append([Buf(f"QG{i}[{h}]", a) for h in range(8)])
        PT, bPT = [], []
        for i in range(3):
            a, v = region(f"PT{i}", [128, 2, 512], BF16, "B")
            PT.append(v)
            bPT.append(Buf(f"PT{i}", a))
        RDEN, bRDEN = [], []
        for i in range(2):
            a, v = region(f"RDEN{i}", [128, 512], F32, "B")
            RDEN.append(v)
            bRDEN.append(Buf(f"RDEN{i}", a))
        a_MM, MMASK = region("MMASK", [128, 256], BF16, "B")
        a_SM, SMASK = region("SMASK", [128, 2, 2, 512], BF16, "B")
        bMM, bSM = Buf("MMASK", a_MM), Buf("SMASK", a_SM)
        CSB, bCSB = [], []
        for i in range(1):
            a, v = region(f"CSB{i}", [32, 2, 512], F32, "B", p0=64)
            CSB.append(v)
            bCSB.append(Buf(f"CSB{i}", a))
        a_TQB, TQB = region("TQB", [32, 2, 2, 512], F32, "B", p0=64)
        bTQB = [[Buf(f"TQB[{s}][{i}]", a_TQB) for i in range(2)] for s in range(2)]
        a_SK8, SK8 = region("SK8", [128, 8], F32, "B")
        a_ESK, ESK = region("ESK", [128, 2, 512], F32, "B")
        bSK8, bESK = Buf("SK8", a_SK8), Buf("ESK", a_ESK)

        S.dma("pool", lambda h: h.dma_start(out=MMASK, in_=mmask_d), bMM, w=[bMM])
        S.dma("pool", lambda h: h.dma_start(out=SMASK.rearrange("p a b c -> p (a b c)"), in_=smask_d), bSM, w=[bSM])
        S.dma("sp", lambda h: h.dma_start(out=SK8, in_=bcast_row(sinks_d, 8)), bSK8, w=[bSK8])
        S.op("act", lambda h: h.activation(out=SK8, in_=SK8, func=AF.Exp), r=[bSK8], w=[bSK8])
        for hh in range(8):
            kv, g = hh // 4, hh % 4
            S.op("dve", lambda h, hh=hh, kv=kv, g=g: h.tensor_scalar(out=ESK[:, kv, g * 128:(g + 1) * 128], in0=ONESF,
                                                                  scalar1=SK8[:, hh:hh + 1], scalar2=None, op0=ALU.mult),
                 r=[bSK8, b_onesf], w=[bESK])

        MLA_SCALE = float(96 ** -0.5)
        SWA_SCALE = 0.125
        SPAIRS = [(0, 1), (2, 3), (4, 5)]
        OBANKS = [6, 7]
        NPT = 3
        DEPTH = 2
        sp_ctr = [0]
        pt_ctr = [0]
        ob_ctr = [0]
        rd_ctr = [0]

        def sc_view(pair, idx, c0, c1, rows=(0, 128)):
            return banks[pair[idx]][rows[0]:rows[1], c0:c1]

        class Unit:
            pass

        def load_cs(G):
            S.dma("sp", lambda h: h.dma_start(out=CSB[0], in_=cs_d[:, :, G * 512:(G + 1) * 512]), bCSB[0], w=[bCSB[0]])

        def qp_unit(G, hh):
            u = Unit()
            u.kind, u.G, u.h = "qp", G, hh
            u.ob = OBANKS[ob_ctr[0] % 2]
            return u

        def emit_qp_mm(u):
            G, hh = u.G, u.h
            gcols = slice(G * 512, (G + 1) * 512)
            bm = u.ob
            mm_group(bm, None, 512, [WQB[:, c, hh * 128:(hh + 1) * 128] for c in range(3)],
                     [QLN[:, c, gcols] for c in range(3)], [bQLN[G], bWQB])

        def emit_qp_evac(u):
            G, hh = u.G, u.h
            qs = G % 2
            ts = hh % 2
            bm = u.ob
            S.op("act", lambda h: h.activation(out=QG[qs][0:64, hh, :], in_=bk(bm, (0, 64)), func=AF.Copy),
                 r=[bankb[bm]], w=[bQG[qs][hh]])
            S.op("dve", lambda h: h.tensor_tensor(out=TQB[:, ts, 0, :], in0=bk(bm, (64, 96)), in1=CSB[0][:, 0, :],
                                                  op=ALU.mult), r=[bankb[bm], bCSB[0]], w=[bTQB[ts][0]])
            S.op("dve", lambda h: h.tensor_tensor(out=TQB[:, ts, 1, :], in0=bk(bm, (96, 128)), in1=CSB[0][:, 1, :],
                                                  op=ALU.mult), r=[bankb[bm], bCSB[0]], w=[bTQB[ts][1]])
            S.op("pool", lambda h: h.tensor_tensor(out=QG[qs][64:96, hh, :], in0=TQB[:, ts, 0, :],
                                                   in1=TQB[:, ts, 1, :], op=ALU.add),
                 r=bTQB[ts], w=[bQG[qs][hh]])

        def mla_units(G, hh):
            qs = G % 2
            units = []
            for j0 in range(0, 4 * G, 2):
                u = Unit()
                u.tiles = [(j0, 0, 512, None, 0), (j0 + 1, 0, 512, None, 0)]
                u.e0 = 0
                units.append(u)
            tri = MMASK[:, 128:256]
            negtri = MMASK[:, 0:256]
            u = Unit()
            u.tiles = [(4 * G, 0, 512, (0, 128, tri), 0), (4 * G + 1, 0, 512, (0, 256, negtri), 128)]
            u.e0 = 0
            units.append(u)
            u = Unit()
            u.tiles = [(4 * G + 2, 256, 512, (256, 128, tri), 256), (4 * G + 3, 256, 512, (256, 256, negtri), 384)]
            u.e0 = 256
            units.append(u)
            ob = OBANKS[ob_ctr[0] % 2]
            ob_ctr[0] += 1
            for u in units:
                u.G, u.h, u.qs, u.ob = G, hh, qs, ob
                u.kind = "mla"
            units[0].first = True
            units[-1].last = True
            return units

        def emit_scores(u):
            u.pair = SPAIRS[sp_ctr[0] % len(SPAIRS)]
            sp_ctr[0] += 1
            if u.kind == "qp":
                u.ob = u.pair[0]
                emit_qp_mm(u)
                return
            if u.kind == "mla":
                for idx, (j, c0, c1, msk, p0) in enumerate(u.tiles):
                    b = u.pair[idx]
                    S.op("pe", lambda h, j=j, c0=c0, c1=c1, b=b, msk=msk: h.matmul(
                        bk(b, None, (c0, c1)), lhsT=K_[0:96, u.h, j * 128:(j + 1) * 128], rhs=QG[u.qs][0:96, u.h, c0:c1],
                        start=True, stop=(msk is None)),
                         r=[bK[u.h][j // 4], bQG[u.qs][u.h]], w=[bankb[b]])
                    if msk is not None:
                        S.op("pe", lambda h, msk=msk, b=b: h.matmul(bk(b, None, (msk[0], msk[0] + msk[1])), lhsT=IDENTB,
                                                                    rhs=msk[2], start=False, stop=True),
                             r=[b_identb, bMM], w=[bankb[b]])
            else:
                n, kv = u.n, u.kv
                rows = (kv * 64, kv * 64 + 64)
                qv = SWQ[rows[0]:rows[1], n, :, :].rearrange("p a b -> p (a b)")
                for idx in ((0, 1) if n > 0 else (1,)):
                    tkt = n - 1 + idx
                    b = u.pair[idx]
                    S.op("pe", lambda h, tkt=tkt, b=b: h.matmul(bk(b), lhsT=SWK[rows[0]:rows[1], tkt * 128:(tkt + 1) * 128],
                                                                rhs=qv, start=True, stop=False),
                         r=[bSWK[tkt // 4], bSWQ[n // 4]], w=[bankb[b]])
                    S.op("pe", lambda h, idx=idx, b=b: h.matmul(bk(b), lhsT=IDENTB, rhs=SMASK[:, idx, kv, :],
                                                                start=False, stop=True),
                         r=[b_identb, bSM], w=[bankb[b]])

        def emit_exp(u):
            if u.kind == "qp":
                emit_qp_evac(u)
                return
            u.pt = pt_ctr[0] % NPT
            pt_ctr[0] += 1
            p = u.pt
            if u.kind == "mla":
                e0 = u.e0
                pairv = PS[:, u.pair[0]:u.pair[0] + 2, e0:512]
                S.op("act", lambda h: h.activation(out=PT[p][:, :, e0:512], in_=pairv, func=AF.Exp, scale=MLA_SCALE),
                     r=[bankb[u.pair[0]], bankb[u.pair[1]]], w=[bPT[p]])
            else:
                if u.n > 0:
                    pairv = bk_pair(u.pair[0])
                    S.op("act", lambda h: h.activation(out=PT[p], in_=pairv, func=AF.Exp, scale=SWA_SCALE),
                         r=[bankb[u.pair[0]], bankb[u.pair[1]]], w=[bPT[p]])
                else:
                    b = u.pair[1]
                    S.op("act", lambda h: h.activation(out=PT[p][:, 1, :], in_=bk(b), func=AF.Exp, scale=SWA_SCALE),
                         r=[bankb[b]], w=[bPT[p]])

        cur_ob = [None]
        deferred = []

        NFILL = 0

        def emit_pv(u):
            if u.kind == "qp":
                return
            p = u.pt
            for _ in range(NFILL):
                S.op("pe", lambda h: h.ldweights(IDENTB), r=[b_identb])
            if u.kind == "mla":
                ob = u.ob
                open_bank[0] = None if getattr(u, "last", False) else ob
                for idx, (j, c0, c1, msk, p0) in enumerate(u.tiles):
                    first = getattr(u, "first", False) and idx == 0
                    last = getattr(u, "last", False) and idx == len(u.tiles) - 1
                    S.op("pe", lambda h, j=j, p0=p0, idx=idx, first=first, last=last: h.matmul(
                        bk(ob, None, (p0, 512)), lhsT=vaug(j, u.h), rhs=PT[p][:, idx, p0:512], start=first, stop=last),
                         r=[bV[j], bVones, bPT[p]], w=[bankb[ob]])
                if getattr(u, "last", False):
                    hh, G = u.h, u.G
                    orow = (0, 64) if hh % 2 == 0 else (64, 128)
                    drow = (64, 128) if hh % 2 == 0 else (0, 64)
                    rs = rd_ctr[0] % 2
                    rd_ctr[0] += 1
                    if G >= 1:
                        S.op("dve", lambda h: h.reciprocal(out=RDEN[rs][orow[0]:orow[1], :], in_=bk(ob, drow)),
                             r=[bankb[ob]], w=[bRDEN[rs]])
                    else:
                        S.op("act", lambda h: h.activation(out=RDEN[rs][orow[0]:orow[1], :], in_=bk(ob, drow), func=AF.Ln),
                             r=[bankb[ob]], w=[bRDEN[rs]])
                        S.op("act", lambda h: h.activation(out=RDEN[rs][orow[0]:orow[1], :],
                                                           in_=RDEN[rs][orow[0]:orow[1], :], func=AF.Exp, scale=-1.0),
                             r=[bRDEN[rs]], w=[bRDEN[rs]])
                    S.op("dve", lambda h: h.tensor_tensor(out=MIXT[orow[0]:orow[1], hh // 2, G * 512:(G + 1) * 512],
                                                          in0=bk(ob, orow), in1=RDEN[rs][orow[0]:orow[1], :], op=ALU.mult),
                         r=[bankb[ob], bRDEN[rs]], w=[bMIXT[hh // 2][G]])
            else:
                n, kv = u.n, u.kv
                ob = OBANKS[ob_ctr[0] % 2]
                ob_ctr[0] += 1
                u.ob = ob
                idxs = (0, 1) if n > 0 else (1,)
                for ii, idx in enumerate(idxs):
                    tkt = n - 1 + idx
                    S.op("pe", lambda h, tkt=tkt, idx=idx, ii=ii: h.matmul(bk(ob), lhsT=swvaug(tkt, kv), rhs=PT[p][:, idx, :],
                                                                         start=(ii == 0), stop=(ii == len(idxs) - 1)),
                         r=[bSWV[tkt], bSWVones, bPT[p]], w=[bankb[ob]])
                orow = (0, 64) if kv == 0 else (64, 128)
                drow = (64, 128) if kv == 0 else (0, 64)
                rs = rd_ctr[0] % 2
                rd_ctr[0] += 1
                S.op("dve", lambda h: h.tensor_tensor(out=RDEN[rs][orow[0]:orow[1], :], in0=bk(ob, drow),
                                                      in1=ESK[orow[0]:orow[1], kv, :], op=ALU.add),
                     r=[bankb[ob], bESK], w=[bRDEN[rs]])
                S.op("act", lambda h: h.activation(out=RDEN[rs][orow[0]:orow[1], :], in_=RDEN[rs][orow[0]:orow[1], :],
                                                   func=AF.Ln), r=[bRDEN[rs]], w=[bRDEN[rs]])
                S.op("act", lambda h: h.activation(out=RDEN[rs][orow[0]:orow[1], :], in_=RDEN[rs][orow[0]:orow[1], :],
                                                   func=AF.Exp, scale=-1.0), r=[bRDEN[rs]], w=[bRDEN[rs]])
                G = n // 4

                def final_mult():
                    S.op("dve", lambda h: h.tensor_tensor(
                        out=MIXT[orow[0]:orow[1], 4:8, n * 128:(n + 1) * 128],
                        in0=bk(ob, orow).rearrange("p (g q) -> p g q", g=4, q=128),
                        in1=RDEN[rs][orow[0]:orow[1], :].rearrange("p (g q) -> p g q", g=4, q=128), op=ALU.mult),
                         r=[bankb[ob], bRDEN[rs]], w=[bMIXT[4 + g][G] for g in range(4)])
                deferred.append([1, final_mult])

        pending = []

        def run_deferred(force=False):
            for e in list(deferred):
                if force or e[0] <= 0:
                    deferred.remove(e)
                    e[1]()
                else:
                    e[0] -= 1

        qp_queue = []
        qp_free = []

        open_bank = [None]

        def issue_qp(bank):
            if bank == open_bank[0]:
                bank = OBANKS[1 - OBANKS.index(bank)]
            if qp_queue:
                G2, h2 = qp_queue.pop(0)
                u2 = qp_unit(G2, h2)
                u2.ob = bank
                emit_qp_mm(u2)
                emit_qp_evac(u2)

        def after_pv(v):
            pass

        def push_unit(u):
            emit_scores(u)
            emit_exp(u)
            run_deferred(force=True)
            pending.append(u)
            if len(pending) > DEPTH:
                v = pending.pop(0)
                emit_pv(v)
                after_pv(v)
            for e in list(qp_free):
                if e[0] <= 0:
                    qp_free.remove(e)
                    issue_qp(e[1])
                else:
                    e[0] -= 1

        def flush_units():
            while pending:
                run_deferred(force=True)
                v = pending.pop(0)
                emit_pv(v)
                after_pv(v)
            for e in list(qp_free):
                qp_free.remove(e)
                issue_qp(e[1])
            run_deferred(force=True)

        def swa_group(G):
            for n in range(4 * G, 4 * G + 4):
                for kv in range(2):
                    u = Unit()
                    u.kind, u.n, u.kv = "swa", n, kv
                    push_unit(u)
                    if qp_queue:
                        push_unit(qp_unit(*qp_queue.pop(0)))

        a_WO, WO = region("WO", [128, 8, D], BF16, "BC")
        bWO = Buf("WO", a_WO)
        a_G1, G1B = region("G1B", [128, D], F32, "B")
        bG1 = [Buf(f"G1B{i}", a_G1) for i in range(2)]

        def fold_wo_expand():
            expand_bc(lambda jj: MODCOL[:, 16 + jj:17 + jj], bMODCOL[8:12], G1B, bG1, [6, 7])

        def fold_wo_chunk(c):
            S.op("dve", lambda h: h.tensor_tensor(out=WO[:, c, :], in0=WO[:, c, :], in1=G1B, op=ALU.mult),
                 r=[bWO] + bG1, w=[bWO])

        load_cs(0)
        S.dma("pool", lambda h: h.dma_start(out=WO, in_=wo_d.rearrange("(c p) n -> p c n", p=128)), bWO, w=[bWO])
        for hh in range(2):
            push_unit(qp_unit(0, hh))
        for G in range(NG):
            if G == NG - 1:
                fold_wo_expand()
            for hh in range(8):
                for iu, u in enumerate(mla_units(G, hh)):
                    push_unit(u)
                    if G == 0 and iu == 0 and hh + 2 < 8:
                        push_unit(qp_unit(0, hh + 2))
                if G == NG - 1:
                    fold_wo_chunk(hh)
            if G + 1 < NG:
                assert not qp_queue
                load_cs(G + 1)
                qp_queue.extend((G + 1, hh) for hh in range(8))
            swa_group(G)
        flush_units()

        a_X1, X1 = region("X1", [128, NT, D], F32, "CDEF")
        bX1 = [Buf(f"X1[{t}]", a_X1) for t in range(NT)]
        a_BC2A, BC2A = region("BC2A", [128, D], F32, "C")
        a_BC2B, BC2B = region("BC2B", [128, D], F32, "C")
        bBC2A = [Buf(f"BC2A{i}", a_BC2A) for i in range(2)]
        bBC2B = [Buf(f"BC2B{i}", a_BC2B) for i in range(2)]
        a_H2T, H2T = region("H2T", [128, 8, T], BF16, "CE")
        bH2T = [Buf(f"H2T[{t}]", a_H2T) for t in range(NT)]
        T1D, bT1D, HBD, bHBD = make_norm_bufs("C")
        WUP, bWUP, WDN, bWDN = [], [], [], []
        for i in range(2):
            a, v = region(f"WUP{i}", [128, 8, 1024], BF16, "CE" if i == 0 else "E")
            WUP.append(v)
            bWUP.append(Buf(f"WUP{i}", a))
            a, v = region(f"WDN{i}", [128, 8, 1024], BF16, "CE" if i == 0 else "E")
            WDN.append(v)
            bWDN.append(Buf(f"WDN{i}", a))
        wup_v = wup_d.rearrange("(c p) n -> p c n", p=128)
        wdn_v = wdn_d.rearrange("(q c p) n -> q p c n", q=4, p=128)

        def load_pass(q):
            s_ = q % 2
            for c in range(8):
                S.dma("pool", lambda h, c=c: h.dma_start(out=WUP[s_][:, c, :], in_=wup_v[:, c, q * 1024:(q + 1) * 1024]),
                      bWUP[s_], w=[bWUP[s_]] if c == 0 else [])
            S.dma("pool", lambda h: h.dma_start(out=WDN[s_], in_=wdn_v[q]), bWDN[s_], w=[bWDN[s_]])

        for t in range(NT):
            S.dma("sp", lambda h, t=t: h.dma_start(out=X1[:, t, :], in_=x_d[t * 128:(t + 1) * 128, :]), bX1[t], w=[bX1[t]])
        load_pass(0)
        a_G2, G2B = region("G2B", [128, D], F32, "CE")
        bG2 = [Buf(f"G2B{i}", a_G2) for i in range(2)]
        cb = [0]

        def c_wo(t):
            for half in range(2):
                b = cb[0] % 4
                cb[0] += 1
                mm_group(b, None, 512, [MIXT[:, c, t * 128:(t + 1) * 128] for c in range(8)],
                         [WO[:, c, half * 512:(half + 1) * 512] for c in range(8)],
                         [bMIXT[c][t // 4] for c in range(8)] + [bWO])
                S.op("dve", lambda h, half=half, b=b: h.tensor_tensor(out=X1[:, t, half * 512:(half + 1) * 512], in0=bk(b),
                                                                      in1=X1[:, t, half * 512:(half + 1) * 512], op=ALU.add),
                     r=[bankb[b], bX1[t]], w=[bX1[t]])

        def c_pre(t):
            norm_pre(X1[:, t, :], [bX1[t]], T1D, bT1D, HBD, bHBD, t % 2, BC2B, bBC2B, BC2A, bBC2A, f"n2_{t}")

        def c_tr(t):
            norm_tr(HBD, bHBD, t % 2, 6 + (t % 2), H2T[:, :, t * 128:(t + 1) * 128], [bH2T[t]], evac="act")

        c_wo(0)
        c_wo(1)
        expand_bc(lambda jj: GCOLS[:, 3, jj:jj + 1], [bGCOLS[3]], BC2B, bBC2B, [4, 5])
        expand_bc(lambda jj: MODCOL[:, 24 + jj:25 + jj], bMODCOL[12:16], BC2A, bBC2A, [4, 5])
        c_pre(0)
        for t in range(NT):
            if t + 2 < NT:
                c_wo(t + 2)
            if t + 1 < NT:
                c_pre(t + 1)
            c_tr(t)
            if t == 8:
                expand_bc(lambda jj: MODCOL[:, 40 + jj:41 + jj], bMODCOL[20:24], G2B, bG2, [4, 5])

        UT, bUT = [], []
        for i in range(2):
            a, v = region(f"UT{i}", [128, 8, 512], BF16, "E")
            UT.append(v)
            bUT.append([Buf(f"UT{i}[{f}]", a) for f in range(8)])
        RT, bRT, GT, bGT = [], [], [], []
        for i in range(2):
            a, v = region(f"RT{i}", [128, 512], F32, "E")
            RT.append(v)
            bRT.append(Buf(f"RT{i}", a))
            a, v = region(f"GT{i}", [128, 512], F32, "E")
            GT.append(v)
            bGT.append(Buf(f"GT{i}", a))
        load_pass(1)

        ub = [0]
        rt_ctr = [0]

        def mlp_up(q, G):
            s_ = q % 2
            us = (q * 4 + G) % 2
            for fc in range(8):
                b = ub[0] % 3
                ub[0] += 1
                mm_group(b, None, 512, [WUP[s_][:, k, fc * 128:(fc + 1) * 128] for k in range(8)],
                         [H2T[:, k, G * 512:(G + 1) * 512] for k in range(8)], [bH2T[G * 4 + j] for j in range(4)] + [bWUP[s_]])
                rs = rt_ctr[0] % 2
                rt_ctr[0] += 1
                S.op("act", lambda h, b=b, rs=rs: h.activation(out=RT[rs], in_=bk(b), func=AF.Relu), r=[bankb[b]], w=[bRT[rs]])
                S.op("act", lambda h, rs=rs, fc=fc: h.activation(out=UT[us][:, fc, :], in_=RT[rs], func=AF.Square),
                     r=[bRT[rs]], w=[bUT[us][fc]])

        db = [0]

        def mlp_down(q, G):
            s_ = q % 2
            us = (q * 4 + G) % 2
            for j in range(4):
                t = G * 4 + j
                for half in range(2):
                    b = 3 + db[0] % 3
                    gs_ = db[0] % 2
                    db[0] += 1
                    mm_group(b, None, 512, [UT[us][:, fc, j * 128:(j + 1) * 128] for fc in range(8)],
                             [WDN[s_][:, fc, half * 512:(half + 1) * 512] for fc in range(8)], bUT[us] + [bWDN[s_]])
                    S.op("dve", lambda h, half=half, b=b, gs_=gs_: h.tensor_tensor(
                        out=GT[gs_], in0=bk(b), in1=G2B[:, half * 512:(half + 1) * 512], op=ALU.mult),
                         r=[bankb[b], bG2[half]], w=[bGT[gs_]])
                    S.op("dve", lambda h, t=t, half=half, gs_=gs_: h.tensor_tensor(
                        out=X1[:, t, half * 512:(half + 1) * 512], in0=GT[gs_], in1=X1[:, t, half * 512:(half + 1) * 512],
                        op=ALU.add), r=[bGT[gs_], bX1[t]], w=[bX1[t]])

        a_FG, FGB = region("FGB", [128, D], F32, "EF")
        bFG = Buf("FGB", a_FG)
        S.dma("sp", lambda h: h.dma_start(out=FGB, in_=bcast_row(fg_d, D)), bFG, w=[bFG])

        def final_tile(t):
            rstd, bs = rms_rstd(X1[:, t, :], [bX1[t]], D, f"nf_{t}")
            S.op("dve", lambda h: h.scalar_tensor_tensor(out=X1[:, t, :], in0=X1[:, t, :], scalar=rstd,
                                                         in1=FGB, op0=ALU.mult, op1=ALU.mult),
                 r=[bX1[t], bs, bFG], w=[bX1[t]])
            od = S.dma("sp", lambda h: h.dma_start(out=out_d[t * 128:(t + 1) * 128, :], in_=X1[:, t, :]), bX1[t],
                       r=[bX1[t]])
            S.final.append(od)

        seq = [(q, G) for q in range(4) for G in range(NG)]
        for i, (q, G) in enumerate(seq):
            if i == 0:
                mlp_up(q, G)
            if i + 1 < len(seq):
                mlp_up(*seq[i + 1])
            mlp_down(q, G)
            if G == NG - 1 and q + 2 < 4:
                load_pass(q + 2)
            if q == 3:
                for j in range(4):
                    final_tile(G * 4 + j)


        block = es.enter_context(nc.Block())
        S.emit(nc, es, block, None)
    return nc


def _host_consts():
    identf = np.eye(128, dtype=np.float32)
    onesf = np.ones((128, 128), dtype=np.float32)
    i = np.arange(32)
    freqs = 10000.0 ** (-(i % 16).astype(np.float64) / 16.0)
    ang = freqs[:, None] * np.arange(T, dtype=np.float64)[None, :]
    cs = np.stack([np.cos(ang), np.sin(ang)], axis=1).astype(np.float32)
    tk = np.arange(128)[:, None]
    tq = np.arange(128)[None, :]
    mmask = np.concatenate([np.full((128, 128), NEG), np.where(tk <= tq, 0.0, NEG)], axis=1).astype(np.float32)
    smask = np.zeros((128, 2, 2, 4, 128), dtype=np.float32)
    for kv in range(2):
        for g in range(4):
            h = kv * 4 + g
            slope = 2.0 ** (-(h + 1))
            d_own = (tq - tk).astype(np.float64)
            d_prev = (128 + tq - tk).astype(np.float64)
            smask[:, 1, kv, g, :] = np.where(tk <= tq, -slope * d_own * 8.0, NEG)
            smask[:, 0, kv, g, :] = np.where(tk > tq, -slope * d_prev * 8.0, NEG)
    return identf, onesf, cs, mmask, smask.reshape(128, 2048)


_NC_CACHE = {}


def kernel(x, c, w_ada, b_ada, norm_mix_g, w_in, g_qa, w_qb, g_kva, w_kvb, sinks,
           w_o, norm_mlp_g, w_up, w_down, final_g):
    f = lambda a: np.ascontiguousarray(np.asarray(a, dtype=np.float32))
    x, c = f(x), f(c)
    w_in0 = f(w_in)[0]
    o3 = 672
    swq = w_in0[:, o3:o3 + 512].reshape(D, 2, 4, 64).transpose(0, 2, 1, 3).reshape(D, 512)
    kr = w_in0[:, 640:672]
    win_l = np.ascontiguousarray(np.concatenate(
        [w_in0[:, 0:672], swq, w_in0[:, o3 + 512:1440], kr[:, 16:32], kr[:, 0:16]], axis=1))
    wqb0 = f(w_qb)[0]
    wqb_l = np.ascontiguousarray(np.concatenate(
        [wqb0[:, :, 0:96], wqb0[:, :, 80:96], wqb0[:, :, 64:80]], axis=2).reshape(384, 1024))
    wkvb0 = f(w_kvb)[0]
    wkvk_l = np.ascontiguousarray(wkvb0[:, :, 0:64].reshape(256, 512))
    wkvv_l = np.ascontiguousarray(wkvb0[:, :, 64:128].reshape(256, 512))
    wo0 = f(w_o)[0]
    wo_swa = wo0[512:].reshape(2, 4, 64, D).transpose(1, 0, 2, 3).reshape(512, D)
    wo_l = np.ascontiguousarray(np.concatenate([wo0[:512], wo_swa], axis=0))
    identf, onesf, cs, mmask, smask = _host_consts()
    shared = {
        "badacol": np.ascontiguousarray(f(b_ada)[0].reshape(48, 128).T),
        "gmixcol": np.ascontiguousarray(f(norm_mix_g)[0].reshape(8, 128).T),
        "gmlpcol": np.ascontiguousarray(f(norm_mlp_g)[0].reshape(8, 128).T), "fg": f(final_g).reshape(1, D),
        "gqa": np.ascontiguousarray(f(g_qa)[0].reshape(3, 128).T), "gkva": np.ascontiguousarray(f(g_kva)[0].reshape(2, 128).T),
        "sinks": f(sinks)[0].reshape(1, 8), "wada": f(w_ada)[0], "win": win_l, "wqb": wqb_l, "wkvk": wkvk_l,
        "wkvv": wkvv_l, "wo": wo_l, "wup": f(w_up)[0], "wdn": f(w_down)[0],
        "identf": identf, "onesf": onesf, "cs": cs, "mmask": mmask, "smask": smask,
    }
    in_maps = []
    for b in range(8):
        m = dict(shared)
        m["x"] = x[b]
        m["ccol"] = np.ascontiguousarray(c[b].reshape(8, 128).T)
        in_maps.append(m)
    if "nc" not in _NC_CACHE:
        _NC_CACHE["nc"] = build_program()
    nc = _NC_CACHE["nc"]
    res = run_bass_kernel_spmd(nc, in_maps, core_ids=list(range(8)))
    out = np.stack([np.asarray(res.results[b]["out"], dtype=np.float32).reshape(T, D) for b in range(8)], axis=0)
    return out
```

```python
import numpy as np
import concourse.bass as bass
import concourse.mybir as mybir
from concourse.bass_utils import run_bass_kernel_spmd
from contextlib import ExitStack

F32 = mybir.dt.float32
BF16 = mybir.dt.bfloat16
ALU = mybir.AluOpType
AF = mybir.ActivationFunctionType

T = 2048
D = 1024
NT = 16
NG = 4
EPS = 1e-6
NEG = -1.0e6
WINC = 1472
PH_ALL = "SABCDEF"

DEBUG_DUMPS = False


def _sz(dt):
    return 2 if dt == BF16 else 4


class Alloc:
    def __init__(self, name, lo, hi, phases):
        self.name, self.lo, self.hi, self.phases = name, lo, hi, phases
        self.bufs = []
        self.overl = []


class Buf:
    __slots__ = ("name", "alloc", "w", "r", "dsem", "dcnt")

    def __init__(self, name, alloc=None):
        self.name, self.alloc = name, alloc
        self.w = None
        self.r = []
        self.dsem = None
        self.dcnt = 0
        if alloc is not None:
            alloc.bufs.append(self)


class Op:
    __slots__ = ("eng", "fn", "deps", "sig", "cnt", "dma", "dbuf", "dval")

    def __init__(self, eng, fn):
        self.eng, self.fn = eng, fn
        self.deps = []
        self.sig = False
        self.cnt = 0
        self.dma = False
        self.dbuf = None
        self.dval = 0


class Sched:
    ENGS = ["pe", "act", "dve", "pool", "sp"]

    def __init__(self):
        self.ops = {e: [] for e in self.ENGS}
        self.final = []

    def _add_dep(self, op, d, kind):
        if d is None or d is op:
            return
        if (not d.dma) and d.eng == op.eng:
            if op.eng == "pe" or op.eng == "sp":
                return
        op.deps.append((d, d.dbuf.dcnt if d.dma else 0))

    def _collect(self, op, r, w):
        for b in r:
            self._add_dep(op, b.w, "raw")
        for b in w:
            self._add_dep(op, b.w, "waw")
            for x in b.r:
                self._add_dep(op, x, "war")
            if b.alloc is not None:
                for a2 in b.alloc.overl:
                    for b2 in a2.bufs:
                        self._add_dep(op, b2.w, "waw")
                        for x in b2.r:
                            self._add_dep(op, x, "waw")
        for b in r:
            b.r.append(op)
        for b in w:
            b.w = op
            b.r = []

    def op(self, eng, fn, r=(), w=()):
        o = Op(eng, fn)
        self._collect(o, r, w)
        self.ops[eng].append(o)
        return o

    def dma(self, q, fn, key, r=(), w=()):
        o = Op(q, fn)
        o.dma = True
        o.dbuf = key
        self._collect(o, r, w)
        key.dcnt += 16
        o.dval = key.dcnt
        self.ops[q].append(o)
        return o

    def emit(self, nc, es, block, handles):
        sem_budget = [0]

        def newsem(name):
            sem_budget[0] += 1
            return es.enter_context(nc.semaphore(name))

        esem = {e: newsem("s_" + e) for e in ["pe", "act", "dve", "pool"]}
        for e in self.ENGS:
            for o in self.ops[e]:
                for (d, dv) in o.deps:
                    if d.dma:
                        if d.dbuf.dsem is None:
                            d.dbuf.dsem = newsem("d_" + d.dbuf.name.replace("[", "_").replace("]", ""))
                    else:
                        d.sig = True
            for o in self.ops[e]:
                if o.dma and o.dbuf.dsem is None:
                    o.dbuf.dsem = newsem("d_" + o.dbuf.name.replace("[", "_").replace("]", ""))
        for o in self.final:
            if o.dbuf.dsem is None:
                o.dbuf.dsem = newsem("d_" + o.dbuf.name)
        for e in self.ENGS:
            c = 0
            for o in self.ops[e]:
                if o.sig:
                    c += 1
                o.cnt = c

        def run(e, h):
            seen = {}
            for o in self.ops[e]:
                need = {}
                for (d, dv) in o.deps:
                    if d.dma:
                        k, v = d.dbuf.dsem, dv
                    else:
                        k, v = esem[d.eng], d.cnt
                    kk = k.num
                    if seen.get(kk, 0) >= v:
                        continue
                    if kk not in need or need[kk][1] < v:
                        need[kk] = (k, v)
                for kk, (k, v) in need.items():
                    h.wait_ge(k, v)
                    seen[kk] = v
                ins = o.fn(h)
                if o.dma:
                    ins.then_inc(o.dbuf.dsem, 16)
                elif o.sig:
                    ins.then_inc(esem[e], 1)
            if e == "sp":
                fin = {}
                for o in self.final:
                    kk = o.dbuf.dsem.num
                    if kk not in fin or fin[kk][1] < o.dval:
                        fin[kk] = (o.dbuf.dsem, o.dval)
                for kk, (k, v) in fin.items():
                    h.wait_ge(k, v)

        @block.tensor
        def _(h):
            run("pe", h)

        @block.scalar
        def _(h):
            run("act", h)

        @block.vector
        def _(h):
            run("dve", h)

        @block.gpsimd
        def _(h):
            run("pool", h)

        @block.sync
        def _(h):
            run("sp", h)


class Arena:
    def __init__(self, nbytes):
        self.nbytes = nbytes
        self.allocs = []
        self.t = None

    def alloc(self, name, nbytes, phases):
        nbytes = (nbytes + 63) // 64 * 64
        lo = 0
        placed = None
        ivs = sorted((a.lo, a.hi) for a in self.allocs if set(a.phases) & set(phases))
        for (l, h) in ivs:
            if l - lo >= nbytes:
                placed = lo
                break
            lo = max(lo, h)
        if placed is None:
            if self.nbytes - lo >= nbytes:
                placed = lo
            else:
                raise RuntimeError(f"arena OOM for {name} ({nbytes}B, phases {phases}); top={lo}")
        a = Alloc(name, placed, placed + nbytes, phases)
        for o in self.allocs:
            if o.lo < a.hi and a.lo < o.hi:
                o.overl.append(a)
                a.overl.append(o)
        self.allocs.append(a)
        return a

    def ap(self, a, shape, dtype, byte_off=0, p0=0):
        free = 1
        for s in shape[1:]:
            free *= s
        nb = free * _sz(dtype)
        assert a.lo + byte_off + nb <= a.hi, (a.name, shape)
        e0 = (a.lo + byte_off) // 2
        v = self.t[p0:p0 + shape[0], e0:e0 + nb // 2]
        if dtype != BF16:
            v = v.bitcast(dtype)
        if len(shape) == 3:
            v = v.rearrange("p (a b) -> p a b", a=shape[1], b=shape[2])
        elif len(shape) == 4:
            v = v.rearrange("p (a b c) -> p a b c", a=shape[1], b=shape[2], c=shape[3])
        return v

    def __lt__(self, o):
        return False


Alloc.__lt__ = lambda self, o: self.lo < o.lo


def build_program():
    nc = bass.Bass("TRN2", target_bir_lowering=False)
    S = Sched()
    AR = Arena(207 * 1024)
    _NC_CACHE["arena"] = AR

    def din(name, shape, dt=F32):
        return nc.dram_tensor(name, list(shape), dt, kind="ExternalInput").ap()

    x_d = din("x", [T, D])
    ccol_d = din("ccol", [128, 8])
    badacol_d = din("badacol", [128, 48])
    gmixcol_d = din("gmixcol", [128, 8])
    gmlpcol_d = din("gmlpcol", [128, 8])
    fg_d = din("fg", [1, D])
    gqa_d = din("gqa", [128, 3])
    gkva_d = din("gkva", [128, 2])
    sinks_d = din("sinks", [1, 8])
    wada_d = din("wada", [D, 6144])
    win_d = din("win", [D, WINC])
    wqb_d = din("wqb", [384, 1024])
    wkvk_d = din("wkvk", [256, 512])
    wkvv_d = din("wkvv", [256, 512])
    wo_d = din("wo", [D, D])
    wup_d = din("wup", [D, 4096])
    wdn_d = din("wdn", [4096, D])
    identf_d = din("identf", [128, 128])
    onesf_d = din("onesf", [128, 128])
    cs_d = din("cs", [32, 2, T])
    mmask_d = din("mmask", [128, 256])
    smask_d = din("smask", [128, 2048])
    out_d = nc.dram_tensor("out", [T, D], F32, kind="ExternalOutput").ap()
    dbg = {}

    es = ExitStack()
    with es:
        AR.t = es.enter_context(nc.sbuf_tensor("arena", [128, AR.nbytes // 2], BF16))
        PS = es.enter_context(nc.psum_tensor("ps", [128, 8, 512], F32))
        bankb = [Buf(f"bank{i}") for i in range(8)]

        def bk(i, rows=None, cols=None):
            r0, r1 = rows if rows else (0, 128)
            c0, c1 = cols if cols else (0, 512)
            return PS[r0:r1, i, c0:c1]

        def bk_bf(i):
            return PS[:, i, :].bitcast(BF16)

        def bk_pair(i):
            return PS[:, i:i + 2, :]

        KB = 1024

        prealloc = {}

        def region(name, shape, dtype, phases, p0=0):
            free = 1
            for s in shape[1:]:
                free *= s
            nb = free * _sz(dtype)
            if name in prealloc:
                a = prealloc[name]
                assert a.phases == phases and a.hi - a.lo >= nb, name
            else:
                a = AR.alloc(name, nb, phases)
            return a, AR.ap(a, shape, dtype, p0=p0)

        for (nm, nbytes, ph) in [
            ("identb", 256, PH_ALL), ("onesb", 256, PH_ALL), ("onesf", 512, PH_ALL), ("stat", 768, PH_ALL),
            ("junk", 2048, PH_ALL), ("SILU", 16, PH_ALL), ("IDENTF", 512, PH_ALL), ("MODCOL", 192, PH_ALL),
            ("GCOLS", 128, PH_ALL), ("DIAG0", 512, PH_ALL), ("DIAG1", 512, PH_ALL), ("EPSC", 4, PH_ALL),
            ("X1", NT * D * 4, "CDEF"), ("H2T", 8 * T * 2, "CE"), ("WUP0", 16384, "CE"), ("WDN0", 16384, "CE"),
            ("BC2A", 4096, "C"), ("BC2B", 4096, "C"), ("T1C", 4096, "C"), ("HBC0", 2048, "C"), ("HBC1", 2048, "C"),
            ("MIXT", 8 * T * 2, "BC"), ("WO", 16384, "BC"),
        ]:
            prealloc[nm] = AR.alloc(nm, nbytes, ph)

        a_identb, IDENTB = region("identb", [128, 128], BF16, PH_ALL)
        a_onesb, ONESB = region("onesb", [128, 128], BF16, PH_ALL)
        a_onesf, ONESF = region("onesf", [128, 128], F32, PH_ALL)
        a_stat, STAT = region("stat", [128, 192], F32, PH_ALL)
        b_identb, b_onesb, b_onesf = Buf("identb", a_identb), Buf("onesb", a_onesb), Buf("onesf", a_onesf)
        a_junk, JUNK = region("junk", [128, 1024], BF16, PH_ALL)
        a_EPS, EPSC = region("EPSC", [128, 1], F32, PH_ALL)
        bEPS = Buf("EPSC", a_EPS)
        S.op("dve", lambda h: h.memset(EPSC, EPS), w=[bEPS])

        S.dma("pool", lambda h: h.dma_start(out=IDENTB, in_=identf_d), b_identb, w=[b_identb])
        S.dma("pool", lambda h: h.dma_start(out=ONESB, in_=onesf_d), b_onesb, w=[b_onesb])
        S.dma("sp", lambda h: h.dma_start(out=ONESF, in_=onesf_d), b_onesf, w=[b_onesf])

        a_K, K_ = region("K", [96, 8, T], BF16, "AB")
        a_V = AR.alloc("V", 16 * 768 * 2, "AB")
        a_SWQ, SWQ = region("SWQ", [128, NT, 4, 128], BF16, "AB")
        a_SWK, SWK = region("SWK", [128, T], BF16, "AB")
        a_SWV = AR.alloc("SWV", 16 * 192 * 2, "AB")
        a_QLN, QLN = region("QLN", [128, 3, T], BF16, "AB")
        bK = [[Buf(f"K[{h}][{g}]", a_K) for g in range(NG)] for h in range(8)]
        bV = [Buf(f"V[{t}]", a_V) for t in range(NT)]
        bVones = Buf("Vones", a_V)
        bSWQ = [Buf(f"SWQ[{g}]", a_SWQ) for g in range(NG)]
        bSWK = [Buf(f"SWK[{g}]", a_SWK) for g in range(NG)]
        bSWV = [Buf(f"SWV[{t}]", a_SWV) for t in range(NT)]
        bSWVones = Buf("SWVones", a_SWV)
        bQLN = [Buf(f"QLN[{g}]", a_QLN) for g in range(NG)]

        V_E0 = a_V.lo // 2
        SWV_E0 = a_SWV.lo // 2
        PSTR = AR.nbytes // 2
        arena_handle = AR.t

        def rap(off, dims, parts=128):
            return bass.AP(tensor=arena_handle, offset=off, ap=[[PSTR, parts]] + [list(d) for d in dims])

        def v_tile_out(t):
            return rap(V_E0 + t * 768, [[192, 4], [128, 2], [1, 64]])

        def vaug(t, h):
            base = V_E0 + t * 768 + (h // 2) * 192
            return rap(base + (0 if h % 2 == 0 else 64), [[1, 128]])

        def swv_tile_out(t):
            return rap(SWV_E0 + t * 192, [[128, 2], [1, 64]])

        def swvaug(t, kv):
            return rap(SWV_E0 + t * 192 + (0 if kv == 0 else 64), [[1, 128]])

        S.op("dve", lambda h: h.memset(rap(V_E0 + 64, [[192, 64], [1, 64]]), 1.0), w=[bVones])
        S.op("dve", lambda h: h.memset(rap(SWV_E0 + 64, [[192, 16], [1, 64]]), 1.0), w=[bSWVones])

        a_WIN, WIN = region("WIN", [128, 8, WINC], BF16, "SA")
        a_WKVK, WKVK = region("WKVK", [128, 2, 512], BF16, "SA")
        a_WKVV, WKVV = region("WKVV", [128, 2, 512], BF16, "SA")
        a_WQB, WQB = region("WQB", [128, 3, 1024], BF16, "SAB")
        a_GQA, GQA = region("GQA", [128, 3], F32, "SAB")
        a_GKVA, GKVA = region("GKVA", [128, 2], F32, "SAB")
        bWINL, bWIN, bWKVK, bWKVV, bWQB = (Buf("WINL", a_WIN), Buf("WIN", a_WIN), Buf("WKVK", a_WKVK), Buf("WKVV", a_WKVV),
                                             Buf("WQB", a_WQB))
        bGQA, bGKVA = Buf("GQA", a_GQA), Buf("GKVA", a_GKVA)

        a_CCOL, CCOL = region("CCOL", [128, 8], F32, "S")
        a_SILU, SILU = region("SILU", [128, 8], BF16, PH_ALL)
        bCCOL, bSILU = Buf("CCOL", a_CCOL), Buf("SILU", a_SILU)
        a_IDF, IDENTF = region("IDENTF", [128, 128], F32, PH_ALL)
        b_identf = Buf("IDENTF", a_IDF)
        a_MODCOL, MODCOL = region("MODCOL", [128, 48], F32, PH_ALL)
        bMODCOL = [Buf(f"MODCOL[{j}]", a_MODCOL) for j in range(24)]
        a_BADAC, BADACOL = region("BADACOL", [128, 48], F32, "SA")
        bBADAC = Buf("BADACOL", a_BADAC)
        a_GCOLS, GCOLS = region("GCOLS", [128, 4, 8], F32, PH_ALL)
        bGCOLS = [Buf(f"GCOLS[{i}]", a_GCOLS) for i in range(4)]
        DIAG, bDIAG = [], []
        for i in range(2):
            a, v = region(f"DIAG{i}", [128, 128], F32, PH_ALL)
            DIAG.append(v)
            bDIAG.append(Buf(f"DIAG{i}", a))
        WADA, bWADA = [], []
        for i in range(2):
            a, v = region(f"WADA{i}", [128, 8, 256], BF16, "SA")
            WADA.append(v)
            bWADA.append(Buf(f"WADA{i}", a))

        S.dma("sp", lambda h: h.dma_start(out=CCOL, in_=ccol_d), bCCOL, w=[bCCOL])
        S.op("act", lambda h: h.activation(out=SILU, in_=CCOL, func=AF.Silu), r=[bCCOL], w=[bSILU])
        S.dma("sp", lambda h: h.dma_start(out=IDENTF, in_=identf_d), b_identf, w=[b_identf])
        S.dma("sp", lambda h: h.dma_start(out=BADACOL, in_=badacol_d), bBADAC, w=[bBADAC])
        S.dma("sp", lambda h: h.dma_start(out=GCOLS[:, 0, :], in_=gmixcol_d), bGCOLS[0], w=[bGCOLS[0]])
        S.dma("sp", lambda h: h.dma_start(out=GCOLS[:, 1, :], in_=gmlpcol_d), bGCOLS[1], w=[bGCOLS[1]])

        ada_ctr = [0]
        wada_v = wada_d.rearrange("(c p) n -> p c n", p=128)

        ada_issued = set()
        raw_issued = set()

        STG = [AR.ap(a_K, [128, 8, 256], F32, byte_off=i * 8192) for i in range(2)]
        a_STG = Alloc("STG", a_K.lo, a_K.lo + 16384, "S")
        a_STG.overl = [a_K] + list(a_K.overl)
        for o_ in a_STG.overl:
            o_.overl.append(a_STG)
        bSTG = [Buf(f"STG{i}", a_STG) for i in range(2)]

        STAGED = [1, 3, 5, 6, 7]

        def adaln_raw(n2):
            if n2 in raw_issued or n2 not in STAGED:
                return
            raw_issued.add(n2)
            sg = STAGED.index(n2) % 2
            S.dma("sp", lambda h: h.dma_start(out=STG[sg], in_=wada_v[:, :, n2 * 256:(n2 + 1) * 256]),
                  bSTG[sg], w=[bSTG[sg]])

        def adaln_dma(n2):
            if n2 in ada_issued:
                return
            ada_issued.add(n2)
            s_ = n2 % 2
            if n2 in STAGED:
                adaln_raw(n2)
                i_ = STAGED.index(n2)
                sg = i_ % 2
                S.op("dve", lambda h: h.tensor_copy(out=WADA[s_], in_=STG[sg]), r=[bSTG[sg]], w=[bWADA[s_]])
                if i_ + 2 < len(STAGED):
                    adaln_raw(STAGED[i_ + 2])
                return
            S.dma("pool", lambda h: h.dma_start(out=WADA[s_], in_=wada_v[:, :, n2 * 256:(n2 + 1) * 256]),
                  bWADA[s_], w=[bWADA[s_]])

        def adaln_cols(n2, bank):
            s_ = n2 % 2
            adaln_dma(n2)
            for half in range(2):
                j = 2 * n2 + half
                for k in range(8):
                    S.op("pe", lambda h, k=k, j=j, half=half: h.matmul(
                        bk(bank, None, (j, j + 1)), lhsT=WADA[s_][:, k, half * 128:(half + 1) * 128], rhs=SILU[:, k:k + 1],
                        start=(k == 0), stop=(k == 7)), r=[bSILU, bWADA[s_]], w=[bankb[bank]])
            S.op("dve", lambda h: h.tensor_tensor(out=MODCOL[:, 2 * n2:2 * n2 + 2], in0=bk(bank, None, (2 * n2, 2 * n2 + 2)),
                                                  in1=BADACOL[:, 2 * n2:2 * n2 + 2], op=ALU.add),
                 r=[bankb[bank], bBADAC], w=[bMODCOL[n2]])
            if n2 + 2 < 24 and not (n2 < 8 <= n2 + 2):
                adaln_dma(n2 + 2)

        dg_ctr = [0]

        def expand_bc(col_ap_fn, col_bufs, dst, dst_bufs, bank_list):
            for half in range(2):
                b = bank_list[half % len(bank_list)]
                for q in range(4):
                    jj = half * 4 + q
                    ds = dg_ctr[0] % 2
                    dg_ctr[0] += 1
                    S.op("dve", lambda h, jj=jj, ds=ds: h.tensor_scalar(out=DIAG[ds], in0=IDENTF, scalar1=col_ap_fn(jj),
                                                                        scalar2=None, op0=ALU.mult),
                         r=[b_identf] + col_bufs, w=[bDIAG[ds]])
                    S.op("pe", lambda h, q=q, ds=ds, b=b: h.matmul(bk(b, None, (q * 128, (q + 1) * 128)), lhsT=ONESF,
                                                                    rhs=DIAG[ds], start=True, stop=True),
                         r=[bDIAG[ds], b_onesf], w=[bankb[b]])
                S.op("dve", lambda h, half=half, b=b: h.tensor_copy(out=dst[:, half * 512:(half + 1) * 512], in_=bk(b)),
                     r=[bankb[b]], w=[dst_bufs[half]])

        a_BCA, BCA = region("BCA", [128, D], F32, "SA")
        a_BCB, BCB = region("BCB", [128, D], F32, "SA")
        bBCA = [Buf(f"BCA{i}", a_BCA) for i in range(2)]
        bBCB = [Buf(f"BCB{i}", a_BCB) for i in range(2)]

        def bcast_row(d_ap, n):
            return bass.AP(tensor=d_ap.tensor, offset=d_ap.offset, ap=[[0, 128], [1, n]])

        win_v = win_d.rearrange("(c p) n -> p c n", p=128)
        adaln_raw(1)
        adaln_raw(3)
        for n2 in range(8):
            adaln_cols(n2, n2 % 2)
            if n2 == 3:
                expand_bc(lambda jj: MODCOL[:, jj:jj + 1], bMODCOL[0:4], BCA, bBCA, [2, 3])
        S.op("dve", lambda h: h.scalar_tensor_tensor(out=GCOLS[:, 2, :], in0=MODCOL[:, 8:16], scalar=1.0, in1=GCOLS[:, 0, :],
                                                     op0=ALU.add, op1=ALU.mult),
             r=bMODCOL[4:8] + [bGCOLS[0]], w=[bGCOLS[2]])
        expand_bc(lambda jj: GCOLS[:, 2, jj:jj + 1], [bGCOLS[2]], BCB, bBCB, [2, 3])

        for c in range(8):
            S.dma("pool", lambda h, c=c: h.dma_start(out=WIN[:, c, 0:640], in_=win_v[:, c, 0:640]), bWINL,
                  w=[bWINL] if c == 0 else [])
        for c in range(8):
            S.dma("pool", lambda h, c=c: h.dma_start(out=WIN[:, c, 640:WINC], in_=win_v[:, c, 640:WINC]), bWIN,
                  w=[bWIN] if c == 0 else [])
        S.dma("pool", lambda h: h.dma_start(out=WKVK, in_=wkvk_d.rearrange("(c p) n -> p c n", p=128)), bWKVK, w=[bWKVK])
        S.dma("pool", lambda h: h.dma_start(out=WKVV, in_=wkvv_d.rearrange("(c p) n -> p c n", p=128)), bWKVV, w=[bWKVV])
        S.dma("pool", lambda h: h.dma_start(out=WQB, in_=wqb_d.rearrange("(c p) n -> p c n", p=128)), bWQB, w=[bWQB])
        adaln_dma(8)
        adaln_dma(9)
        S.dma("sp", lambda h: h.dma_start(out=GQA, in_=gqa_d), bGQA, w=[bGQA])
        S.dma("sp", lambda h: h.dma_start(out=GKVA, in_=gkva_d), bGKVA, w=[bGKVA])
        def weight_prep_a():
            S.op("dve", lambda h: h.tensor_scalar(out=WIN[:, :, 1440:1456], in0=WIN[:, :, 1440:1456], scalar1=-1.0,
                                                  scalar2=None, op0=ALU.mult), r=[bWIN], w=[bWIN])
            for c in range(2):
                S.op("dve", lambda h, c=c: h.tensor_scalar(out=WKVK[:, c, :], in0=WKVK[:, c, :], scalar1=GKVA[:, c:c + 1],
                                                           scalar2=None, op0=ALU.mult), r=[bWKVK, bGKVA], w=[bWKVK])
                S.op("dve", lambda h, c=c: h.tensor_scalar(out=WKVV[:, c, :], in0=WKVV[:, c, :], scalar1=GKVA[:, c:c + 1],
                                                           scalar2=None, op0=ALU.mult), r=[bWKVV, bGKVA], w=[bWKVV])

        def weight_prep_b():
            for c in range(3):
                S.op("dve", lambda h, c=c: h.tensor_scalar(out=WQB[:, c, :], in0=WQB[:, c, :], scalar1=GQA[:, c:c + 1],
                                                           scalar2=None, op0=ALU.mult), r=[bWQB, bGQA], w=[bWQB])
                wq4 = WQB[:, c, :].rearrange("p (h e) -> p h e", h=8, e=128)
                S.op("dve", lambda h, wq4=wq4: h.tensor_scalar(out=wq4[:, :, 96:112], in0=wq4[:, :, 96:112], scalar1=-1.0,
                                                               scalar2=None, op0=ALU.mult), r=[bWQB], w=[bWQB])

        def make_norm_bufs(ph):
            a_T1, T1 = region("T1" + ph, [128, D], F32, ph)
            HB, bHB = [], []
            for i in range(2):
                a, v = region(f"HB{ph}{i}", [128, D], BF16, ph)
                HB.append(v)
                bHB.append(Buf(f"HB{ph}{i}", a))
            return T1, Buf("T1" + ph, a_T1), HB, bHB

        stat_ctr = [0]

        def stat_cols(n):
            c = stat_ctr[0]
            stat_ctr[0] += n
            assert stat_ctr[0] <= 192
            return c

        def rms_rstd(src, src_bufs, nfeat, tag):
            c = stat_cols(3)
            bs = Buf(f"st_{tag}", a_stat)
            S.op("act", lambda h: h.activation(out=JUNK[:, 0:nfeat], in_=src, func=AF.Square, accum_out=STAT[:, c:c + 1]),
                 r=src_bufs, w=[bs])
            S.op("act", lambda h: h.activation(out=STAT[:, c + 1:c + 2], in_=STAT[:, c:c + 1], func=AF.Ln,
                                               scale=1.0 / nfeat, bias=EPSC), r=[bs, bEPS], w=[bs])
            S.op("act", lambda h: h.activation(out=STAT[:, c + 2:c + 3], in_=STAT[:, c + 1:c + 2], func=AF.Exp, scale=-0.5),
                 r=[bs], w=[bs])
            return STAT[:, c + 2:c + 3], bs


        def norm_pre(src, src_bufs, T1, bT1, HB, bHB, hs, GM, bGM, SH, bSH, tag, add_eng="dve", extra_r=(), stats=None):
            rstd, bs = stats if stats is not None else rms_rstd(src, src_bufs, D, tag)
            S.op("dve", lambda h: h.scalar_tensor_tensor(out=T1, in0=src, scalar=rstd, in1=GM, op0=ALU.mult, op1=ALU.mult),
                 r=src_bufs + [bs] + bGM, w=[bT1])
            S.op(add_eng, lambda h: h.tensor_tensor(out=HB[hs], in0=T1, in1=SH, op=ALU.add),
                 r=[bT1] + bSH + list(extra_r), w=[bHB[hs]])

        def norm_tr(HB, bHB, hs, tp_bank, dstT, dst_bufs, evac="dve"):
            tpv = bk_bf(tp_bank)
            for k in range(8):
                S.op("pe", lambda h, k=k: h.transpose(tpv[:, k * 128:(k + 1) * 128], HB[hs][:, k * 128:(k + 1) * 128], IDENTB),
                     r=[bHB[hs], b_identb], w=[bankb[tp_bank]])
            if evac == "act":
                S.op("act", lambda h: h.activation(out=dstT, in_=tpv.rearrange("p (a b) -> p a b", a=8, b=128), func=AF.Copy),
                     r=[bankb[tp_bank]], w=dst_bufs)
            else:
                S.op("dve", lambda h: h.tensor_copy(out=dstT, in_=tpv.rearrange("p (a b) -> p a b", a=8, b=128)),
                     r=[bankb[tp_bank]], w=dst_bufs)

        XT, bXT = [], []
        for i in range(2):
            a, v = region(f"XT{i}", [128, D], F32, "A")
            XT.append(v)
            bXT.append(Buf(f"XT{i}", a))
        HBA, bHBA = [], []
        for i in range(2):
            a, v = region(f"HBA{i}", [128, D], BF16, "A")
            HBA.append(v)
            bHBA.append(Buf(f"HBA{i}", a))
        H1T, bH1T = [], []
        for i in range(2):
            a, v = region(f"H1T{i}", [128, 8, 512], BF16, "A")
            H1T.append(v)
            bH1T.append([Buf(f"H1T{i}[{j}]", a) for j in range(4)])
        a_LATF, LATF = region("LATF", [128, 5, 512], F32, "A")
        a_SQN, SQN = region("SQN", [128, 5, 512], BF16, "A")
        a_RSTD, RSTD = region("RSTD", [128, 2, 512], F32, "A")
        bLATF = [Buf(f"LATF[{m}]", a_LATF) for m in range(5)]
        bSQN = [Buf(f"SQN[{m}]", a_SQN) for m in range(5)]
        bRSTD = [Buf(f"RSTD[{m}]", a_RSTD) for m in range(2)]
        CSA, bCSA = [], []
        for i in range(1):
            a, v = region(f"CSA{i}", [32, 2, 512], F32, "A", p0=64)
            CSA.append(v)
            bCSA.append(Buf(f"CSA{i}", a))
        a_TQA, TQA = region("TQA", [32, 2, 512], F32, "A", p0=64)
        bTQA = [Buf(f"TQA[{i}]", a_TQA) for i in range(2)]
        a_KPE, KPE = region("KPE", [32, 512], BF16, "A", p0=64)
        bKPEt = Buf("KPEt", a_KPE)

        def mm_group(bank, rows, ncols, lhs_list, rhs_list, rbufs):
            n = len(lhs_list)
            for i in range(n):
                S.op("pe", lambda h, i=i: h.matmul(bk(bank, rows, (0, ncols)), lhsT=lhs_list[i], rhs=rhs_list[i],
                                                   start=(i == 0), stop=(i == n - 1)),
                     r=rbufs, w=[bankb[bank]])

        brr = [0]

        def next_bank(lst):
            b = lst[brr[0] % len(lst)]
            brr[0] += 1
            return b

        tp_ctr = [0]

        a_H1T1 = bH1T[1][0].alloc
        XTALT = [AR.ap(a_H1T1, [128, D], F32, byte_off=i * D * 4) for i in range(2)]
        a_XTALT = Alloc("XTALT", a_H1T1.lo, a_H1T1.hi, "A")
        a_XTALT.overl = [a_H1T1] + list(a_H1T1.overl)
        for o_ in a_XTALT.overl:
            o_.overl.append(a_XTALT)
        bXTALT = [Buf(f"XTALT{i}", a_XTALT) for i in range(2)]

        pre_stats = {}
        x_issued = set()

        def a_xslot(t):
            return (XTALT[t - 2], bXTALT[t - 2]) if t in (2, 3) else (XT[t % 2], bXT[t % 2])

        def x_dma(t):
            if t in x_issued:
                return
            x_issued.add(t)
            xa, bxa = a_xslot(t)
            S.dma("sp", lambda h: h.dma_start(out=xa, in_=x_d[t * 128:(t + 1) * 128, :]), bxa, w=[bxa])

        def a_stats_early(t):
            x_dma(t)
            xa, bxa = a_xslot(t)
            pre_stats[t] = rms_rstd(xa, [bxa], D, f"n1_{t}")

        def a_pre(t):
            x_dma(t)
            xa, bxa = a_xslot(t)
            norm_pre(xa, [bxa], xa, bxa, HBA, bHBA, t % 2, BCB, bBCB, BCA, bBCA, f"n1_{t}", stats=pre_stats.get(t))

        def a_tr(t):
            G, j = t // 4, t % 4
            gs = G % 2
            tpb = tp_ctr[0] % 2
            tp_ctr[0] += 1
            norm_tr(HBA, bHBA, t % 2, tpb, H1T[gs][:, :, j * 128:(j + 1) * 128], [bH1T[gs][j]],
                    evac=("act" if t < 4 else "dve"))

        def a_part1(G):
            gs = G % 2
            gcols = slice(G * 512, (G + 1) * 512)
            hb = bH1T[gs]
            hrhs = [H1T[gs][:, k, :] for k in range(8)]
            for m in range(5):
                b = 2 + (m % 2)
                mm_group(b, None, 512, [WIN[:, k, m * 128:(m + 1) * 128] for k in range(8)], hrhs, hb + [bWINL])
                S.op("act", lambda h, b=b, m=m: h.activation(out=LATF[:, m, :], in_=bk(b), func=AF.Copy),
                     r=[bankb[b]], w=[bLATF[m]])
                S.op("act", lambda h, b=b, m=m: h.activation(out=SQN[:, m, :], in_=bk(b), func=AF.Square),
                     r=[bankb[b]], w=[bSQN[m]])

        def a_stats(G):
            gcols = slice(G * 512, (G + 1) * 512)
            mm_group(4, None, 512, [ONESB] * 3, [SQN[:, m, :] for m in range(3)], bSQN[0:3] + [b_onesb])
            mm_group(5, None, 512, [ONESB] * 2, [SQN[:, m, :] for m in range(3, 5)], bSQN[3:5] + [b_onesb])
            for (i, b, nf) in ((1, 5, 256), (0, 4, 384)):
                S.op("act", lambda h, i=i, b=b, nf=nf: h.activation(out=RSTD[:, i, :], in_=bk(b), func=AF.Ln,
                                                                    scale=1.0 / nf, bias=EPSC),
                     r=[bankb[b], bEPS], w=[bRSTD[i]])
                S.op("act", lambda h, i=i: h.activation(out=RSTD[:, i, :], in_=RSTD[:, i, :], func=AF.Exp, scale=-0.5),
                     r=[bRSTD[i]], w=[bRSTD[i]])
            for m in range(3, 5):
                S.op("dve", lambda h, m=m: h.tensor_tensor(out=SQN[:, m, :], in0=LATF[:, m, :], in1=RSTD[:, 1, :],
                                                           op=ALU.mult),
                     r=[bLATF[m], bRSTD[1]], w=[bSQN[m]])
            for m in range(3):
                S.op("dve", lambda h, m=m: h.tensor_tensor(out=QLN[:, m, gcols], in0=LATF[:, m, :],
                                                           in1=RSTD[:, 0, :], op=ALU.mult),
                     r=[bLATF[m], bRSTD[0]], w=[bQLN[G]])

        def a_swq(G, c4):
            gs = G % 2
            hb = bH1T[gs]
            hrhs = [H1T[gs][:, k, :] for k in range(8)]
            b = 2 + (c4 % 2)
            mm_group(b, None, 512, [WIN[:, k, 672 + c4 * 128:672 + (c4 + 1) * 128] for k in range(8)], hrhs, hb + [bWIN])
            if c4 % 2 == 0:
                S.op("act", lambda h: h.activation(out=SWQ[:, G * 4:(G + 1) * 4, c4, :],
                                                   in_=bk(b).rearrange("p (a b) -> p a b", a=4, b=128), func=AF.Copy),
                     r=[bankb[b]], w=[bSWQ[G]])
            else:
                S.op("dve", lambda h: h.tensor_copy(out=SWQ[:, G * 4:(G + 1) * 4, c4, :],
                                                    in_=bk(b).rearrange("p (a b) -> p a b", a=4, b=128)),
                     r=[bankb[b]], w=[bSWQ[G]])

        def a_part2_tail(G):
            gs = G % 2
            gcols = slice(G * 512, (G + 1) * 512)
            hb = bH1T[gs]
            hrhs = [H1T[gs][:, k, :] for k in range(8)]
            S.dma("sp", lambda h: h.dma_start(out=CSA[0], in_=cs_d[:, :, G * 512:(G + 1) * 512]), bCSA[0], w=[bCSA[0]])
            mm_group(4, None, 512, [WIN[:, k, 1184:1312] for k in range(8)], hrhs, hb + [bWIN])
            S.op("dve", lambda h: h.tensor_copy(out=SWK[:, gcols], in_=bk(4)), r=[bankb[4]], w=[bSWK[G]])
            mm_group(6, (64, 96), 512, [WIN[:, k, 640:672] for k in range(8)], hrhs, hb + [bWIN])
            mm_group(7, (64, 96), 512, [WIN[:, k, 1440:1472] for k in range(8)], hrhs, hb + [bWIN])
            S.op("dve", lambda h: h.tensor_tensor(out=TQA[:, 0, :], in0=bk(6, (64, 96)), in1=CSA[0][:, 0, :], op=ALU.mult),
                 r=[bankb[6], bCSA[0]], w=[bTQA[0]])
            S.op("dve", lambda h: h.tensor_tensor(out=TQA[:, 1, :], in0=bk(7, (64, 96)), in1=CSA[0][:, 1, :], op=ALU.mult),
                 r=[bankb[7], bCSA[0]], w=[bTQA[1]])
            S.op("dve", lambda h: h.tensor_tensor(out=KPE, in0=TQA[:, 0, :], in1=TQA[:, 1, :], op=ALU.add),
                 r=bTQA, w=[bKPEt])
            for hh in range(8):
                S.op("dve", lambda h, hh=hh: h.tensor_copy(out=K_[64:96, hh, gcols], in_=KPE),
                     r=[bKPEt], w=[bK[hh][G]])
            for j in range(4):
                t = G * 4 + j
                b2 = 6 + (j % 2)
                mm_group(b2, None, 128, [H1T[gs][:, k, j * 128:(j + 1) * 128] for k in range(8)],
                         [WIN[:, k, 1312:1440] for k in range(8)], hb + [bWIN])
                S.op("act", lambda h, t=t, b2=b2: h.activation(out=swv_tile_out(t), in_=bk(b2, None, (0, 128)).rearrange("p (a b) -> p a b", a=2, b=64), func=AF.Copy),
                     r=[bankb[b2]], w=[bSWV[t]])

        def a_part3(G):
            gcols = slice(G * 512, (G + 1) * 512)
            kvln = [SQN[:, 3, :], SQN[:, 4, :]]
            bkvln = [bSQN[3], bSQN[4]]
            for hp in range(4):
                b = 2 + (hp % 2)
                mm_group(b, None, 512, [WKVK[:, c, hp * 128:(hp + 1) * 128] for c in range(2)], kvln, bkvln + [bWKVK])
                for half in range(2):
                    hh = 2 * hp + half
                    S.op("act", lambda h, hh=hh, b=b, half=half: h.activation(
                        out=K_[0:64, hh, gcols], in_=bk(b, (half * 64, half * 64 + 64)), func=AF.Copy),
                         r=[bankb[b]], w=[bK[hh][G]])
            for j in range(4):
                t = G * 4 + j
                b = 4 + (j % 2)
                mm_group(b, None, 512, [kvln[c][:, j * 128:(j + 1) * 128] for c in range(2)],
                         [WKVV[:, c, :] for c in range(2)], bkvln + [bWKVV])
                S.op("dve", lambda h, t=t, b=b: h.tensor_copy(out=v_tile_out(t), in_=bk(b).rearrange("p (a b c) -> p a b c", a=4, b=2, c=64)), r=[bankb[b]], w=[bV[t]])

        for t_ in range(4):
            a_stats_early(t_)
        a_pre(0)
        a_pre(1)
        a_tr(0)
        a_pre(2)
        a_tr(1)
        a_pre(3)
        a_tr(2)
        a_tr(3)
        weight_prep_a()
        for G in range(NG):
            nxt = G + 1 < NG
            t0 = (G + 1) * 4
            n2 = 8 + 4 * G
            a_part1(G)
            if nxt:
                a_pre(t0)
                a_pre(t0 + 1)
            a_stats(G)
            adaln_cols(n2, 6)
            a_swq(G, 0)
            if nxt:
                a_tr(t0)
                a_pre(t0 + 2)
            a_swq(G, 1)
            if nxt:
                a_tr(t0 + 1)
                a_pre(t0 + 3)
            adaln_cols(n2 + 1, 7)
            a_swq(G, 2)
            if nxt:
                a_tr(t0 + 2)
            a_swq(G, 3)
            if nxt:
                a_tr(t0 + 3)
            adaln_cols(n2 + 2, 5)
            a_part2_tail(G)
            adaln_cols(n2 + 3, 1)
            a_part3(G)
            if G == 0:
                weight_prep_b()
        S.op("dve", lambda h: h.scalar_tensor_tensor(out=GCOLS[:, 3, :], in0=MODCOL[:, 32:40], scalar=1.0, in1=GCOLS[:, 1, :],
                                                     op0=ALU.add, op1=ALU.mult),
             r=bMODCOL[16:20] + [bGCOLS[1]], w=[bGCOLS[3]])


        a_MIXT, MIXT = region("MIXT", [128, 8, T], BF16, "BC")
        bMIXT = [[Buf(f"MIXT[{c}][{g}]", a_MIXT) for g in range(NG)] for c in range(8)]
        QG, bQG = [], []
        for i in range(2):
            a, v = region(f"QG{i}", [96, 8, 512], BF16, "B")
            QG.append(v)
            bQG.append([Buf(f"QG{i}[{h}]", a) for h in range(8)])
        PT, bPT = [], []
        for i in range(3):
            a, v = region(f"PT{i}", [128, 2, 512], BF16, "B")
            PT.append(v)
            bPT.append(Buf(f"PT{i}", a))
        RDEN, bRDEN = [], []
        for i in range(2):
            a, v = region(f"RDEN{i}", [128, 512], F32, "B")
            RDEN.append(v)
            bRDEN.append(Buf(f"RDEN{i}", a))
        a_MM, MMASK = region("MMASK", [128, 256], BF16, "B")
        a_SM, SMASK = region("SMASK", [128, 2, 2, 512], BF16, "B")
        bMM, bSM = Buf("MMASK", a_MM), Buf("SMASK", a_SM)
        CSB, bCSB = [], []
        for i in range(1):
            a, v = region(f"CSB{i}", [32, 2, 512], F32, "B", p0=64)
            CSB.append(v)
            bCSB.append(Buf(f"CSB{i}", a))
        a_TQB, TQB = region("TQB", [32, 2, 2, 512], F32, "B", p0=64)
        bTQB = [[Buf(f"TQB[{s}][{i}]", a_TQB) for i in range(2)] for s in range(2)]
        a_SK8, SK8 = region("SK8", [128, 8], F32, "B")
        a_ESK, ESK = region("ESK", [128, 2, 512], F32, "B")
        bSK8, bESK = Buf("SK8", a_SK8), Buf("ESK", a_ESK)

        S.dma("pool", lambda h: h.dma_start(out=MMASK, in_=mmask_d), bMM, w=[bMM])
        S.dma("pool", lambda h: h.dma_start(out=SMASK.rearrange("p a b c -> p (a b c)"), in_=smask_d), bSM, w=[bSM])
        S.dma("sp", lambda h: h.dma_start(out=SK8, in_=bcast_row(sinks_d, 8)), bSK8, w=[bSK8])
        S.op("act", lambda h: h.activation(out=SK8, in_=SK8, func=AF.Exp), r=[bSK8], w=[bSK8])
        for hh in range(8):
            kv, g = hh // 4, hh % 4
            S.op("dve", lambda h, hh=hh, kv=kv, g=g: h.tensor_scalar(out=ESK[:, kv, g * 128:(g + 1) * 128], in0=ONESF,
                                                                  scalar1=SK8[:, hh:hh + 1], scalar2=None, op0=ALU.mult),
                 r=[bSK8, b_onesf], w=[bESK])

        MLA_SCALE = float(96 ** -0.5)
        SWA_SCALE = 0.125
        SPAIRS = [(0, 1), (2, 3), (4, 5)]
        OBANKS = [6, 7]
        NPT = 3
        DEPTH = 2
        sp_ctr = [0]
        pt_ctr = [0]
        ob_ctr = [0]
        rd_ctr = [0]

        def sc_view(pair, idx, c0, c1, rows=(0, 128)):
            return banks[pair[idx]][rows[0]:rows[1], c0:c1]

        class Unit:
            pass

        def load_cs(G):
            S.dma("sp", lambda h: h.dma_start(out=CSB[0], in_=cs_d[:, :, G * 512:(G + 1) * 512]), bCSB[0], w=[bCSB[0]])

        def qp_unit(G, hh):
            u = Unit()
            u.kind, u.G, u.h = "qp", G, hh
            u.ob = OBANKS[ob_ctr[0] % 2]
            return u

        def emit_qp_mm(u):
            G, hh = u.G, u.h
            gcols = slice(G * 512, (G + 1) * 512)
            bm = u.ob
            mm_group(bm, None, 512, [WQB[:, c, hh * 128:(hh + 1) * 128] for c in range(3)],
                     [QLN[:, c, gcols] for c in range(3)], [bQLN[G], bWQB])

        def emit_qp_evac(u):
            G, hh = u.G, u.h
            qs = G % 2
            ts = hh % 2
            bm = u.ob
            S.op("act", lambda h: h.activation(out=QG[qs][0:64, hh, :], in_=bk(bm, (0, 64)), func=AF.Copy),
                 r=[bankb[bm]], w=[bQG[qs][hh]])
            S.op("dve", lambda h: h.tensor_tensor(out=TQB[:, ts, 0, :], in0=bk(bm, (64, 96)), in1=CSB[0][:, 0, :],
                                                  op=ALU.mult), r=[bankb[bm], bCSB[0]], w=[bTQB[ts][0]])
            S.op("dve", lambda h: h.tensor_tensor(out=TQB[:, ts, 1, :], in0=bk(bm, (96, 128)), in1=CSB[0][:, 1, :],
                                                  op=ALU.mult), r=[bankb[bm], bCSB[0]], w=[bTQB[ts][1]])
            S.op("pool", lambda h: h.tensor_tensor(out=QG[qs][64:96, hh, :], in0=TQB[:, ts, 0, :],
                                                   in1=TQB[:, ts, 1, :], op=ALU.add),
                 r=bTQB[ts], w=[bQG[qs][hh]])

        def mla_units(G, hh):
            qs = G % 2
            units = []
            for j0 in range(0, 4 * G, 2):
                u = Unit()
                u.tiles = [(j0, 0, 512, None, 0), (j0 + 1, 0, 512, None, 0)]
                u.e0 = 0
                units.append(u)
            tri = MMASK[:, 128:256]
            negtri = MMASK[:, 0:256]
            u = Unit()
            u.tiles = [(4 * G, 0, 512, (0, 128, tri), 0), (4 * G + 1, 0, 512, (0, 256, negtri), 128)]
            u.e0 = 0
            units.append(u)
            u = Unit()
            u.tiles = [(4 * G + 2, 256, 512, (256, 128, tri), 256), (4 * G + 3, 256, 512, (256, 256, negtri), 384)]
            u.e0 = 256
            units.append(u)
            ob = OBANKS[ob_ctr[0] % 2]
            ob_ctr[0] += 1
            for u in units:
                u.G, u.h, u.qs, u.ob = G, hh, qs, ob
                u.kind = "mla"
            units[0].first = True
            units[-1].last = True
            return units

        def emit_scores(u):
            u.pair = SPAIRS[sp_ctr[0] % len(SPAIRS)]
            sp_ctr[0] += 1
            if u.kind == "qp":
                u.ob = u.pair[0]
                emit_qp_mm(u)
                return
            if u.kind == "mla":
                for idx, (j, c0, c1, msk, p0) in enumerate(u.tiles):
                    b = u.pair[idx]
                    S.op("pe", lambda h, j=j, c0=c0, c1=c1, b=b, msk=msk: h.matmul(
                        bk(b, None, (c0, c1)), lhsT=K_[0:96, u.h, j * 128:(j + 1) * 128], rhs=QG[u.qs][0:96, u.h, c0:c1],
                        start=True, stop=(msk is None)),
                         r=[bK[u.h][j // 4], bQG[u.qs][u.h]], w=[bankb[b]])
                    if msk is not None:
                        S.op("pe", lambda h, msk=msk, b=b: h.matmul(bk(b, None, (msk[0], msk[0] + msk[1])), lhsT=IDENTB,
                                                                    rhs=msk[2], start=False, stop=True),
                             r=[b_identb, bMM], w=[bankb[b]])
            else:
                n, kv = u.n, u.kv
                rows = (kv * 64, kv * 64 + 64)
                qv = SWQ[rows[0]:rows[1], n, :, :].rearrange("p a b -> p (a b)")
                for idx in ((0, 1) if n > 0 else (1,)):
                    tkt = n - 1 + idx
                    b = u.pair[idx]
                    S.op("pe", lambda h, tkt=tkt, b=b: h.matmul(bk(b), lhsT=SWK[rows[0]:rows[1], tkt * 128:(tkt + 1) * 128],
                                                                rhs=qv, start=True, stop=False),
                         r=[bSWK[tkt // 4], bSWQ[n // 4]], w=[bankb[b]])
                    S.op("pe", lambda h, idx=idx, b=b: h.matmul(bk(b), lhsT=IDENTB, rhs=SMASK[:, idx, kv, :],
                                                                start=False, stop=True),
                         r=[b_identb, bSM], w=[bankb[b]])

        def emit_exp(u):
            if u.kind == "qp":
                emit_qp_evac(u)
                return
            u.pt = pt_ctr[0] % NPT
            pt_ctr[0] += 1
            p = u.pt
            if u.kind == "mla":
                e0 = u.e0
                pairv = PS[:, u.pair[0]:u.pair[0] + 2, e0:512]
                S.op("act", lambda h: h.activation(out=PT[p][:, :, e0:512], in_=pairv, func=AF.Exp, scale=MLA_SCALE),
                     r=[bankb[u.pair[0]], bankb[u.pair[1]]], w=[bPT[p]])
            else:
                if u.n > 0:
                    pairv = bk_pair(u.pair[0])
                    S.op("act", lambda h: h.activation(out=PT[p], in_=pairv, func=AF.Exp, scale=SWA_SCALE),
                         r=[bankb[u.pair[0]], bankb[u.pair[1]]], w=[bPT[p]])
                else:
                    b = u.pair[1]
                    S.op("act", lambda h: h.activation(out=PT[p][:, 1, :], in_=bk(b), func=AF.Exp, scale=SWA_SCALE),
                         r=[bankb[b]], w=[bPT[p]])

        cur_ob = [None]
        deferred = []

        NFILL = 0

        def emit_pv(u):
            if u.kind == "qp":
                return
            p = u.pt
            for _ in range(NFILL):
                S.op("pe", lambda h: h.ldweights(IDENTB), r=[b_identb])
            if u.kind == "mla":
                ob = u.ob
                open_bank[0] = None if getattr(u, "last", False) else ob
                for idx, (j, c0, c1, msk, p0) in enumerate(u.tiles):
                    first = getattr(u, "first", False) and idx == 0
                    last = getattr(u, "last", False) and idx == len(u.tiles) - 1
                    S.op("pe", lambda h, j=j, p0=p0, idx=idx, first=first, last=last: h.matmul(
                        bk(ob, None, (p0, 512)), lhsT=vaug(j, u.h), rhs=PT[p][:, idx, p0:512], start=first, stop=last),
                         r=[bV[j], bVones, bPT[p]], w=[bankb[ob]])
                if getattr(u, "last", False):
                    hh, G = u.h, u.G
                    orow = (0, 64) if hh % 2 == 0 else (64, 128)
                    drow = (64, 128) if hh % 2 == 0 else (0, 64)
                    rs = rd_ctr[0] % 2
                    rd_ctr[0] += 1
                    if G >= 1:
                        S.op("dve", lambda h: h.reciprocal(out=RDEN[rs][orow[0]:orow[1], :], in_=bk(ob, drow)),
                             r=[bankb[ob]], w=[bRDEN[rs]])
                    else:
                        S.op("act", lambda h: h.activation(out=RDEN[rs][orow[0]:orow[1], :], in_=bk(ob, drow), func=AF.Ln),
                             r=[bankb[ob]], w=[bRDEN[rs]])
                        S.op("act", lambda h: h.activation(out=RDEN[rs][orow[0]:orow[1], :],
                                                           in_=RDEN[rs][orow[0]:orow[1], :], func=AF.Exp, scale=-1.0),
                             r=[bRDEN[rs]], w=[bRDEN[rs]])
                    S.op("dve", lambda h: h.tensor_tensor(out=MIXT[orow[0]:orow[1], hh // 2, G * 512:(G + 1) * 512],
                                                          in0=bk(ob, orow), in1=RDEN[rs][orow[0]:orow[1], :], op=ALU.mult),
                         r=[bankb[ob], bRDEN[rs]], w=[bMIXT[hh // 2][G]])
            else:
                n, kv = u.n, u.kv
                ob = OBANKS[ob_ctr[0] % 2]
                ob_ctr[0] += 1
                u.ob = ob
                idxs = (0, 1) if n > 0 else (1,)
                for ii, idx in enumerate(idxs):
                    tkt = n - 1 + idx
                    S.op("pe", lambda h, tkt=tkt, idx=idx, ii=ii: h.matmul(bk(ob), lhsT=swvaug(tkt, kv), rhs=PT[p][:, idx, :],
                                                                         start=(ii == 0), stop=(ii == len(idxs) - 1)),
                         r=[bSWV[tkt], bSWVones, bPT[p]], w=[bankb[ob]])
                orow = (0, 64) if kv == 0 else (64, 128)
                drow = (64, 128) if kv == 0 else (0, 64)
                rs = rd_ctr[0] % 2
                rd_ctr[0] += 1
                S.op("dve", lambda h: h.tensor_tensor(out=RDEN[rs][orow[0]:orow[1], :], in0=bk(ob, drow),
                                                      in1=ESK[orow[0]:orow[1], kv, :], op=ALU.add),
                     r=[bankb[ob], bESK], w=[bRDEN[rs]])
                S.op("act", lambda h: h.activation(out=RDEN[rs][orow[0]:orow[1], :], in_=RDEN[rs][orow[0]:orow[1], :],
                                                   func=AF.Ln), r=[bRDEN[rs]], w=[bRDEN[rs]])
                S.op("act", lambda h: h.activation(out=RDEN[rs][orow[0]:orow[1], :], in_=RDEN[rs][orow[0]:orow[1], :],
                                                   func=AF.Exp, scale=-1.0), r=[bRDEN[rs]], w=[bRDEN[rs]])
                G = n // 4

                def final_mult():
                    S.op("dve", lambda h: h.tensor_tensor(
                        out=MIXT[orow[0]:orow[1], 4:8, n * 128:(n + 1) * 128],
                        in0=bk(ob, orow).rearrange("p (g q) -> p g q", g=4, q=128),
                        in1=RDEN[rs][orow[0]:orow[1], :].rearrange("p (g q) -> p g q", g=4, q=128), op=ALU.mult),
                         r=[bankb[ob], bRDEN[rs]], w=[bMIXT[4 + g][G] for g in range(4)])
                deferred.append([1, final_mult])

        pending = []

        def run_deferred(force=False):
            for e in list(deferred):
                if force or e[0] <= 0:
                    deferred.remove(e)
                    e[1]()
                else:
                    e[0] -= 1

        qp_queue = []
        qp_free = []

        open_bank = [None]

        def issue_qp(bank):
            if bank == open_bank[0]:
                bank = OBANKS[1 - OBANKS.index(bank)]
            if qp_queue:
                G2, h2 = qp_queue.pop(0)
                u2 = qp_unit(G2, h2)
                u2.ob = bank
                emit_qp_mm(u2)
                emit_qp_evac(u2)

        def after_pv(v):
            pass

        def push_unit(u):
            emit_scores(u)
            emit_exp(u)
            run_deferred(force=True)
            pending.append(u)
            if len(pending) > DEPTH:
                v = pending.pop(0)
                emit_pv(v)
                after_pv(v)
            for e in list(qp_free):
                if e[0] <= 0:
                    qp_free.remove(e)
                    issue_qp(e[1])
                else:
                    e[0] -= 1

        def flush_units():
            while pending:
                run_deferred(force=True)
                v = pending.pop(0)
                emit_pv(v)
                after_pv(v)
            for e in list(qp_free):
                qp_free.remove(e)
                issue_qp(e[1])
            run_deferred(force=True)

        def swa_group(G):
            for n in range(4 * G, 4 * G + 4):
                for kv in range(2):
                    u = Unit()
                    u.kind, u.n, u.kv = "swa", n, kv
                    push_unit(u)
                    if qp_queue:
                        push_unit(qp_unit(*qp_queue.pop(0)))

        a_WO, WO = region("WO", [128, 8, D], BF16, "BC")
        bWO = Buf("WO", a_WO)
        a_G1, G1B = region("G1B", [128, D], F32, "B")
        bG1 = [Buf(f"G1B{i}", a_G1) for i in range(2)]

        def fold_wo_expand():
            expand_bc(lambda jj: MODCOL[:, 16 + jj:17 + jj], bMODCOL[8:12], G1B, bG1, [6, 7])

        def fold_wo_chunk(c):
            S.op("dve", lambda h: h.tensor_tensor(out=WO[:, c, :], in0=WO[:, c, :], in1=G1B, op=ALU.mult),
                 r=[bWO] + bG1, w=[bWO])

        load_cs(0)
        S.dma("pool", lambda h: h.dma_start(out=WO, in_=wo_d.rearrange("(c p) n -> p c n", p=128)), bWO, w=[bWO])
        for hh in range(2):
            push_unit(qp_unit(0, hh))
        for G in range(NG):
            if G == NG - 1:
                fold_wo_expand()
            for hh in range(8):
                for iu, u in enumerate(mla_units(G, hh)):
                    push_unit(u)
                    if G == 0 and iu == 0 and hh + 2 < 8:
                        push_unit(qp_unit(0, hh + 2))
                if G == NG - 1:
                    fold_wo_chunk(hh)
            if G + 1 < NG:
                assert not qp_queue
                load_cs(G + 1)
                qp_queue.extend((G + 1, hh) for hh in range(8))
            swa_group(G)
        flush_units()

        a_X1, X1 = region("X1", [128, NT, D], F32, "CDEF")
        bX1 = [Buf(f"X1[{t}]", a_X1) for t in range(NT)]
        a_BC2A, BC2A = region("BC2A", [128, D], F32, "C")
        a_BC2B, BC2B = region("BC2B", [128, D], F32, "C")
        bBC2A = [Buf(f"BC2A{i}", a_BC2A) for i in range(2)]
        bBC2B = [Buf(f"BC2B{i}", a_BC2B) for i in range(2)]
        a_H2T, H2T = region("H2T", [128, 8, T], BF16, "CE")
        bH2T = [Buf(f"H2T[{t}]", a_H2T) for t in range(NT)]
        T1D, bT1D, HBD, bHBD = make_norm_bufs("C")
        WUP, bWUP, WDN, bWDN = [], [], [], []
        for i in range(2):
            a, v = region(f"WUP{i}", [128, 8, 1024], BF16, "CE" if i == 0 else "E")
            WUP.append(v)
            bWUP.append(Buf(f"WUP{i}", a))
            a, v = region(f"WDN{i}", [128, 8, 1024], BF16, "CE" if i == 0 else "E")
            WDN.append(v)
            bWDN.append(Buf(f"WDN{i}", a))
        wup_v = wup_d.rearrange("(c p) n -> p c n", p=128)
        wdn_v = wdn_d.rearrange("(q c p) n -> q p c n", q=4, p=128)

        def load_pass(q):
            s_ = q % 2
            for c in range(8):
                S.dma("pool", lambda h, c=c: h.dma_start(out=WUP[s_][:, c, :], in_=wup_v[:, c, q * 1024:(q + 1) * 1024]),
                      bWUP[s_], w=[bWUP[s_]] if c == 0 else [])
            S.dma("pool", lambda h: h.dma_start(out=WDN[s_], in_=wdn_v[q]), bWDN[s_], w=[bWDN[s_]])

        for t in range(NT):
            S.dma("sp", lambda h, t=t: h.dma_start(out=X1[:, t, :], in_=x_d[t * 128:(t + 1) * 128, :]), bX1[t], w=[bX1[t]])
        load_pass(0)
        a_G2, G2B = region("G2B", [128, D], F32, "CE")
        bG2 = [Buf(f"G2B{i}", a_G2) for i in range(2)]
        cb = [0]

        def c_wo(t):
            for half in range(2):
                b = cb[0] % 4
                cb[0] += 1
                mm_group(b, None, 512, [MIXT[:, c, t * 128:(t + 1) * 128] for c in range(8)],
                         [WO[:, c, half * 512:(half + 1) * 512] for c in range(8)],
                         [bMIXT[c][t // 4] for c in range(8)] + [bWO])
                S.op("dve", lambda h, half=half, b=b: h.tensor_tensor(out=X1[:, t, half * 512:(half + 1) * 512], in0=bk(b),
                                                                      in1=X1[:, t, half * 512:(half + 1) * 512], op=ALU.add),
                     r=[bankb[b], bX1[t]], w=[bX1[t]])

        def c_pre(t):
            norm_pre(X1[:, t, :], [bX1[t]], T1D, bT1D, HBD, bHBD, t % 2, BC2B, bBC2B, BC2A, bBC2A, f"n2_{t}")

        def c_tr(t):
            norm_tr(HBD, bHBD, t % 2, 6 + (t % 2), H2T[:, :, t * 128:(t + 1) * 128], [bH2T[t]], evac="act")

        c_wo(0)
        c_wo(1)
        expand_bc(lambda jj: GCOLS[:, 3, jj:jj + 1], [bGCOLS[3]], BC2B, bBC2B, [4, 5])
        expand_bc(lambda jj: MODCOL[:, 24 + jj:25 + jj], bMODCOL[12:16], BC2A, bBC2A, [4, 5])
        c_pre(0)
        for t in range(NT):
            if t + 2 < NT:
                c_wo(t + 2)
            if t + 1 < NT:
                c_pre(t + 1)
            c_tr(t)
            if t == 8:
                expand_bc(lambda jj: MODCOL[:, 40 + jj:41 + jj], bMODCOL[20:24], G2B, bG2, [4, 5])

        UT, bUT = [], []
        for i in range(2):
            a, v = region(f"UT{i}", [128, 8, 512], BF16, "E")
            UT.append(v)
            bUT.append([Buf(f"UT{i}[{f}]", a) for f in range(8)])
        RT, bRT, GT, bGT = [], [], [], []
        for i in range(2):
            a, v = region(f"RT{i}", [128, 512], F32, "E")
            RT.append(v)
            bRT.append(Buf(f"RT{i}", a))
            a, v = region(f"GT{i}", [128, 512], F32, "E")
            GT.append(v)
            bGT.append(Buf(f"GT{i}", a))
        load_pass(1)

        ub = [0]
        rt_ctr = [0]

        def mlp_up(q, G):
            s_ = q % 2
            us = (q * 4 + G) % 2
            for fc in range(8):
                b = ub[0] % 3
                ub[0] += 1
                mm_group(b, None, 512, [WUP[s_][:, k, fc * 128:(fc + 1) * 128] for k in range(8)],
                         [H2T[:, k, G * 512:(G + 1) * 512] for k in range(8)], [bH2T[G * 4 + j] for j in range(4)] + [bWUP[s_]])
                rs = rt_ctr[0] % 2
                rt_ctr[0] += 1
                S.op("act", lambda h, b=b, rs=rs: h.activation(out=RT[rs], in_=bk(b), func=AF.Relu), r=[bankb[b]], w=[bRT[rs]])
                S.op("act", lambda h, rs=rs, fc=fc: h.activation(out=UT[us][:, fc, :], in_=RT[rs], func=AF.Square),
                     r=[bRT[rs]], w=[bUT[us][fc]])

        db = [0]

        def mlp_down(q, G):
            s_ = q % 2
            us = (q * 4 + G) % 2
            for j in range(4):
                t = G * 4 + j
                for half in range(2):
                    b = 3 + db[0] % 3
                    gs_ = db[0] % 2
                    db[0] += 1
                    mm_group(b, None, 512, [UT[us][:, fc, j * 128:(j + 1) * 128] for fc in range(8)],
                             [WDN[s_][:, fc, half * 512:(half + 1) * 512] for fc in range(8)], bUT[us] + [bWDN[s_]])
                    S.op("dve", lambda h, half=half, b=b, gs_=gs_: h.tensor_tensor(
                        out=GT[gs_], in0=bk(b), in1=G2B[:, half * 512:(half + 1) * 512], op=ALU.mult),
                         r=[bankb[b], bG2[half]], w=[bGT[gs_]])
                    S.op("dve", lambda h, t=t, half=half, gs_=gs_: h.tensor_tensor(
                        out=X1[:, t, half * 512:(half + 1) * 512], in0=GT[gs_], in1=X1[:, t, half * 512:(half + 1) * 512],
                        op=ALU.add), r=[bGT[gs_], bX1[t]], w=[bX1[t]])

        a_FG, FGB = region("FGB", [128, D], F32, "EF")
        bFG = Buf("FGB", a_FG)
        S.dma("sp", lambda h: h.dma_start(out=FGB, in_=bcast_row(fg_d, D)), bFG, w=[bFG])

        def final_tile(t):
            rstd, bs = rms_rstd(X1[:, t, :], [bX1[t]], D, f"nf_{t}")
            S.op("dve", lambda h: h.scalar_tensor_tensor(out=X1[:, t, :], in0=X1[:, t, :], scalar=rstd,
                                                         in1=FGB, op0=ALU.mult, op1=ALU.mult),
                 r=[bX1[t], bs, bFG], w=[bX1[t]])
            od = S.dma("sp", lambda h: h.dma_start(out=out_d[t * 128:(t + 1) * 128, :], in_=X1[:, t, :]), bX1[t],
                       r=[bX1[t]])
            S.final.append(od)

        seq = [(q, G) for q in range(4) for G in range(NG)]
        for i, (q, G) in enumerate(seq):
            if i == 0:
                mlp_up(q, G)
            if i + 1 < len(seq):
                mlp_up(*seq[i + 1])
            mlp_down(q, G)
            if G == NG - 1 and q + 2 < 4:
                load_pass(q + 2)
            if q == 3:
                for j in range(4):
                    final_tile(G * 4 + j)


        block = es.enter_context(nc.Block())
        S.emit(nc, es, block, None)
    return nc


def _host_consts():
    identf = np.eye(128, dtype=np.float32)
    onesf = np.ones((128, 128), dtype=np.float32)
    i = np.arange(32)
    freqs = 10000.0 ** (-(i % 16).astype(np.float64) / 16.0)
    ang = freqs[:, None] * np.arange(T, dtype=np.float64)[None, :]
    cs = np.stack([np.cos(ang), np.sin(ang)], axis=1).astype(np.float32)
    tk = np.arange(128)[:, None]
    tq = np.arange(128)[None, :]
    mmask = np.concatenate([np.full((128, 128), NEG), np.where(tk <= tq, 0.0, NEG)], axis=1).astype(np.float32)
    smask = np.zeros((128, 2, 2, 4, 128), dtype=np.float32)
    for kv in range(2):
        for g in range(4):
            h = kv * 4 + g
            slope = 2.0 ** (-(h + 1))
            d_own = (tq - tk).astype(np.float64)
            d_prev = (128 + tq - tk).astype(np.float64)
            smask[:, 1, kv, g, :] = np.where(tk <= tq, -slope * d_own * 8.0, NEG)
            smask[:, 0, kv, g, :] = np.where(tk > tq, -slope * d_prev * 8.0, NEG)
    return identf, onesf, cs, mmask, smask.reshape(128, 2048)


_NC_CACHE = {}


def kernel(x, c, w_ada, b_ada, norm_mix_g, w_in, g_qa, w_qb, g_kva, w_kvb, sinks,
           w_o, norm_mlp_g, w_up, w_down, final_g):
    f = lambda a: np.ascontiguousarray(np.asarray(a, dtype=np.float32))
    x, c = f(x), f(c)
    w_in0 = f(w_in)[0]
    o3 = 672
    swq = w_in0[:, o3:o3 + 512].reshape(D, 2, 4, 64).transpose(0, 2, 1, 3).reshape(D, 512)
    kr = w_in0[:, 640:672]
    win_l = np.ascontiguousarray(np.concatenate(
        [w_in0[:, 0:672], swq, w_in0[:, o3 + 512:1440], kr[:, 16:32], kr[:, 0:16]], axis=1))
    wqb0 = f(w_qb)[0]
    wqb_l = np.ascontiguousarray(np.concatenate(
        [wqb0[:, :, 0:96], wqb0[:, :, 80:96], wqb0[:, :, 64:80]], axis=2).reshape(384, 1024))
    wkvb0 = f(w_kvb)[0]
    wkvk_l = np.ascontiguousarray(wkvb0[:, :, 0:64].reshape(256, 512))
    wkvv_l = np.ascontiguousarray(wkvb0[:, :, 64:128].reshape(256, 512))
    wo0 = f(w_o)[0]
    wo_swa = wo0[512:].reshape(2, 4, 64, D).transpose(1, 0, 2, 3).reshape(512, D)
    wo_l = np.ascontiguousarray(np.concatenate([wo0[:512], wo_swa], axis=0))
    identf, onesf, cs, mmask, smask = _host_consts()
    shared = {
        "badacol": np.ascontiguousarray(f(b_ada)[0].reshape(48, 128).T),
        "gmixcol": np.ascontiguousarray(f(norm_mix_g)[0].reshape(8, 128).T),
        "gmlpcol": np.ascontiguousarray(f(norm_mlp_g)[0].reshape(8, 128).T), "fg": f(final_g).reshape(1, D),
        "gqa": np.ascontiguousarray(f(g_qa)[0].reshape(3, 128).T), "gkva": np.ascontiguousarray(f(g_kva)[0].reshape(2, 128).T),
        "sinks": f(sinks)[0].reshape(1, 8), "wada": f(w_ada)[0], "win": win_l, "wqb": wqb_l, "wkvk": wkvk_l,
        "wkvv": wkvv_l, "wo": wo_l, "wup": f(w_up)[0], "wdn": f(w_down)[0],
        "identf": identf, "onesf": onesf, "cs": cs, "mmask": mmask, "smask": smask,
    }
    in_maps = []
    for b in range(8):
        m = dict(shared)
        m["x"] = x[b]
        m["ccol"] = np.ascontiguousarray(c[b].reshape(8, 128).T)
        in_maps.append(m)
    if "nc" not in _NC_CACHE:
        _NC_CACHE["nc"] = build_program()
    nc = _NC_CACHE["nc"]
    res = run_bass_kernel_spmd(nc, in_maps, core_ids=list(range(8)))
    out = np.stack([np.asarray(res.results[b]["out"], dtype=np.float32).reshape(T, D) for b in range(8)], axis=0)
    return out
```

```python
import numpy as np
import concourse.bass as bass
import concourse.mybir as mybir
from concourse.bass_utils import run_bass_kernel_spmd
from contextlib import ExitStack

F32 = mybir.dt.float32
BF16 = mybir.dt.bfloat16
ALU = mybir.AluOpType
AF = mybir.ActivationFunctionType

T = 2048
D = 1024
NT = 16
NG = 4
EPS = 1e-6
NEG = -1.0e6
WINC = 1472
PH_ALL = "SABCDEF"

DEBUG_DUMPS = False


def _sz(dt):
    return 2 if dt == BF16 else 4


class Alloc:
    def __init__(self, name, lo, hi, phases):
        self.name, self.lo, self.hi, self.phases = name, lo, hi, phases
        self.bufs = []
        self.overl = []


class Buf:
    __slots__ = ("name", "alloc", "w", "r", "dsem", "dcnt")

    def __init__(self, name, alloc=None):
        self.name, self.alloc = name, alloc
        self.w = None
        self.r = []
        self.dsem = None
        self.dcnt = 0
        if alloc is not None:
            alloc.bufs.append(self)


class Op:
    __slots__ = ("eng", "fn", "deps", "sig", "cnt", "dma", "dbuf", "dval")

    def __init__(self, eng, fn):
        self.eng, self.fn = eng, fn
        self.deps = []
        self.sig = False
        self.cnt = 0
        self.dma = False
        self.dbuf = None
        self.dval = 0


class Sched:
    ENGS = ["pe", "act", "dve", "pool", "sp"]

    def __init__(self):
        self.ops = {e: [] for e in self.ENGS}
        self.final = []

    def _add_dep(self, op, d, kind):
        if d is None or d is op:
            return
        if (not d.dma) and d.eng == op.eng:
            if op.eng == "pe" or op.eng == "sp":
                return
        op.deps.append((d, d.dbuf.dcnt if d.dma else 0))

    def _collect(self, op, r, w):
        for b in r:
            self._add_dep(op, b.w, "raw")
        for b in w:
            self._add_dep(op, b.w, "waw")
            for x in b.r:
                self._add_dep(op, x, "war")
            if b.alloc is not None:
                for a2 in b.alloc.overl:
                    for b2 in a2.bufs:
                        self._add_dep(op, b2.w, "waw")
                        for x in b2.r:
                            self._add_dep(op, x, "waw")
        for b in r:
            b.r.append(op)
        for b in w:
            b.w = op
            b.r = []

    def op(self, eng, fn, r=(), w=()):
        o = Op(eng, fn)
        self._collect(o, r, w)
        self.ops[eng].append(o)
        return o

    def dma(self, q, fn, key, r=(), w=()):
        o = Op(q, fn)
        o.dma = True
        o.dbuf = key
        self._collect(o, r, w)
        key.dcnt += 16
        o.dval = key.dcnt
        self.ops[q].append(o)
        return o

    def emit(self, nc, es, block, handles):
        sem_budget = [0]

        def newsem(name):
            sem_budget[0] += 1
            return es.enter_context(nc.semaphore(name))

        esem = {e: newsem("s_" + e) for e in ["pe", "act", "dve", "pool"]}
        for e in self.ENGS:
            for o in self.ops[e]:
                for (d, dv) in o.deps:
                    if d.dma:
                        if d.dbuf.dsem is None:
                            d.dbuf.dsem = newsem("d_" + d.dbuf.name.replace("[", "_").replace("]", ""))
                    else:
                        d.sig = True
            for o in self.ops[e]:
                if o.dma and o.dbuf.dsem is None:
                    o.dbuf.dsem = newsem("d_" + o.dbuf.name.replace("[", "_").replace("]", ""))
        for o in self.final:
            if o.dbuf.dsem is None:
                o.dbuf.dsem = newsem("d_" + o.dbuf.name)
        for e in self.ENGS:
            c = 0
            for o in self.ops[e]:
                if o.sig:
                    c += 1
                o.cnt = c

        def run(e, h):
            seen = {}
            for o in self.ops[e]:
                need = {}
                for (d, dv) in o.deps:
                    if d.dma:
                        k, v = d.dbuf.dsem, dv
                    else:
                        k, v = esem[d.eng], d.cnt
                    kk = k.num
                    if seen.get(kk, 0) >= v:
                        continue
                    if kk not in need or need[kk][1] < v:
                        need[kk] = (k, v)
                for kk, (k, v) in need.items():
                    h.wait_ge(k, v)
                    seen[kk] = v
                ins = o.fn(h)
                if o.dma:
                    ins.then_inc(o.dbuf.dsem, 16)
                elif o.sig:
                    ins.then_inc(esem[e], 1)
            if e == "sp":
                fin = {}
                for o in self.final:
                    kk = o.dbuf.dsem.num
                    if kk not in fin or fin[kk][1] < o.dval:
                        fin[kk] = (o.dbuf.dsem, o.dval)
                for kk, (k, v) in fin.items():
                    h.wait_ge(k, v)

        @block.tensor
        def _(h):
            run("pe", h)

        @block.scalar
        def _(h):
            run("act", h)

        @block.vector
        def _(h):
            run("dve", h)

        @block.gpsimd
        def _(h):
            run("pool", h)

        @block.sync
        def _(h):
            run("sp", h)


class Arena:
    def __init__(self, nbytes):
        self.nbytes = nbytes
        self.allocs = []
        self.t = None

    def alloc(self, name, nbytes, phases):
        nbytes = (nbytes + 63) // 64 * 64
        lo = 0
        placed = None
        ivs = sorted((a.lo, a.hi) for a in self.allocs if set(a.phases) & set(phases))
        for (l, h) in ivs:
            if l - lo >= nbytes:
                placed = lo
                break
            lo = max(lo, h)
        if placed is None:
            if self.nbytes - lo >= nbytes:
                placed = lo
            else:
                raise RuntimeError(f"arena OOM for {name} ({nbytes}B, phases {phases}); top={lo}")
        a = Alloc(name, placed, placed + nbytes, phases)
        for o in self.allocs:
            if o.lo < a.hi and a.lo < o.hi:
                o.overl.append(a)
                a.overl.append(o)
        self.allocs.append(a)
        return a

    def ap(self, a, shape, dtype, byte_off=0, p0=0):
        free = 1
        for s in shape[1:]:
            free *= s
        nb = free * _sz(dtype)
        assert a.lo + byte_off + nb <= a.hi, (a.name, shape)
        e0 = (a.lo + byte_off) // 2
        v = self.t[p0:p0 + shape[0], e0:e0 + nb // 2]
        if dtype != BF16:
            v = v.bitcast(dtype)
        if len(shape) == 3:
            v = v.rearrange("p (a b) -> p a b", a=shape[1], b=shape[2])
        elif len(shape) == 4:
            v = v.rearrange("p (a b c) -> p a b c", a=shape[1], b=shape[2], c=shape[3])
        return v

    def __lt__(self, o):
        return False


Alloc.__lt__ = lambda self, o: self.lo < o.lo


def build_program():
    nc = bass.Bass("TRN2", target_bir_lowering=False)
    S = Sched()
    AR = Arena(207 * 1024)
    _NC_CACHE["arena"] = AR

    def din(name, shape, dt=F32):
        return nc.dram_tensor(name, list(shape), dt, kind="ExternalInput").ap()

    x_d = din("x", [T, D])
    ccol_d = din("ccol", [128, 8])
    badacol_d = din("badacol", [128, 48])
    gmixcol_d = din("gmixcol", [128, 8])
    gmlpcol_d = din("gmlpcol", [128, 8])
    fg_d = din("fg", [1, D])
    gqa_d = din("gqa", [128, 3])
    gkva_d = din("gkva", [128, 2])
    sinks_d = din("sinks", [1, 8])
    wada_d = din("wada", [D, 6144])
    win_d = din("win", [D, WINC])
    wqb_d = din("wqb", [384, 1024])
    wkvk_d = din("wkvk", [256, 512])
    wkvv_d = din("wkvv", [256, 512])
    wo_d = din("wo", [D, D])
    wup_d = din("wup", [D, 4096])
    wdn_d = din("wdn", [4096, D])
    identf_d = din("identf", [128, 128])
    onesf_d = din("onesf", [128, 128])
    cs_d = din("cs", [32, 2, T])
    mmask_d = din("mmask", [128, 256])
    smask_d = din("smask", [128, 2048])
    out_d = nc.dram_tensor("out", [T, D], F32, kind="ExternalOutput").ap()
    dbg = {}

    es = ExitStack()
    with es:
        AR.t = es.enter_context(nc.sbuf_tensor("arena", [128, AR.nbytes // 2], BF16))
        PS = es.enter_context(nc.psum_tensor("ps", [128, 8, 512], F32))
        bankb = [Buf(f"bank{i}") for i in range(8)]

        def bk(i, rows=None, cols=None):
            r0, r1 = rows if rows else (0, 128)
            c0, c1 = cols if cols else (0, 512)
            return PS[r0:r1, i, c0:c1]

        def bk_bf(i):
            return PS[:, i, :].bitcast(BF16)

        def bk_pair(i):
            return PS[:, i:i + 2, :]

        KB = 1024

        prealloc = {}

        def region(name, shape, dtype, phases, p0=0):
            free = 1
            for s in shape[1:]:
                free *= s
            nb = free * _sz(dtype)
            if name in prealloc:
                a = prealloc[name]
                assert a.phases == phases and a.hi - a.lo >= nb, name
            else:
                a = AR.alloc(name, nb, phases)
            return a, AR.ap(a, shape, dtype, p0=p0)

        for (nm, nbytes, ph) in [
            ("identb", 256, PH_ALL), ("onesb", 256, PH_ALL), ("onesf", 512, PH_ALL), ("stat", 768, PH_ALL),
            ("junk", 2048, PH_ALL), ("SILU", 16, PH_ALL), ("IDENTF", 512, PH_ALL), ("MODCOL", 192, PH_ALL),
            ("GCOLS", 128, PH_ALL), ("DIAG0", 512, PH_ALL), ("DIAG1", 512, PH_ALL), ("EPSC", 4, PH_ALL),
            ("X1", NT * D * 4, "CDEF"), ("H2T", 8 * T * 2, "CE"), ("WUP0", 16384, "CE"), ("WDN0", 16384, "CE"),
            ("BC2A", 4096, "C"), ("BC2B", 4096, "C"), ("T1C", 4096, "C"), ("HBC0", 2048, "C"), ("HBC1", 2048, "C"),
            ("MIXT", 8 * T * 2, "BC"), ("WO", 16384, "BC"),
        ]:
            prealloc[nm] = AR.alloc(nm, nbytes, ph)

        a_identb, IDENTB = region("identb", [128, 128], BF16, PH_ALL)
        a_onesb, ONESB = region("onesb", [128, 128], BF16, PH_ALL)
        a_onesf, ONESF = region("onesf", [128, 128], F32, PH_ALL)
        a_stat, STAT = region("stat", [128, 192], F32, PH_ALL)
        b_identb, b_onesb, b_onesf = Buf("identb", a_identb), Buf("onesb", a_onesb), Buf("onesf", a_onesf)
        a_junk, JUNK = region("junk", [128, 1024], BF16, PH_ALL)
        a_EPS, EPSC = region("EPSC", [128, 1], F32, PH_ALL)
        bEPS = Buf("EPSC", a_EPS)
        S.op("dve", lambda h: h.memset(EPSC, EPS), w=[bEPS])

        S.dma("pool", lambda h: h.dma_start(out=IDENTB, in_=identf_d), b_identb, w=[b_identb])
        S.dma("pool", lambda h: h.dma_start(out=ONESB, in_=onesf_d), b_onesb, w=[b_onesb])
        S.dma("sp", lambda h: h.dma_start(out=ONESF, in_=onesf_d), b_onesf, w=[b_onesf])

        a_K, K_ = region("K", [96, 8, T], BF16, "AB")
        a_V = AR.alloc("V", 16 * 768 * 2, "AB")
        a_SWQ, SWQ = region("SWQ", [128, NT, 4, 128], BF16, "AB")
        a_SWK, SWK = region("SWK", [128, T], BF16, "AB")
        a_SWV = AR.alloc("SWV", 16 * 192 * 2, "AB")
        a_QLN, QLN = region("QLN", [128, 3, T], BF16, "AB")
        bK = [[Buf(f"K[{h}][{g}]", a_K) for g in range(NG)] for h in range(8)]
        bV = [Buf(f"V[{t}]", a_V) for t in range(NT)]
        bVones = Buf("Vones", a_V)
        bSWQ = [Buf(f"SWQ[{g}]", a_SWQ) for g in range(NG)]
        bSWK = [Buf(f"SWK[{g}]", a_SWK) for g in range(NG)]
        bSWV = [Buf(f"SWV[{t}]", a_SWV) for t in range(NT)]
        bSWVones = Buf("SWVones", a_SWV)
        bQLN = [Buf(f"QLN[{g}]", a_QLN) for g in range(NG)]

        V_E0 = a_V.lo // 2
        SWV_E0 = a_SWV.lo // 2
        PSTR = AR.nbytes // 2
        arena_handle = AR.t

        def rap(off, dims, parts=128):
            return bass.AP(tensor=arena_handle, offset=off, ap=[[PSTR, parts]] + [list(d) for d in dims])

        def v_tile_out(t):
            return rap(V_E0 + t * 768, [[192, 4], [128, 2], [1, 64]])

        def vaug(t, h):
            base = V_E0 + t * 768 + (h // 2) * 192
            return rap(base + (0 if h % 2 == 0 else 64), [[1, 128]])

        def swv_tile_out(t):
            return rap(SWV_E0 + t * 192, [[128, 2], [1, 64]])

        def swvaug(t, kv):
            return rap(SWV_E0 + t * 192 + (0 if kv == 0 else 64), [[1, 128]])

        S.op("dve", lambda h: h.memset(rap(V_E0 + 64, [[192, 64], [1, 64]]), 1.0), w=[bVones])
        S.op("dve", lambda h: h.memset(rap(SWV_E0 + 64, [[192, 16], [1, 64]]), 1.0), w=[bSWVones])

        a_WIN, WIN = region("WIN", [128, 8, WINC], BF16, "SA")
        a_WKVK, WKVK = region("WKVK", [128, 2, 512], BF16, "SA")
        a_WKVV, WKVV = region("WKVV", [128, 2, 512], BF16, "SA")
        a_WQB, WQB = region("WQB", [128, 3, 1024], BF16, "SAB")
        a_GQA, GQA = region("GQA", [128, 3], F32, "SAB")
        a_GKVA, GKVA = region("GKVA", [128, 2], F32, "SAB")
        bWINL, bWIN, bWKVK, bWKVV, bWQB = (Buf("WINL", a_WIN), Buf("WIN", a_WIN), Buf("WKVK", a_WKVK), Buf("WKVV", a_WKVV),
                                             Buf("WQB", a_WQB))
        bGQA, bGKVA = Buf("GQA", a_GQA), Buf("GKVA", a_GKVA)

        a_CCOL, CCOL = region("CCOL", [128, 8], F32, "S")
        a_SILU, SILU = region("SILU", [128, 8], BF16, PH_ALL)
        bCCOL, bSILU = Buf("CCOL", a_CCOL), Buf("SILU", a_SILU)
        a_IDF, IDENTF = region("IDENTF", [128, 128], F32, PH_ALL)
        b_identf = Buf("IDENTF", a_IDF)
        a_MODCOL, MODCOL = region("MODCOL", [128, 48], F32, PH_ALL)
        bMODCOL = [Buf(f"MODCOL[{j}]", a_MODCOL) for j in range(24)]
        a_BADAC, BADACOL = region("BADACOL", [128, 48], F32, "SA")
        bBADAC = Buf("BADACOL", a_BADAC)
        a_GCOLS, GCOLS = region("GCOLS", [128, 4, 8], F32, PH_ALL)
        bGCOLS = [Buf(f"GCOLS[{i}]", a_GCOLS) for i in range(4)]
        DIAG, bDIAG = [], []
        for i in range(2):
            a, v = region(f"DIAG{i}", [128, 128], F32, PH_ALL)
            DIAG.append(v)
            bDIAG.append(Buf(f"DIAG{i}", a))
        WADA, bWADA = [], []
        for i in range(2):
            a, v = region(f"WADA{i}", [128, 8, 256], BF16, "SA")
            WADA.append(v)
            bWADA.append(Buf(f"WADA{i}", a))

        S.dma("sp", lambda h: h.dma_start(out=CCOL, in_=ccol_d), bCCOL, w=[bCCOL])
        S.op("act", lambda h: h.activation(out=SILU, in_=CCOL, func=AF.Silu), r=[bCCOL], w=[bSILU])
        S.dma("sp", lambda h: h.dma_start(out=IDENTF, in_=identf_d), b_identf, w=[b_identf])
        S.dma("sp", lambda h: h.dma_start(out=BADACOL, in_=badacol_d), bBADAC, w=[bBADAC])
        S.dma("sp", lambda h: h.dma_start(out=GCOLS[:, 0, :], in_=gmixcol_d), bGCOLS[0], w=[bGCOLS[0]])
        S.dma("sp", lambda h: h.dma_start(out=GCOLS[:, 1, :], in_=gmlpcol_d), bGCOLS[1], w=[bGCOLS[1]])

        ada_ctr = [0]
        wada_v = wada_d.rearrange("(c p) n -> p c n", p=128)

        ada_issued = set()
        raw_issued = set()

        STG = [AR.ap(a_K, [128, 8, 256], F32, byte_off=i * 8192) for i in range(2)]
        a_STG = Alloc("STG", a_K.lo, a_K.lo + 16384, "S")
        a_STG.overl = [a_K] + list(a_K.overl)
        for o_ in a_STG.overl:
            o_.overl.append(a_STG)
        bSTG = [Buf(f"STG{i}", a_STG) for i in range(2)]

        STAGED = [1, 3, 5, 6, 7]

        def adaln_raw(n2):
            if n2 in raw_issued or n2 not in STAGED:
                return
            raw_issued.add(n2)
            sg = STAGED.index(n2) % 2
            S.dma("sp", lambda h: h.dma_start(out=STG[sg], in_=wada_v[:, :, n2 * 256:(n2 + 1) * 256]),
                  bSTG[sg], w=[bSTG[sg]])

        def adaln_dma(n2):
            if n2 in ada_issued:
                return
            ada_issued.add(n2)
            s_ = n2 % 2
            if n2 in STAGED:
                adaln_raw(n2)
                i_ = STAGED.index(n2)
                sg = i_ % 2
                S.op("dve", lambda h: h.tensor_copy(out=WADA[s_], in_=STG[sg]), r=[bSTG[sg]], w=[bWADA[s_]])
                if i_ + 2 < len(STAGED):
                    adaln_raw(STAGED[i_ + 2])
                return
            S.dma("pool", lambda h: h.dma_start(out=WADA[s_], in_=wada_v[:, :, n2 * 256:(n2 + 1) * 256]),
                  bWADA[s_], w=[bWADA[s_]])

        def adaln_cols(n2, bank):
            s_ = n2 % 2
            adaln_dma(n2)
            for half in range(2):
                j = 2 * n2 + half
                for k in range(8):
                    S.op("pe", lambda h, k=k, j=j, half=half: h.matmul(
                        bk(bank, None, (j, j + 1)), lhsT=WADA[s_][:, k, half * 128:(half + 1) * 128], rhs=SILU[:, k:k + 1],
                        start=(k == 0), stop=(k == 7)), r=[bSILU, bWADA[s_]], w=[bankb[bank]])
            S.op("dve", lambda h: h.tensor_tensor(out=MODCOL[:, 2 * n2:2 * n2 + 2], in0=bk(bank, None, (2 * n2, 2 * n2 + 2)),
                                                  in1=BADACOL[:, 2 * n2:2 * n2 + 2], op=ALU.add),
                 r=[bankb[bank], bBADAC], w=[bMODCOL[n2]])
            if n2 + 2 < 24 and not (n2 < 8 <= n2 + 2):
                adaln_dma(n2 + 2)

        dg_ctr = [0]

        def expand_bc(col_ap_fn, col_bufs, dst, dst_bufs, bank_list):
            for half in range(2):
                b = bank_list[half % len(bank_list)]
                for q in range(4):
                    jj = half * 4 + q
                    ds = dg_ctr[0] % 2
                    dg_ctr[0] += 1
                    S.op("dve", lambda h, jj=jj, ds=ds: h.tensor_scalar(out=DIAG[ds], in0=IDENTF, scalar1=col_ap_fn(jj),
                                                                        scalar2=None, op0=ALU.mult),
                         r=[b_identf] + col_bufs, w=[bDIAG[ds]])
                    S.op("pe", lambda h, q=q, ds=ds, b=b: h.matmul(bk(b, None, (q * 128, (q + 1) * 128)), lhsT=ONESF,
                                                                    rhs=DIAG[ds], start=True, stop=True),
                         r=[bDIAG[ds], b_onesf], w=[bankb[b]])
                S.op("dve", lambda h, half=half, b=b: h.tensor_copy(out=dst[:, half * 512:(half + 1) * 512], in_=bk(b)),
                     r=[bankb[b]], w=[dst_bufs[half]])

        a_BCA, BCA = region("BCA", [128, D], F32, "SA")
        a_BCB, BCB = region("BCB", [128, D], F32, "SA")
        bBCA = [Buf(f"BCA{i}", a_BCA) for i in range(2)]
        bBCB = [Buf(f"BCB{i}", a_BCB) for i in range(2)]

        def bcast_row(d_ap, n):
            return bass.AP(tensor=d_ap.tensor, offset=d_ap.offset, ap=[[0, 128], [1, n]])

        win_v = win_d.rearrange("(c p) n -> p c n", p=128)
        adaln_raw(1)
        adaln_raw(3)
        for n2 in range(8):
            adaln_cols(n2, n2 % 2)
            if n2 == 3:
                expand_bc(lambda jj: MODCOL[:, jj:jj + 1], bMODCOL[0:4], BCA, bBCA, [2, 3])
        S.op("dve", lambda h: h.scalar_tensor_tensor(out=GCOLS[:, 2, :], in0=MODCOL[:, 8:16], scalar=1.0, in1=GCOLS[:, 0, :],
                                                     op0=ALU.add, op1=ALU.mult),
             r=bMODCOL[4:8] + [bGCOLS[0]], w=[bGCOLS[2]])
        expand_bc(lambda jj: GCOLS[:, 2, jj:jj + 1], [bGCOLS[2]], BCB, bBCB, [2, 3])

        for c in range(8):
            S.dma("pool", lambda h, c=c: h.dma_start(out=WIN[:, c, 0:640], in_=win_v[:, c, 0:640]), bWINL,
                  w=[bWINL] if c == 0 else [])
        for c in range(8):
            S.dma("pool", lambda h, c=c: h.dma_start(out=WIN[:, c, 640:WINC], in_=win_v[:, c, 640:WINC]), bWIN,
                  w=[bWIN] if c == 0 else [])
        S.dma("pool", lambda h: h.dma_start(out=WKVK, in_=wkvk_d.rearrange("(c p) n -> p c n", p=128)), bWKVK, w=[bWKVK])
        S.dma("pool", lambda h: h.dma_start(out=WKVV, in_=wkvv_d.rearrange("(c p) n -> p c n", p=128)), bWKVV, w=[bWKVV])
        S.dma("pool", lambda h: h.dma_start(out=WQB, in_=wqb_d.rearrange("(c p) n -> p c n", p=128)), bWQB, w=[bWQB])
        adaln_dma(8)
        adaln_dma(9)
        S.dma("sp", lambda h: h.dma_start(out=GQA, in_=gqa_d), bGQA, w=[bGQA])
        S.dma("sp", lambda h: h.dma_start(out=GKVA, in_=gkva_d), bGKVA, w=[bGKVA])
        def weight_prep_a():
            S.op("dve", lambda h: h.tensor_scalar(out=WIN[:, :, 1440:1456], in0=WIN[:, :, 1440:1456], scalar1=-1.0,
                                                  scalar2=None, op0=ALU.mult), r=[bWIN], w=[bWIN])
            for c in range(2):
                S.op("dve", lambda h, c=c: h.tensor_scalar(out=WKVK[:, c, :], in0=WKVK[:, c, :], scalar1=GKVA[:, c:c + 1],
                                                           scalar2=None, op0=ALU.mult), r=[bWKVK, bGKVA], w=[bWKVK])
                S.op("dve", lambda h, c=c: h.tensor_scalar(out=WKVV[:, c, :], in0=WKVV[:, c, :], scalar1=GKVA[:, c:c + 1],
                                                           scalar2=None, op0=ALU.mult), r=[bWKVV, bGKVA], w=[bWKVV])

        def weight_prep_b():
            for c in range(3):
                S.op("dve", lambda h, c=c: h.tensor_scalar(out=WQB[:, c, :], in0=WQB[:, c, :], scalar1=GQA[:, c:c + 1],
                                                           scalar2=None, op0=ALU.mult), r=[bWQB, bGQA], w=[bWQB])
                wq4 = WQB[:, c, :].rearrange("p (h e) -> p h e", h=8, e=128)
                S.op("dve", lambda h, wq4=wq4: h.tensor_scalar(out=wq4[:, :, 96:112], in0=wq4[:, :, 96:112], scalar1=-1.0,
                                                               scalar2=None, op0=ALU.mult), r=[bWQB], w=[bWQB])

        def make_norm_bufs(ph):
            a_T1, T1 = region("T1" + ph, [128, D], F32, ph)
            HB, bHB = [], []
            for i in range(2):
                a, v = region(f"HB{ph}{i}", [128, D], BF16, ph)
                HB.append(v)
                bHB.append(Buf(f"HB{ph}{i}", a))
            return T1, Buf("T1" + ph, a_T1), HB, bHB

        stat_ctr = [0]

        def stat_cols(n):
            c = stat_ctr[0]
            stat_ctr[0] += n
            assert stat_ctr[0] <= 192
            return c

        def rms_rstd(src, src_bufs, nfeat, tag):
            c = stat_cols(3)
            bs = Buf(f"st_{tag}", a_stat)
            S.op("act", lambda h: h.activation(out=JUNK[:, 0:nfeat], in_=src, func=AF.Square, accum_out=STAT[:, c:c + 1]),
                 r=src_bufs, w=[bs])
            S.op("act", lambda h: h.activation(out=STAT[:, c + 1:c + 2], in_=STAT[:, c:c + 1], func=AF.Ln,
                                               scale=1.0 / nfeat, bias=EPSC), r=[bs, bEPS], w=[bs])
            S.op("act", lambda h: h.activation(out=STAT[:, c + 2:c + 3], in_=STAT[:, c + 1:c + 2], func=AF.Exp, scale=-0.5),
                 r=[bs], w=[bs])
            return STAT[:, c + 2:c + 3], bs


        def norm_pre(src, src_bufs, T1, bT1, HB, bHB, hs, GM, bGM, SH, bSH, tag, add_eng="dve", extra_r=()):
            rstd, bs = rms_rstd(src, src_bufs, D, tag)
            S.op("dve", lambda h: h.scalar_tensor_tensor(out=T1, in0=src, scalar=rstd, in1=GM, op0=ALU.mult, op1=ALU.mult),
                 r=src_bufs + [bs] + bGM, w=[bT1])
            S.op(add_eng, lambda h: h.tensor_tensor(out=HB[hs], in0=T1, in1=SH, op=ALU.add),
                 r=[bT1] + bSH + list(extra_r), w=[bHB[hs]])

        def norm_tr(HB, bHB, hs, tp_bank, dstT, dst_bufs, evac="dve"):
            tpv = bk_bf(tp_bank)
            for k in range(8):
                S.op("pe", lambda h, k=k: h.transpose(tpv[:, k * 128:(k + 1) * 128], HB[hs][:, k * 128:(k + 1) * 128], IDENTB),
                     r=[bHB[hs], b_identb], w=[bankb[tp_bank]])
            if evac == "act":
                S.op("act", lambda h: h.activation(out=dstT, in_=tpv.rearrange("p (a b) -> p a b", a=8, b=128), func=AF.Copy),
                     r=[bankb[tp_bank]], w=dst_bufs)
            else:
                S.op("dve", lambda h: h.tensor_copy(out=dstT, in_=tpv.rearrange("p (a b) -> p a b", a=8, b=128)),
                     r=[bankb[tp_bank]], w=dst_bufs)

        XT, bXT = [], []
        for i in range(2):
            a, v = region(f"XT{i}", [128, D], F32, "A")
            XT.append(v)
            bXT.append(Buf(f"XT{i}", a))
        HBA, bHBA = [], []
        for i in range(2):
            a, v = region(f"HBA{i}", [128, D], BF16, "A")
            HBA.append(v)
            bHBA.append(Buf(f"HBA{i}", a))
        H1T, bH1T = [], []
        for i in range(2):
            a, v = region(f"H1T{i}", [128, 8, 512], BF16, "A")
            H1T.append(v)
            bH1T.append([Buf(f"H1T{i}[{j}]", a) for j in range(4)])
        a_LATF, LATF = region("LATF", [128, 5, 512], F32, "A")
        a_SQN, SQN = region("SQN", [128, 5, 512], BF16, "A")
        a_RSTD, RSTD = region("RSTD", [128, 2, 512], F32, "A")
        bLATF = [Buf(f"LATF[{m}]", a_LATF) for m in range(5)]
        bSQN = [Buf(f"SQN[{m}]", a_SQN) for m in range(5)]
        bRSTD = [Buf(f"RSTD[{m}]", a_RSTD) for m in range(2)]
        CSA, bCSA = [], []
        for i in range(1):
            a, v = region(f"CSA{i}", [32, 2, 512], F32, "A", p0=64)
            CSA.append(v)
            bCSA.append(Buf(f"CSA{i}", a))
        a_TQA, TQA = region("TQA", [32, 2, 512], F32, "A", p0=64)
        bTQA = [Buf(f"TQA[{i}]", a_TQA) for i in range(2)]
        a_KPE, KPE = region("KPE", [32, 512], BF16, "A", p0=64)
        bKPEt = Buf("KPEt", a_KPE)

        def mm_group(bank, rows, ncols, lhs_list, rhs_list, rbufs):
            n = len(lhs_list)
            for i in range(n):
                S.op("pe", lambda h, i=i: h.matmul(bk(bank, rows, (0, ncols)), lhsT=lhs_list[i], rhs=rhs_list[i],
                                                   start=(i == 0), stop=(i == n - 1)),
                     r=rbufs, w=[bankb[bank]])

        brr = [0]

        def next_bank(lst):
            b = lst[brr[0] % len(lst)]
            brr[0] += 1
            return b

        tp_ctr = [0]

        a_H1T1 = bH1T[1][0].alloc
        XTALT = [AR.ap(a_H1T1, [128, D], F32, byte_off=i * D * 4) for i in range(2)]
        a_XTALT = Alloc("XTALT", a_H1T1.lo, a_H1T1.hi, "A")
        a_XTALT.overl = [a_H1T1] + list(a_H1T1.overl)
        for o_ in a_XTALT.overl:
            o_.overl.append(a_XTALT)
        bXTALT = [Buf(f"XTALT{i}", a_XTALT) for i in range(2)]

        def a_pre(t):
            if t in (2, 3):
                xa, bxa = XTALT[t - 2], bXTALT[t - 2]
                S.dma("sp", lambda h: h.dma_start(out=xa, in_=x_d[t * 128:(t + 1) * 128, :]), bxa, w=[bxa])
                norm_pre(xa, [bxa], xa, bxa, HBA, bHBA, t % 2, BCB, bBCB, BCA, bBCA, f"n1_{t}")
                return
            xs = t % 2
            S.dma("sp", lambda h: h.dma_start(out=XT[xs], in_=x_d[t * 128:(t + 1) * 128, :]), bXT[xs], w=[bXT[xs]])
            norm_pre(XT[xs], [bXT[xs]], XT[xs], bXT[xs], HBA, bHBA, t % 2, BCB, bBCB, BCA, bBCA, f"n1_{t}")

        def a_tr(t):
            G, j = t // 4, t % 4
            gs = G % 2
            tpb = tp_ctr[0] % 2
            tp_ctr[0] += 1
            norm_tr(HBA, bHBA, t % 2, tpb, H1T[gs][:, :, j * 128:(j + 1) * 128], [bH1T[gs][j]],
                    evac=("act" if t < 4 else "dve"))

        def a_part1(G):
            gs = G % 2
            gcols = slice(G * 512, (G + 1) * 512)
            hb = bH1T[gs]
            hrhs = [H1T[gs][:, k, :] for k in range(8)]
            for m in range(5):
                b = 2 + (m % 2)
                mm_group(b, None, 512, [WIN[:, k, m * 128:(m + 1) * 128] for k in range(8)], hrhs, hb + [bWINL])
                S.op("act", lambda h, b=b, m=m: h.activation(out=LATF[:, m, :], in_=bk(b), func=AF.Copy),
                     r=[bankb[b]], w=[bLATF[m]])
                S.op("act", lambda h, b=b, m=m: h.activation(out=SQN[:, m, :], in_=bk(b), func=AF.Square),
                     r=[bankb[b]], w=[bSQN[m]])

        def a_stats(G):
            gcols = slice(G * 512, (G + 1) * 512)
            mm_group(4, None, 512, [ONESB] * 3, [SQN[:, m, :] for m in range(3)], bSQN[0:3] + [b_onesb])
            mm_group(5, None, 512, [ONESB] * 2, [SQN[:, m, :] for m in range(3, 5)], bSQN[3:5] + [b_onesb])
            for (i, b, nf) in ((1, 5, 256), (0, 4, 384)):
                S.op("act", lambda h, i=i, b=b, nf=nf: h.activation(out=RSTD[:, i, :], in_=bk(b), func=AF.Ln,
                                                                    scale=1.0 / nf, bias=EPSC),
                     r=[bankb[b], bEPS], w=[bRSTD[i]])
                S.op("act", lambda h, i=i: h.activation(out=RSTD[:, i, :], in_=RSTD[:, i, :], func=AF.Exp, scale=-0.5),
                     r=[bRSTD[i]], w=[bRSTD[i]])
            for m in range(3, 5):
                S.op("dve", lambda h, m=m: h.tensor_tensor(out=SQN[:, m, :], in0=LATF[:, m, :], in1=RSTD[:, 1, :],
                                                           op=ALU.mult),
                     r=[bLATF[m], bRSTD[1]], w=[bSQN[m]])
            for m in range(3):
                S.op("dve", lambda h, m=m: h.tensor_tensor(out=QLN[:, m, gcols], in0=LATF[:, m, :],
                                                           in1=RSTD[:, 0, :], op=ALU.mult),
                     r=[bLATF[m], bRSTD[0]], w=[bQLN[G]])

        def a_swq(G, c4):
            gs = G % 2
            hb = bH1T[gs]
            hrhs = [H1T[gs][:, k, :] for k in range(8)]
            b = 2 + (c4 % 2)
            mm_group(b, None, 512, [WIN[:, k, 672 + c4 * 128:672 + (c4 + 1) * 128] for k in range(8)], hrhs, hb + [bWIN])
            if c4 % 2 == 0:
                S.op("act", lambda h: h.activation(out=SWQ[:, G * 4:(G + 1) * 4, c4, :],
                                                   in_=bk(b).rearrange("p (a b) -> p a b", a=4, b=128), func=AF.Copy),
                     r=[bankb[b]], w=[bSWQ[G]])
            else:
                S.op("dve", lambda h: h.tensor_copy(out=SWQ[:, G * 4:(G + 1) * 4, c4, :],
                                                    in_=bk(b).rearrange("p (a b) -> p a b", a=4, b=128)),
                     r=[bankb[b]], w=[bSWQ[G]])

        def a_part2_tail(G):
            gs = G % 2
            gcols = slice(G * 512, (G + 1) * 512)
            hb = bH1T[gs]
            hrhs = [H1T[gs][:, k, :] for k in range(8)]
            S.dma("sp", lambda h: h.dma_start(out=CSA[0], in_=cs_d[:, :, G * 512:(G + 1) * 512]), bCSA[0], w=[bCSA[0]])
            mm_group(4, None, 512, [WIN[:, k, 1184:1312] for k in range(8)], hrhs, hb + [bWIN])
            S.op("dve", lambda h: h.tensor_copy(out=SWK[:, gcols], in_=bk(4)), r=[bankb[4]], w=[bSWK[G]])
            mm_group(6, (64, 96), 512, [WIN[:, k, 640:672] for k in range(8)], hrhs, hb + [bWIN])
            mm_group(7, (64, 96), 512, [WIN[:, k, 1440:1472] for k in range(8)], hrhs, hb + [bWIN])
            S.op("dve", lambda h: h.tensor_tensor(out=TQA[:, 0, :], in0=bk(6, (64, 96)), in1=CSA[0][:, 0, :], op=ALU.mult),
                 r=[bankb[6], bCSA[0]], w=[bTQA[0]])
            S.op("dve", lambda h: h.tensor_tensor(out=TQA[:, 1, :], in0=bk(7, (64, 96)), in1=CSA[0][:, 1, :], op=ALU.mult),
                 r=[bankb[7], bCSA[0]], w=[bTQA[1]])
            S.op("dve", lambda h: h.tensor_tensor(out=KPE, in0=TQA[:, 0, :], in1=TQA[:, 1, :], op=ALU.add),
                 r=bTQA, w=[bKPEt])
            for hh in range(8):
                S.op("dve", lambda h, hh=hh: h.tensor_copy(out=K_[64:96, hh, gcols], in_=KPE),
                     r=[bKPEt], w=[bK[hh][G]])
            for j in range(4):
                t = G * 4 + j
                b2 = 6 + (j % 2)
                mm_group(b2, None, 128, [H1T[gs][:, k, j * 128:(j + 1) * 128] for k in range(8)],
                         [WIN[:, k, 1312:1440] for k in range(8)], hb + [bWIN])
                S.op("act", lambda h, t=t, b2=b2: h.activation(out=swv_tile_out(t), in_=bk(b2, None, (0, 128)).rearrange("p (a b) -> p a b", a=2, b=64), func=AF.Copy),
                     r=[bankb[b2]], w=[bSWV[t]])

        def a_part3(G):
            gcols = slice(G * 512, (G + 1) * 512)
            kvln = [SQN[:, 3, :], SQN[:, 4, :]]
            bkvln = [bSQN[3], bSQN[4]]
            for hp in range(4):
                b = 2 + (hp % 2)
                mm_group(b, None, 512, [WKVK[:, c, hp * 128:(hp + 1) * 128] for c in range(2)], kvln, bkvln + [bWKVK])
                for half in range(2):
                    hh = 2 * hp + half
                    S.op("act", lambda h, hh=hh, b=b, half=half: h.activation(
                        out=K_[0:64, hh, gcols], in_=bk(b, (half * 64, half * 64 + 64)), func=AF.Copy),
                         r=[bankb[b]], w=[bK[hh][G]])
            for j in range(4):
                t = G * 4 + j
                b = 4 + (j % 2)
                mm_group(b, None, 512, [kvln[c][:, j * 128:(j + 1) * 128] for c in range(2)],
                         [WKVV[:, c, :] for c in range(2)], bkvln + [bWKVV])
                S.op("dve", lambda h, t=t, b=b: h.tensor_copy(out=v_tile_out(t), in_=bk(b).rearrange("p (a b c) -> p a b c", a=4, b=2, c=64)), r=[bankb[b]], w=[bV[t]])

        a_pre(0)
        a_pre(1)
        a_tr(0)
        a_pre(2)
        a_tr(1)
        a_pre(3)
        a_tr(2)
        a_tr(3)
        weight_prep_a()
        for G in range(NG):
            nxt = G + 1 < NG
            t0 = (G + 1) * 4
            n2 = 8 + 4 * G
            a_part1(G)
            if nxt:
                a_pre(t0)
                a_pre(t0 + 1)
            a_stats(G)
            adaln_cols(n2, 6)
            a_swq(G, 0)
            if nxt:
                a_tr(t0)
                a_pre(t0 + 2)
            a_swq(G, 1)
            if nxt:
                a_tr(t0 + 1)
                a_pre(t0 + 3)
            adaln_cols(n2 + 1, 7)
            a_swq(G, 2)
            if nxt:
                a_tr(t0 + 2)
            a_swq(G, 3)
            if nxt:
                a_tr(t0 + 3)
            adaln_cols(n2 + 2, 5)
            a_part2_tail(G)
            adaln_cols(n2 + 3, 1)
            a_part3(G)
            if G == 0:
                weight_prep_b()
        S.op("dve", lambda h: h.scalar_tensor_tensor(out=GCOLS[:, 3, :], in0=MODCOL[:, 32:40], scalar=1.0, in1=GCOLS[:, 1, :],
                                                     op0=ALU.add, op1=ALU.mult),
             r=bMODCOL[16:20] + [bGCOLS[1]], w=[bGCOLS[3]])


        a_MIXT, MIXT = region("MIXT", [128, 8, T], BF16, "BC")
        bMIXT = [[Buf(f"MIXT[{c}][{g}]", a_MIXT) for g in range(NG)] for c in range(8)]
        QG, bQG = [], []
        for i in range(2):
            a, v = region(f"QG{i}", [96, 8, 512], BF16, "B")
            QG.append(v)
            bQG.append([Buf(f"QG{i}[{h}]", a) for h in range(8)])
        PT, bPT = [], []
        for i in range(3):
            a, v = region(f"PT{i}", [128, 2, 512], BF16, "B")
            PT.append(v)
            bPT.append(Buf(f"PT{i}", a))
        RDEN, bRDEN = [], []
        for i in range(2):
            a, v = region(f"RDEN{i}", [128, 512], F32, "B")
            RDEN.append(v)
            bRDEN.append(Buf(f"RDEN{i}", a))
        a_MM, MMASK = region("MMASK", [128, 256], BF16, "B")
        a_SM, SMASK = region("SMASK", [128, 2, 2, 512], BF16, "B")
        bMM, bSM = Buf("MMASK", a_MM), Buf("SMASK", a_SM)
        CSB, bCSB = [], []
        for i in range(1):
            a, v = region(f"CSB{i}", [32, 2, 512], F32, "B", p0=64)
            CSB.append(v)
            bCSB.append(Buf(f"CSB{i}", a))
        a_TQB, TQB = region("TQB", [32, 2, 2, 512], F32, "B", p0=64)
        bTQB = [[Buf(f"TQB[{s}][{i}]", a_TQB) for i in range(2)] for s in range(2)]
        a_SK8, SK8 = region("SK8", [128, 8], F32, "B")
        a_ESK, ESK = region("ESK", [128, 2, 512], F32, "B")
        bSK8, bESK = Buf("SK8", a_SK8), Buf("ESK", a_ESK)

        S.dma("pool", lambda h: h.dma_start(out=MMASK, in_=mmask_d), bMM, w=[bMM])
        S.dma("pool", lambda h: h.dma_start(out=SMASK.rearrange("p a b c -> p (a b c)"), in_=smask_d), bSM, w=[bSM])
        S.dma("sp", lambda h: h.dma_start(out=SK8, in_=bcast_row(sinks_d, 8)), bSK8, w=[bSK8])
        S.op("act", lambda h: h.activation(out=SK8, in_=SK8, func=AF.Exp), r=[bSK8], w=[bSK8])
        for hh in range(8):
            kv, g = hh // 4, hh % 4
            S.op("dve", lambda h, hh=hh, kv=kv, g=g: h.tensor_scalar(out=ESK[:, kv, g * 128:(g + 1) * 128], in0=ONESF,
                                                                  scalar1=SK8[:, hh:hh + 1], scalar2=None, op0=ALU.mult),
                 r=[bSK8, b_onesf], w=[bESK])

        MLA_SCALE = float(96 ** -0.5)
        SWA_SCALE = 0.125
        SPAIRS = [(0, 1), (2, 3), (4, 5)]
        OBANKS = [6, 7]
        NPT = 3
        DEPTH = 2
        sp_ctr = [0]
        pt_ctr = [0]
        ob_ctr = [0]
        rd_ctr = [0]

        def sc_view(pair, idx, c0, c1, rows=(0, 128)):
            return banks[pair[idx]][rows[0]:rows[1], c0:c1]

        class Unit:
            pass

        def load_cs(G):
            S.dma("sp", lambda h: h.dma_start(out=CSB[0], in_=cs_d[:, :, G * 512:(G + 1) * 512]), bCSB[0], w=[bCSB[0]])

        def qp_unit(G, hh):
            u = Unit()
            u.kind, u.G, u.h = "qp", G, hh
            u.ob = OBANKS[ob_ctr[0] % 2]
            return u

        def emit_qp_mm(u):
            G, hh = u.G, u.h
            gcols = slice(G * 512, (G + 1) * 512)
            bm = u.ob
            mm_group(bm, None, 512, [WQB[:, c, hh * 128:(hh + 1) * 128] for c in range(3)],
                     [QLN[:, c, gcols] for c in range(3)], [bQLN[G], bWQB])

        def emit_qp_evac(u):
            G, hh = u.G, u.h
            qs = G % 2
            ts = hh % 2
            bm = u.ob
            S.op("act", lambda h: h.activation(out=QG[qs][0:64, hh, :], in_=bk(bm, (0, 64)), func=AF.Copy),
                 r=[bankb[bm]], w=[bQG[qs][hh]])
            S.op("dve", lambda h: h.tensor_tensor(out=TQB[:, ts, 0, :], in0=bk(bm, (64, 96)), in1=CSB[0][:, 0, :],
                                                  op=ALU.mult), r=[bankb[bm], bCSB[0]], w=[bTQB[ts][0]])
            S.op("dve", lambda h: h.tensor_tensor(out=TQB[:, ts, 1, :], in0=bk(bm, (96, 128)), in1=CSB[0][:, 1, :],
                                                  op=ALU.mult), r=[bankb[bm], bCSB[0]], w=[bTQB[ts][1]])
            S.op("pool", lambda h: h.tensor_tensor(out=QG[qs][64:96, hh, :], in0=TQB[:, ts, 0, :],
                                                   in1=TQB[:, ts, 1, :], op=ALU.add),
                 r=bTQB[ts], w=[bQG[qs][hh]])

        def mla_units(G, hh):
            qs = G % 2
            units = []
            for j0 in range(0, 4 * G, 2):
                u = Unit()
                u.tiles = [(j0, 0, 512, None, 0), (j0 + 1, 0, 512, None, 0)]
                u.e0 = 0
                units.append(u)
            tri = MMASK[:, 128:256]
            negtri = MMASK[:, 0:256]
            u = Unit()
            u.tiles = [(4 * G, 0, 512, (0, 128, tri), 0), (4 * G + 1, 0, 512, (0, 256, negtri), 128)]
            u.e0 = 0
            units.append(u)
            u = Unit()
            u.tiles = [(4 * G + 2, 256, 512, (256, 128, tri), 256), (4 * G + 3, 256, 512, (256, 256, negtri), 384)]
            u.e0 = 256
            units.append(u)
            ob = OBANKS[ob_ctr[0] % 2]
            ob_ctr[0] += 1
            for u in units:
                u.G, u.h, u.qs, u.ob = G, hh, qs, ob
                u.kind = "mla"
            units[0].first = True
            units[-1].last = True
            return units

        def emit_scores(u):
            u.pair = SPAIRS[sp_ctr[0] % len(SPAIRS)]
            sp_ctr[0] += 1
            if u.kind == "qp":
                u.ob = u.pair[0]
                emit_qp_mm(u)
                return
            if u.kind == "mla":
                for idx, (j, c0, c1, msk, p0) in enumerate(u.tiles):
                    b = u.pair[idx]
                    S.op("pe", lambda h, j=j, c0=c0, c1=c1, b=b, msk=msk: h.matmul(
                        bk(b, None, (c0, c1)), lhsT=K_[0:96, u.h, j * 128:(j + 1) * 128], rhs=QG[u.qs][0:96, u.h, c0:c1],
                        start=True, stop=(msk is None)),
                         r=[bK[u.h][j // 4], bQG[u.qs][u.h]], w=[bankb[b]])
                    if msk is not None:
                        S.op("pe", lambda h, msk=msk, b=b: h.matmul(bk(b, None, (msk[0], msk[0] + msk[1])), lhsT=IDENTB,
                                                                    rhs=msk[2], start=False, stop=True),
                             r=[b_identb, bMM], w=[bankb[b]])
            else:
                n, kv = u.n, u.kv
                rows = (kv * 64, kv * 64 + 64)
                qv = SWQ[rows[0]:rows[1], n, :, :].rearrange("p a b -> p (a b)")
                for idx in ((0, 1) if n > 0 else (1,)):
                    tkt = n - 1 + idx
                    b = u.pair[idx]
                    S.op("pe", lambda h, tkt=tkt, b=b: h.matmul(bk(b), lhsT=SWK[rows[0]:rows[1], tkt * 128:(tkt + 1) * 128],
                                                                rhs=qv, start=True, stop=False),
                         r=[bSWK[tkt // 4], bSWQ[n // 4]], w=[bankb[b]])
                    S.op("pe", lambda h, idx=idx, b=b: h.matmul(bk(b), lhsT=IDENTB, rhs=SMASK[:, idx, kv, :],
                                                                start=False, stop=True),
                         r=[b_identb, bSM], w=[bankb[b]])

        def emit_exp(u):
            if u.kind == "qp":
                emit_qp_evac(u)
                return
            u.pt = pt_ctr[0] % NPT
            pt_ctr[0] += 1
            p = u.pt
            if u.kind == "mla":
                e0 = u.e0
                pairv = PS[:, u.pair[0]:u.pair[0] + 2, e0:512]
                S.op("act", lambda h: h.activation(out=PT[p][:, :, e0:512], in_=pairv, func=AF.Exp, scale=MLA_SCALE),
                     r=[bankb[u.pair[0]], bankb[u.pair[1]]], w=[bPT[p]])
            else:
                if u.n > 0:
                    pairv = bk_pair(u.pair[0])
                    S.op("act", lambda h: h.activation(out=PT[p], in_=pairv, func=AF.Exp, scale=SWA_SCALE),
                         r=[bankb[u.pair[0]], bankb[u.pair[1]]], w=[bPT[p]])
                else:
                    b = u.pair[1]
                    S.op("act", lambda h: h.activation(out=PT[p][:, 1, :], in_=bk(b), func=AF.Exp, scale=SWA_SCALE),
                         r=[bankb[b]], w=[bPT[p]])

        cur_ob = [None]
        deferred = []

        NFILL = 0

        def emit_pv(u):
            if u.kind == "qp":
                return
            p = u.pt
            for _ in range(NFILL):
                S.op("pe", lambda h: h.ldweights(IDENTB), r=[b_identb])
            if u.kind == "mla":
                ob = u.ob
                open_bank[0] = None if getattr(u, "last", False) else ob
                for idx, (j, c0, c1, msk, p0) in enumerate(u.tiles):
                    first = getattr(u, "first", False) and idx == 0
                    last = getattr(u, "last", False) and idx == len(u.tiles) - 1
                    S.op("pe", lambda h, j=j, p0=p0, idx=idx, first=first, last=last: h.matmul(
                        bk(ob, None, (p0, 512)), lhsT=vaug(j, u.h), rhs=PT[p][:, idx, p0:512], start=first, stop=last),
                         r=[bV[j], bVones, bPT[p]], w=[bankb[ob]])
                if getattr(u, "last", False):
                    hh, G = u.h, u.G
                    orow = (0, 64) if hh % 2 == 0 else (64, 128)
                    drow = (64, 128) if hh % 2 == 0 else (0, 64)
                    rs = rd_ctr[0] % 2
                    rd_ctr[0] += 1
                    if G >= 1:
                        S.op("dve", lambda h: h.reciprocal(out=RDEN[rs][orow[0]:orow[1], :], in_=bk(ob, drow)),
                             r=[bankb[ob]], w=[bRDEN[rs]])
                    else:
                        S.op("act", lambda h: h.activation(out=RDEN[rs][orow[0]:orow[1], :], in_=bk(ob, drow), func=AF.Ln),
                             r=[bankb[ob]], w=[bRDEN[rs]])
                        S.op("act", lambda h: h.activation(out=RDEN[rs][orow[0]:orow[1], :],
                                                           in_=RDEN[rs][orow[0]:orow[1], :], func=AF.Exp, scale=-1.0),
                             r=[bRDEN[rs]], w=[bRDEN[rs]])
                    S.op("dve", lambda h: h.tensor_tensor(out=MIXT[orow[0]:orow[1], hh // 2, G * 512:(G + 1) * 512],
                                                          in0=bk(ob, orow), in1=RDEN[rs][orow[0]:orow[1], :], op=ALU.mult),
                         r=[bankb[ob], bRDEN[rs]], w=[bMIXT[hh // 2][G]])
            else:
                n, kv = u.n, u.kv
                ob = OBANKS[ob_ctr[0] % 2]
                ob_ctr[0] += 1
                u.ob = ob
                idxs = (0, 1) if n > 0 else (1,)
                for ii, idx in enumerate(idxs):
                    tkt = n - 1 + idx
                    S.op("pe", lambda h, tkt=tkt, idx=idx, ii=ii: h.matmul(bk(ob), lhsT=swvaug(tkt, kv), rhs=PT[p][:, idx, :],
                                                                         start=(ii == 0), stop=(ii == len(idxs) - 1)),
                         r=[bSWV[tkt], bSWVones, bPT[p]], w=[bankb[ob]])
                orow = (0, 64) if kv == 0 else (64, 128)
                drow = (64, 128) if kv == 0 else (0, 64)
                rs = rd_ctr[0] % 2
                rd_ctr[0] += 1
                S.op("dve", lambda h: h.tensor_tensor(out=RDEN[rs][orow[0]:orow[1], :], in0=bk(ob, drow),
                                                      in1=ESK[orow[0]:orow[1], kv, :], op=ALU.add),
                     r=[bankb[ob], bESK], w=[bRDEN[rs]])
                S.op("act", lambda h: h.activation(out=RDEN[rs][orow[0]:orow[1], :], in_=RDEN[rs][orow[0]:orow[1], :],
                                                   func=AF.Ln), r=[bRDEN[rs]], w=[bRDEN[rs]])
                S.op("act", lambda h: h.activation(out=RDEN[rs][orow[0]:orow[1], :], in_=RDEN[rs][orow[0]:orow[1], :],
                                                   func=AF.Exp, scale=-1.0), r=[bRDEN[rs]], w=[bRDEN[rs]])
                G = n // 4

                def final_mult():
                    S.op("dve", lambda h: h.tensor_tensor(
                        out=MIXT[orow[0]:orow[1], 4:8, n * 128:(n + 1) * 128],
                        in0=bk(ob, orow).rearrange("p (g q) -> p g q", g=4, q=128),
                        in1=RDEN[rs][orow[0]:orow[1], :].rearrange("p (g q) -> p g q", g=4, q=128), op=ALU.mult),
                         r=[bankb[ob], bRDEN[rs]], w=[bMIXT[4 + g][G] for g in range(4)])
                deferred.append([1, final_mult])

        pending = []

        def run_deferred(force=False):
            for e in list(deferred):
                if force or e[0] <= 0:
                    deferred.remove(e)
                    e[1]()
                else:
                    e[0] -= 1

        qp_queue = []
        qp_free = []

        open_bank = [None]

        def issue_qp(bank):
            if bank == open_bank[0]:
                bank = OBANKS[1 - OBANKS.index(bank)]
            if qp_queue:
                G2, h2 = qp_queue.pop(0)
                u2 = qp_unit(G2, h2)
                u2.ob = bank
                emit_qp_mm(u2)
                emit_qp_evac(u2)

        def after_pv(v):
            pass

        def push_unit(u):
            emit_scores(u)
            emit_exp(u)
            run_deferred(force=True)
            pending.append(u)
            if len(pending) > DEPTH:
                v = pending.pop(0)
                emit_pv(v)
                after_pv(v)
            for e in list(qp_free):
                if e[0] <= 0:
                    qp_free.remove(e)
                    issue_qp(e[1])
                else:
                    e[0] -= 1

        def flush_units():
            while pending:
                run_deferred(force=True)
                v = pending.pop(0)
                emit_pv(v)
                after_pv(v)
            for e in list(qp_free):
                qp_free.remove(e)
                issue_qp(e[1])
            run_deferred(force=True)

        def swa_group(G):
            for n in range(4 * G, 4 * G + 4):
                for kv in range(2):
                    u = Unit()
                    u.kind, u.n, u.kv = "swa", n, kv
                    push_unit(u)
                    if qp_queue:
                        push_unit(qp_unit(*qp_queue.pop(0)))

        a_WO, WO = region("WO", [128, 8, D], BF16, "BC")
        bWO = Buf("WO", a_WO)
        a_G1, G1B = region("G1B", [128, D], F32, "B")
        bG1 = [Buf(f"G1B{i}", a_G1) for i in range(2)]

        def fold_wo_expand():
            expand_bc(lambda jj: MODCOL[:, 16 + jj:17 + jj], bMODCOL[8:12], G1B, bG1, [6, 7])

        def fold_wo_chunk(c):
            S.op("dve", lambda h: h.tensor_tensor(out=WO[:, c, :], in0=WO[:, c, :], in1=G1B, op=ALU.mult),
                 r=[bWO] + bG1, w=[bWO])

        load_cs(0)
        S.dma("pool", lambda h: h.dma_start(out=WO, in_=wo_d.rearrange("(c p) n -> p c n", p=128)), bWO, w=[bWO])
        for hh in range(2):
            push_unit(qp_unit(0, hh))
        for G in range(NG):
            if G == NG - 1:
                fold_wo_expand()
            for hh in range(8):
                for iu, u in enumerate(mla_units(G, hh)):
                    push_unit(u)
                    if G == 0 and iu == 0 and hh + 2 < 8:
                        push_unit(qp_unit(0, hh + 2))
                if G == NG - 1:
                    fold_wo_chunk(hh)
            if G + 1 < NG:
                assert not qp_queue
                load_cs(G + 1)
                qp_queue.extend((G + 1, hh) for hh in range(8))
            swa_group(G)
        flush_units()

        a_X1, X1 = region("X1", [128, NT, D], F32, "CDEF")
        bX1 = [Buf(f"X1[{t}]", a_X1) for t in range(NT)]
        a_BC2A, BC2A = region("BC2A", [128, D], F32, "C")
        a_BC2B, BC2B = region("BC2B", [128, D], F32, "C")
        bBC2A = [Buf(f"BC2A{i}", a_BC2A) for i in range(2)]
        bBC2B = [Buf(f"BC2B{i}", a_BC2B) for i in range(2)]
        a_H2T, H2T = region("H2T", [128, 8, T], BF16, "CE")
        bH2T = [Buf(f"H2T[{t}]", a_H2T) for t in range(NT)]
        T1D, bT1D, HBD, bHBD = make_norm_bufs("C")
        WUP, bWUP, WDN, bWDN = [], [], [], []
        for i in range(2):
            a, v = region(f"WUP{i}", [128, 8, 1024], BF16, "CE" if i == 0 else "E")
            WUP.append(v)
            bWUP.append(Buf(f"WUP{i}", a))
            a, v = region(f"WDN{i}", [128, 8, 1024], BF16, "CE" if i == 0 else "E")
            WDN.append(v)
            bWDN.append(Buf(f"WDN{i}", a))
        wup_v = wup_d.rearrange("(c p) n -> p c n", p=128)
        wdn_v = wdn_d.rearrange("(q c p) n -> q p c n", q=4, p=128)

        def load_pass(q):
            s_ = q % 2
            for c in range(8):
                S.dma("pool", lambda h, c=c: h.dma_start(out=WUP[s_][:, c, :], in_=wup_v[:, c, q * 1024:(q + 1) * 1024]),
                      bWUP[s_], w=[bWUP[s_]] if c == 0 else [])
            S.dma("pool", lambda h: h.dma_start(out=WDN[s_], in_=wdn_v[q]), bWDN[s_], w=[bWDN[s_]])

        for t in range(NT):
            S.dma("sp", lambda h, t=t: h.dma_start(out=X1[:, t, :], in_=x_d[t * 128:(t + 1) * 128, :]), bX1[t], w=[bX1[t]])
        load_pass(0)
        a_G2, G2B = region("G2B", [128, D], F32, "CE")
        bG2 = [Buf(f"G2B{i}", a_G2) for i in range(2)]
        cb = [0]

        def c_wo(t):
            for half in range(2):
                b = cb[0] % 4
                cb[0] += 1
                mm_group(b, None, 512, [MIXT[:, c, t * 128:(t + 1) * 128] for c in range(8)],
                         [WO[:, c, half * 512:(half + 1) * 512] for c in range(8)],
                         [bMIXT[c][t // 4] for c in range(8)] + [bWO])
                S.op("dve", lambda h, half=half, b=b: h.tensor_tensor(out=X1[:, t, half * 512:(half + 1) * 512], in0=bk(b),
                                                                      in1=X1[:, t, half * 512:(half + 1) * 512], op=ALU.add),
                     r=[bankb[b], bX1[t]], w=[bX1[t]])

        def c_pre(t):
            norm_pre(X1[:, t, :], [bX1[t]], T1D, bT1D, HBD, bHBD, t % 2, BC2B, bBC2B, BC2A, bBC2A, f"n2_{t}")

        def c_tr(t):
            norm_tr(HBD, bHBD, t % 2, 6 + (t % 2), H2T[:, :, t * 128:(t + 1) * 128], [bH2T[t]], evac="act")

        c_wo(0)
        c_wo(1)
        expand_bc(lambda jj: GCOLS[:, 3, jj:jj + 1], [bGCOLS[3]], BC2B, bBC2B, [4, 5])
        expand_bc(lambda jj: MODCOL[:, 24 + jj:25 + jj], bMODCOL[12:16], BC2A, bBC2A, [4, 5])
        c_pre(0)
        for t in range(NT):
            if t + 2 < NT:
                c_wo(t + 2)
            if t + 1 < NT:
                c_pre(t + 1)
            c_tr(t)
            if t == 8:
                expand_bc(lambda jj: MODCOL[:, 40 + jj:41 + jj], bMODCOL[20:24], G2B, bG2, [4, 5])

        UT, bUT = [], []
        for i in range(2):
            a, v = region(f"UT{i}", [128, 8, 512], BF16, "E")
            UT.append(v)
            bUT.append([Buf(f"UT{i}[{f}]", a) for f in range(8)])
        RT, bRT, GT, bGT = [], [], [], []
        for i in range(2):
            a, v = region(f"RT{i}", [128, 512], F32, "E")
            RT.append(v)
            bRT.append(Buf(f"RT{i}", a))
            a, v = region(f"GT{i}", [128, 512], F32, "E")
            GT.append(v)
            bGT.append(Buf(f"GT{i}", a))
        load_pass(1)

        ub = [0]
        rt_ctr = [0]

        def mlp_up(q, G):
            s_ = q % 2
            us = (q * 4 + G) % 2
            for fc in range(8):
                b = ub[0] % 4
                ub[0] += 1
                mm_group(b, None, 512, [WUP[s_][:, k, fc * 128:(fc + 1) * 128] for k in range(8)],
                         [H2T[:, k, G * 512:(G + 1) * 512] for k in range(8)], [bH2T[G * 4 + j] for j in range(4)] + [bWUP[s_]])
                rs = rt_ctr[0] % 2
                rt_ctr[0] += 1
                S.op("act", lambda h, b=b, rs=rs: h.activation(out=RT[rs], in_=bk(b), func=AF.Relu), r=[bankb[b]], w=[bRT[rs]])
                S.op("act", lambda h, rs=rs, fc=fc: h.activation(out=UT[us][:, fc, :], in_=RT[rs], func=AF.Square),
                     r=[bRT[rs]], w=[bUT[us][fc]])

        db = [0]

        def mlp_down(q, G):
            s_ = q % 2
            us = (q * 4 + G) % 2
            for j in range(4):
                t = G * 4 + j
                for half in range(2):
                    b = 4 + db[0] % 4
                    gs_ = db[0] % 2
                    db[0] += 1
                    mm_group(b, None, 512, [UT[us][:, fc, j * 128:(j + 1) * 128] for fc in range(8)],
                             [WDN[s_][:, fc, half * 512:(half + 1) * 512] for fc in range(8)], bUT[us] + [bWDN[s_]])
                    S.op("dve", lambda h, half=half, b=b, gs_=gs_: h.tensor_tensor(
                        out=GT[gs_], in0=bk(b), in1=G2B[:, half * 512:(half + 1) * 512], op=ALU.mult),
                         r=[bankb[b], bG2[half]], w=[bGT[gs_]])
                    S.op("dve", lambda h, t=t, half=half, gs_=gs_: h.tensor_tensor(
                        out=X1[:, t, half * 512:(half + 1) * 512], in0=GT[gs_], in1=X1[:, t, half * 512:(half + 1) * 512],
                        op=ALU.add), r=[bGT[gs_], bX1[t]], w=[bX1[t]])

        a_FG, FGB = region("FGB", [128, D], F32, "EF")
        bFG = Buf("FGB", a_FG)
        S.dma("sp", lambda h: h.dma_start(out=FGB, in_=bcast_row(fg_d, D)), bFG, w=[bFG])

        def final_tile(t):
            rstd, bs = rms_rstd(X1[:, t, :], [bX1[t]], D, f"nf_{t}")
            S.op("dve", lambda h: h.scalar_tensor_tensor(out=X1[:, t, :], in0=X1[:, t, :], scalar=rstd,
                                                         in1=FGB, op0=ALU.mult, op1=ALU.mult),
                 r=[bX1[t], bs, bFG], w=[bX1[t]])
            od = S.dma("sp", lambda h: h.dma_start(out=out_d[t * 128:(t + 1) * 128, :], in_=X1[:, t, :]), bX1[t],
                       r=[bX1[t]])
            S.final.append(od)

        seq = [(q, G) for q in range(4) for G in range(NG)]
        for i, (q, G) in enumerate(seq):
            if i == 0:
                mlp_up(q, G)
            if i + 1 < len(seq):
                mlp_up(*seq[i + 1])
            mlp_down(q, G)
            if G == NG - 1 and q + 2 < 4:
                load_pass(q + 2)
            if q == 3:
                for j in range(4):
                    final_tile(G * 4 + j)


        block = es.enter_context(nc.Block())
        S.emit(nc, es, block, None)
    return nc


def _host_consts():
    identf = np.eye(128, dtype=np.float32)
    onesf = np.ones((128, 128), dtype=np.float32)
    i = np.arange(32)
    freqs = 10000.0 ** (-(i % 16).astype(np.float64) / 16.0)
    ang = freqs[:, None] * np.arange(T, dtype=np.float64)[None, :]
    cs = np.stack([np.cos(ang), np.sin(ang)], axis=1).astype(np.float32)
    tk = np.arange(128)[:, None]
    tq = np.arange(128)[None, :]
    mmask = np.concatenate([np.full((128, 128), NEG), np.where(tk <= tq, 0.0, NEG)], axis=1).astype(np.float32)
    smask = np.zeros((128, 2, 2, 4, 128), dtype=np.float32)
    for kv in range(2):
        for g in range(4):
            h = kv * 4 + g
            slope = 2.0 ** (-(h + 1))
            d_own = (tq - tk).astype(np.float64)
            d_prev = (128 + tq - tk).astype(np.float64)
            smask[:, 1, kv, g, :] = np.where(tk <= tq, -slope * d_own * 8.0, NEG)
            smask[:, 0, kv, g, :] = np.where(tk > tq, -slope * d_prev * 8.0, NEG)
    return identf, onesf, cs, mmask, smask.reshape(128, 2048)


_NC_CACHE = {}


def kernel(x, c, w_ada, b_ada, norm_mix_g, w_in, g_qa, w_qb, g_kva, w_kvb, sinks,
           w_o, norm_mlp_g, w_up, w_down, final_g):
    f = lambda a: np.ascontiguousarray(np.asarray(a, dtype=np.float32))
    x, c = f(x), f(c)
    w_in0 = f(w_in)[0]
    o3 = 672
    swq = w_in0[:, o3:o3 + 512].reshape(D, 2, 4, 64).transpose(0, 2, 1, 3).reshape(D, 512)
    kr = w_in0[:, 640:672]
    win_l = np.ascontiguousarray(np.concatenate(
        [w_in0[:, 0:672], swq, w_in0[:, o3 + 512:1440], kr[:, 16:32], kr[:, 0:16]], axis=1))
    wqb0 = f(w_qb)[0]
    wqb_l = np.ascontiguousarray(np.concatenate(
        [wqb0[:, :, 0:96], wqb0[:, :, 80:96], wqb0[:, :, 64:80]], axis=2).reshape(384, 1024))
    wkvb0 = f(w_kvb)[0]
    wkvk_l = np.ascontiguousarray(wkvb0[:, :, 0:64].reshape(256, 512))
    wkvv_l = np.ascontiguousarray(wkvb0[:, :, 64:128].reshape(256, 512))
    wo0 = f(w_o)[0]
    wo_swa = wo0[512:].reshape(2, 4, 64, D).transpose(1, 0, 2, 3).reshape(512, D)
    wo_l = np.ascontiguousarray(np.concatenate([wo0[:512], wo_swa], axis=0))
    identf, onesf, cs, mmask, smask = _host_consts()
    shared = {
        "badacol": np.ascontiguousarray(f(b_ada)[0].reshape(48, 128).T),
        "gmixcol": np.ascontiguousarray(f(norm_mix_g)[0].reshape(8, 128).T),
        "gmlpcol": np.ascontiguousarray(f(norm_mlp_g)[0].reshape(8, 128).T), "fg": f(final_g).reshape(1, D),
        "gqa": np.ascontiguousarray(f(g_qa)[0].reshape(3, 128).T), "gkva": np.ascontiguousarray(f(g_kva)[0].reshape(2, 128).T),
        "sinks": f(sinks)[0].reshape(1, 8), "wada": f(w_ada)[0], "win": win_l, "wqb": wqb_l, "wkvk": wkvk_l,
        "wkvv": wkvv_l, "wo": wo_l, "wup": f(w_up)[0], "wdn": f(w_down)[0],
        "identf": identf, "onesf": onesf, "cs": cs, "mmask": mmask, "smask": smask,
    }
    in_maps = []
    for b in range(8):
        m = dict(shared)
        m["x"] = x[b]
        m["ccol"] = np.ascontiguousarray(c[b].reshape(8, 128).T)
        in_maps.append(m)
    if "nc" not in _NC_CACHE:
        _NC_CACHE["nc"] = build_program()
    nc = _NC_CACHE["nc"]
    res = run_bass_kernel_spmd(nc, in_maps, core_ids=list(range(8)))
    out = np.stack([np.asarray(res.results[b]["out"], dtype=np.float32).reshape(T, D) for b in range(8)], axis=0)
    return out
```
